# Optimizing a Trainium2 kernel written in Bass

```python
import jax, jax.numpy as jnp
from jax import lax
import numpy as np

D_MODEL = 4096
BATCH = 4
SEQ = 2048
DEPTH = 1

RET_HEADS = 8
RET_HEAD_DIM = 256
RET_WIDTH = RET_HEADS * RET_HEAD_DIM
RET_CHUNK = 128
ROPE_BASE = 10000.0
ATTN_GROUPS = ((128, 1), (512, 4), (2048, 16))
ATTN_HEADS_PER_GROUP = 4
ATTN_HEAD_DIM = 128
ATTN_WIDTH = len(ATTN_GROUPS) * ATTN_HEADS_PER_GROUP * ATTN_HEAD_DIM
ATTN_OUT_WIDTH = ATTN_HEADS_PER_GROUP * ATTN_HEAD_DIM
D_FF = 11008
EPS = 1e-6
NEG_INF = -1e30

IN_SIZES = (RET_WIDTH, RET_WIDTH, RET_WIDTH, RET_WIDTH,
            ATTN_WIDTH, ATTN_WIDTH, ATTN_WIDTH,
            D_MODEL, D_MODEL)
IN_WIDTH = sum(IN_SIZES)
IN_SPLITS = tuple(int(s) for s in np.cumsum(IN_SIZES)[:-1])

kernel_name = "hybrid_retention_dilated_attn_macaron"


def rmsnorm(x, g):
    x32 = x.astype(jnp.float32)
    y = x32 * lax.rsqrt(jnp.mean(x32 * x32, axis=-1, keepdims=True) + EPS)
    return (y * g.astype(jnp.float32)).astype(x.dtype)


def swiglu(h, w_gate, w_up, w_down):
    return (jax.nn.silu(h @ w_gate) * (h @ w_up)) @ w_down


def rotary(x):
    S, d = x.shape[1], x.shape[-1]
    pos = jnp.arange(S, dtype=jnp.float32)
    inv_freq = ROPE_BASE ** (-jnp.arange(0, d, 2, dtype=jnp.float32) / d)
    ang = pos[:, None] * inv_freq[None, :]
    cos, sin = jnp.cos(ang)[None, :, None, :], jnp.sin(ang)[None, :, None, :]
    x32 = x.astype(jnp.float32)
    x1, x2 = x32[..., : d // 2], x32[..., d // 2:]
    return jnp.concatenate([x1 * cos - x2 * sin, x2 * cos + x1 * sin], axis=-1)


def retention_chunkwise(q, k, v):
    B_, S, H, dk = q.shape
    dv = v.shape[-1]
    C = RET_CHUNK
    nc = S // C
    log_g = jnp.log(1.0 - 2.0 ** (-5.0 - jnp.arange(H, dtype=jnp.float32)))
    idx = jnp.arange(C, dtype=jnp.float32)
    diff = idx[:, None] - idx[None, :]
    decay_in = jnp.where(diff >= 0, jnp.exp(log_g[:, None, None] * jnp.maximum(diff, 0.0)), 0.0)
    xi = jnp.exp(log_g[:, None] * (idx + 1.0))[None, None]
    zeta = jnp.exp(log_g[:, None] * (C - 1.0 - idx))
    g_chunk = jnp.exp(log_g * C)
    qc = q.astype(jnp.float32).reshape(B_, nc, C, H, dk)
    kc = k.astype(jnp.float32).reshape(B_, nc, C, H, dk) * (dk ** -0.5)
    vc = v.astype(jnp.float32).reshape(B_, nc, C, H, dv)
    scores = jnp.einsum('bnihd,bnjhd->bnhij', qc, kc) * decay_in[None, None]
    y_inner = jnp.einsum('bnhij,bnjhe->bnihe', scores, vc)
    kv = jnp.einsum('bnjhd,hj,bnjhe->bnhde', kc, zeta, vc)

    def step(R, kv_n):
        return R * g_chunk[None, :, None, None] + kv_n, R

    _, R_prev = lax.scan(step, jnp.zeros((B_, H, dk, dv), jnp.float32), jnp.moveaxis(kv, 1, 0))
    R_prev = jnp.moveaxis(R_prev, 0, 1)
    y_cross = jnp.einsum('bnihd,bnhde->bnihe', qc * jnp.moveaxis(xi, 3, 2)[..., None].squeeze(-1)[..., None] if False else qc * jnp.transpose(xi, (0, 1, 3, 2))[..., None], R_prev)
    return (y_inner + y_cross).reshape(B_, S, H, dv)


def dilated_group_attention(q, k, v, window, dilation):
    B_, S, Hg, dh = q.shape
    r = dilation
    L = window // dilation
    U = S // r
    nb = -(-U // L)
    Up = nb * L

    def to_classes(t):
        t = t.astype(jnp.float32).reshape(B_, U, r, Hg, dh).transpose(0, 2, 3, 1, 4)
        t = jnp.pad(t, ((0, 0), (0, 0), (0, 0), (0, Up - U), (0, 0)))
        return t.reshape(B_, r, Hg, nb, L, dh)

    qb, kb, vb = to_classes(q), to_classes(k), to_classes(v)

    def with_prev(t):
        prev = jnp.pad(t, ((0, 0), (0, 0), (0, 0), (1, 0), (0, 0), (0, 0)))[:, :, :, :-1]
        return jnp.concatenate([prev, t], axis=4)

    kw, vw = with_prev(kb), with_prev(vb)
    s = jnp.einsum('brhnid,brhnjd->brhnij', qb, kw) * (dh ** -0.5)
    i = jnp.arange(L)[:, None]
    j = jnp.arange(2 * L)[None, :]
    dist = i + L - j
    band = (dist >= 0) & (dist <= L)
    not_before_start = (jnp.arange(nb)[:, None, None] > 0) | (j[None] >= L)
    mask = band[None] & not_before_start
    s = jnp.where(mask, s, NEG_INF)
    m = jnp.max(s, axis=-1, keepdims=True)
    p = jnp.exp(s - m)
    den = jnp.sum(p, axis=-1)
    o = jnp.einsum('brhnij,brhnjd->brhnid', p, vw) / den[..., None]
    lse = m[..., 0] + jnp.log(den)
    o = o.reshape(B_, r, Hg, Up, dh)[:, :, :, :U].transpose(0, 3, 1, 2, 4).reshape(B_, S, Hg, dh)
    lse = lse.reshape(B_, r, Hg, Up)[..., :U].transpose(0, 3, 1, 2).reshape(B_, S, Hg)
    return o, lse


def dilated_attention(q, k, v):
    B_, S, _ = q.shape
    n_heads = len(ATTN_GROUPS) * ATTN_HEADS_PER_GROUP
    q = q.reshape(B_, S, n_heads, ATTN_HEAD_DIM)
    k = k.reshape(B_, S, n_heads, ATTN_HEAD_DIM)
    v = v.reshape(B_, S, n_heads, ATTN_HEAD_DIM)
    outs, lses = [], []
    for gi, (window, dilation) in enumerate(ATTN_GROUPS):
        sl = slice(gi * ATTN_HEADS_PER_GROUP, (gi + 1) * ATTN_HEADS_PER_GROUP)
        o, lse = dilated_group_attention(q[:, :, sl], k[:, :, sl], v[:, :, sl], window, dilation)
        outs.append(o)
        lses.append(lse)
    o = jnp.stack(outs, axis=0)
    alpha = jax.nn.softmax(jnp.stack(lses, axis=0), axis=0)
    y = jnp.sum(alpha[..., None] * o, axis=0)
    return y.reshape(B_, S, ATTN_OUT_WIDTH)


def head_rms(y):
    return y * lax.rsqrt(jnp.mean(y * y, axis=-1, keepdims=True) + EPS)


def setup_inputs(seed: int = 0) -> dict:
    key = jax.random.key(seed)
    ks = jax.random.split(key, 16)
    nrm = lambda k, shape, fan_in: jax.random.normal(k, shape, jnp.float32) * (fan_in ** -0.5)
    gain = lambda k: 1.0 + 0.05 * jax.random.normal(k, (DEPTH, D_MODEL), jnp.float32)
    return {
        "x": jax.random.normal(ks[0], (BATCH, SEQ, D_MODEL), jnp.float32),
        "ffn1_norm": gain(ks[1]),
        "ffn1_w_gate": nrm(ks[2], (DEPTH, D_MODEL, D_FF), D_MODEL),
        "ffn1_w_up": nrm(ks[3], (DEPTH, D_MODEL, D_FF), D_MODEL),
        "ffn1_w_down": nrm(ks[4], (DEPTH, D_FF, D_MODEL), D_FF),
        "mix_norm": gain(ks[5]),
        "w_in": nrm(ks[6], (DEPTH, D_MODEL, IN_WIDTH), D_MODEL),
        "w_out_ret": nrm(ks[7], (DEPTH, RET_WIDTH, D_MODEL), RET_WIDTH),
        "w_out_attn": nrm(ks[8], (DEPTH, ATTN_OUT_WIDTH, D_MODEL), ATTN_OUT_WIDTH),
        "w_out": nrm(ks[9], (DEPTH, D_MODEL, D_MODEL), D_MODEL),
        "ffn2_norm": gain(ks[10]),
        "ffn2_w_gate": nrm(ks[11], (DEPTH, D_MODEL, D_FF), D_MODEL),
        "ffn2_w_up": nrm(ks[12], (DEPTH, D_MODEL, D_FF), D_MODEL),
        "ffn2_w_down": nrm(ks[13], (DEPTH, D_FF, D_MODEL), D_FF),
        "final_norm": 1.0 + 0.05 * jax.random.normal(ks[14], (D_MODEL,), jnp.float32),
    }


def reference(x, ffn1_norm, ffn1_w_gate, ffn1_w_up, ffn1_w_down, mix_norm, w_in,
              w_out_ret, w_out_attn, w_out, ffn2_norm, ffn2_w_gate, ffn2_w_up, ffn2_w_down,
              final_norm):
    B_, S, _ = x.shape
    for l in range(DEPTH):
        x = x + 0.5 * swiglu(rmsnorm(x, ffn1_norm[l]), ffn1_w_gate[l], ffn1_w_up[l], ffn1_w_down[l])
        h = rmsnorm(x, mix_norm[l])
        q_r, k_r, v_r, g_r, q_a, k_a, v_a, u_ret, u_attn = jnp.split(h @ w_in[l], IN_SPLITS, axis=-1)
        q_r = rotary(q_r.reshape(B_, S, RET_HEADS, RET_HEAD_DIM))
        k_r = rotary(k_r.reshape(B_, S, RET_HEADS, RET_HEAD_DIM))
        y_ret = head_rms(retention_chunkwise(q_r, k_r, v_r.reshape(B_, S, RET_HEADS, RET_HEAD_DIM)))
        y_ret = (jax.nn.silu(g_r.astype(jnp.float32)) * y_ret.reshape(B_, S, RET_WIDTH)).astype(x.dtype)
        y_ret = y_ret @ w_out_ret[l]
        y_attn = dilated_attention(q_a, k_a, v_a).astype(x.dtype) @ w_out_attn[l]
        merged = jax.nn.sigmoid(u_ret) * y_ret + jax.nn.sigmoid(u_attn) * y_attn
        x = x + merged @ w_out[l]
        x = x + 0.5 * swiglu(rmsnorm(x, ffn2_norm[l]), ffn2_w_gate[l], ffn2_w_up[l], ffn2_w_down[l])
    return rmsnorm(x, final_norm)
```

```python
import numpy as np
import ml_dtypes
import concourse.bass as bass
import concourse.mybir as mybir
from concourse.bass_utils import run_bass_kernel_spmd

F32 = mybir.dt.float32
BF16 = mybir.dt.bfloat16
AF = mybir.ActivationFunctionType
ALU = mybir.AluOpType

NCORES = 8
D = 4096
KC = 32
FF = 11008
NFF = FF // 128
TT = 512
NT = 2
TOK = TT * NT
SEQ = 2048
RH = 8
RW = 2048
AW = 1536
INW = 20992
EPS = 1e-6
NEGM = -30000.0

SEC = [("qr", 16), ("kr", 16), ("vr", 16), ("gr", 16), ("qa", 12), ("ka", 12), ("va", 12),
       ("ua", 32), ("ub", 32)]

OFF_KR = 0
OFF_VR = OFF_KR + 16 * 128 * TOK
OFF_KA = OFF_VR + TOK * RW
OFF_VA = OFF_KA + 12 * 128 * TOK
MINE_N = OFF_VA + TOK * AW
assert MINE_N % 1024 == 0
MINE_ROWS = MINE_N // 1024


class Op:
    __slots__ = ("eng", "fn", "deps", "sig", "is_dma", "key", "val", "need", "inc")

    def __init__(self, eng, fn, is_dma, key):
        self.eng = eng
        self.fn = fn
        self.deps = []
        self.sig = 0
        self.is_dma = is_dma
        self.key = key
        self.val = 0
        self.need = False
        self.inc = 16


class Prog:
    def __init__(self):
        self.ops = []
        self.last_w = {}
        self.readers = {}
        self.last_dma = {}
        self.dma_count = {}
        self.bank = 0
        self.default_w = None

    def nextbank(self, avoid=()):
        while self.bank in avoid:
            self.bank = (self.bank + 1) % 8
        b = self.bank
        self.bank = (self.bank + 1) % 8
        return b

    def fence(self):
        op = Op("sp", lambda E: E.nop(), False, None)
        last = {}
        for o in self.ops:
            if o.is_dma:
                last[("d", o.key)] = o
            else:
                last[("e", o.eng)] = o
        for o in last.values():
            o.need = True
            op.deps.append(o)
        op.need = True
        self.ops.append(op)
        self.last_w = {}
        self.readers = {}
        self.default_w = op

    def add(self, eng, fn, r=(), w=(), dma_key=None, inc=16):
        op = Op(eng, fn, dma_key is not None, dma_key)
        op.inc = inc
        deps = {}
        for k in r:
            lw = self.last_w.get(k, self.default_w)
            if lw is not None:
                deps[id(lw)] = (lw, True)
        for k in w:
            lw = self.last_w.get(k, self.default_w)
            if lw is not None and id(lw) not in deps:
                deps[id(lw)] = (lw, False)
            for rd in self.readers.get(k, ()):
                if id(rd) not in deps:
                    deps[id(rd)] = (rd, False)
        if self.default_w is not None and id(self.default_w) not in deps:
            deps[id(self.default_w)] = (self.default_w, True)
        if dma_key is not None:
            prev = self.last_dma.get(dma_key)
            if prev is not None:
                deps[id(prev)] = (prev, True)
            self.last_dma[dma_key] = op
            n = self.dma_count.get(dma_key, 0) + inc
            self.dma_count[dma_key] = n
            op.val = n
        for d, raw in deps.values():
            if d is op:
                continue
            if (not d.is_dma) and d.eng == eng:
                if eng == "pe":
                    continue
            d.need = True
            op.deps.append(d)
        for k in w:
            self.last_w[k] = op
            self.readers[k] = []
        for k in r:
            self.readers.setdefault(k, []).append(op)
        self.ops.append(op)
        return op

    def emit(self, nc, block, engsems, dmasems):
        cnt = {}
        for op in self.ops:
            if not op.is_dma and op.need:
                cnt[op.eng] = cnt.get(op.eng, 0) + 1
                op.sig = cnt[op.eng]
        by_eng = {}
        for op in self.ops:
            by_eng.setdefault(op.eng, []).append(op)

        def run(engname, E):
            waited = {}
            for op in by_eng.get(engname, []):
                for d in op.deps:
                    if d.is_dma:
                        sem, val, sk = dmasems[d.key], d.val, ("d", d.key)
                    else:
                        sem, val, sk = engsems[d.eng], d.sig, ("e", d.eng)
                    if waited.get(sk, 0) >= val:
                        continue
                    waited[sk] = val
                    E.wait_ge(sem, val)
                ins = op.fn(E)
                if op.is_dma:
                    ins.then_inc(dmasems[op.key], op.inc)
                elif op.need:
                    ins.then_inc(engsems[op.eng], 1)

        @block.tensor
        def _(E):
            run("pe", E)

        @block.scalar
        def _(E):
            run("act", E)

        @block.vector
        def _(E):
            run("dve", E)

        @block.gpsimd
        def _(E):
            run("pool", E)

        @block.sync
        def _(E):
            run("sp", E)


def build_program(dbg=None, ncores=NCORES, stop=99):
    nc = bass.Bass("TRN2", target_bir_lowering=False, num_devices=ncores)
    P = Prog()

    used_inputs = []

    class LazyIn:
        def __init__(self, name, shape, dt):
            self.name, self.shape, self.dt, self._ap = name, shape, dt, None

        def get(self):
            if self._ap is None:
                self._ap = nc.dram_tensor(self.name, list(self.shape), self.dt, kind="ExternalInput").ap()
                used_inputs.append(self.name)
            return self._ap

        def __getitem__(self, idx):
            return self.get()[idx]

    def din(name, shape, dt=F32):
        return LazyIn(name, shape, dt)

    x_in = din("x", [TOK, D])
    wg1 = din("ffn1_w_gate", [D, FF]); wu1 = din("ffn1_w_up", [D, FF]); wd1 = din("ffn1_w_down", [FF, D])
    wg2 = din("ffn2_w_gate", [D, FF]); wu2 = din("ffn2_w_up", [D, FF]); wd2 = din("ffn2_w_down", [FF, D])
    w_in = din("w_in", [D, INW])
    w_oa = din("w_out_ret", [RW, D]); w_ob = din("w_out_attn", [512, D]); w_o = din("w_out", [D, D])
    gains_in = din("gains", [128, 4, KC])
    rot_in = din("rot", [128, 2, TOK])
    rtab_in = din("rtab", [RH, 128, 1408])
    rscal_in = din("rscal", [128, 256])
    amask_in = din("amask", [128, 5, 128], BF16)
    ident_in = din("ident", [128, 128])
    y_out = nc.dram_tensor("y", [TOK, D], F32, kind="ExternalOutput").ap()

    def dscr(name, n, dt=BF16):
        return nc.dram_tensor(name, [n], dt).ap()

    QRd = dscr("QRd", 16 * 128 * TOK).rearrange("(c p t) -> c p t", p=128, t=TOK)
    GRd = dscr("GRd", 16 * 128 * TOK).rearrange("(c p t) -> c p t", p=128, t=TOK)
    QAd = dscr("QAd", 12 * 128 * TOK).rearrange("(c p t) -> c p t", p=128, t=TOK)
    UAd = dscr("UAd", 32 * 128 * TOK).rearrange("(c p t) -> c p t", p=128, t=TOK)
    UBd = dscr("UBd", 32 * 128 * TOK).rearrange("(c p t) -> c p t", p=128, t=TOK)
    YRd = dscr("YRd", 16 * 128 * TOK).rearrange("(c p t) -> c p t", p=128, t=TOK)
    YAd = dscr("YAd", 4 * 128 * TOK).rearrange("(c p t) -> c p t", p=128, t=TOK)
    XSd = dscr("XSd", NT * 128 * KC * TT, F32).rearrange("(n p f) -> n p f", p=128, f=KC * TT)
    SECROWS = {"kr": 1024, "vr": 1024, "ka": 768, "va": 768}
    CCT = {}
    for nm, rows in SECROWS.items():
        for t_ in range(NT):
            CCT[(nm, t_)] = (nc.dram_tensor("M_%s%d" % (nm, t_), [rows, 1024], BF16),
                             nc.dram_tensor("G_%s%d" % (nm, t_), [2 * rows, 1024], BF16))

    def secview(nm, t_, prev):
        rows = SECROWS[nm]
        m, g = CCT[(nm, t_)]
        flat = (g.ap()[0:rows, :] if prev else m.ap()).rearrange("r c -> (r c)")
        if nm in ("kr", "ka"):
            return flat.rearrange("(c p t) -> c p t", p=128, t=TT)
        return flat.rearrange("(t e) -> t e", e=(RW if nm == "vr" else AW))

    KRm = [secview("kr", t_, False) for t_ in range(NT)]; KRp = [secview("kr", t_, True) for t_ in range(NT)]
    VRm = [secview("vr", t_, False) for t_ in range(NT)]; VRp = [secview("vr", t_, True) for t_ in range(NT)]
    KAm = [secview("ka", t_, False) for t_ in range(NT)]; KAp = [secview("ka", t_, True) for t_ in range(NT)]
    VAm = [secview("va", t_, False) for t_ in range(NT)]; VAp = [secview("va", t_, True) for t_ in range(NT)]

    AR = nc.alloc_sbuf_tensor([128, 24576], F32)
    XT = AR[:, 0:16384].rearrange("p (c t) -> p c t", t=TT)
    HTraw = AR[:, 16384:24576]
    HT = HTraw.bitcast(BF16).rearrange("p (c t) -> p c t", t=TT)
    NGU, NWD = 4, 4
    WGU = nc.alloc_sbuf_tensor([128, NGU, 4096], BF16)
    WD = nc.alloc_sbuf_tensor([128, NWD, 4096], BF16)
    ACTB = nc.alloc_sbuf_tensor([128, 2, 2, TT], BF16)
    SG = nc.alloc_sbuf_tensor([128, 2, TT], F32)
    RSTD = nc.alloc_sbuf_tensor([128, TT], F32)
    SQ = nc.alloc_sbuf_tensor([128, 2, TT], BF16)
    TMP = nc.alloc_sbuf_tensor([128, 4, TT], F32)
    OST = nc.alloc_sbuf_tensor([128, 8, TT], BF16)
    UST = nc.alloc_sbuf_tensor([128, 2, 2, TT], BF16)
    GAIN = nc.alloc_sbuf_tensor([128, 4, KC], F32)
    ROT = nc.alloc_sbuf_tensor([128, 2, TOK], F32)
    RSCAL = nc.alloc_sbuf_tensor([128, 256], F32)
    AMASK = nc.alloc_sbuf_tensor([128, 5, 128], BF16)
    IDF = nc.alloc_sbuf_tensor([128, 128], F32)
    IDB = nc.alloc_sbuf_tensor([128, 128], BF16)
    ONES = nc.alloc_sbuf_tensor([128, 128], BF16)
    NEGC = nc.alloc_sbuf_tensor([1, TOK], BF16)
    PS = [nc.alloc_psum_tensor("psb%d" % _i, [128, 512], F32) for _i in range(8)]

    def wgu(s):
        return WGU[:, s, :].rearrange("p (k n) -> p k n", n=128)

    def wd(s):
        return WD[:, s, :]

    state = {"gu": 0, "wd": 0, "ost": 0, "ust": 0}

    def rot_slot(name, n):
        s = state[name]
        state[name] = (s + 1) % n
        return s

    ldn = [0]

    def dma(eng, out, in_, r, w, key):
        if key == "ld":
            key = ("ld", ldn[0] % 16)
            ldn[0] += 1
        P.add(eng, lambda E: E.dma_start(out=out, in_=in_), r=r, w=w, dma_key=key)

    def mm(out, lhsT, rhs, start, stop, r, w, skip=False):
        if skip:
            P.add("pe", lambda E: E.matmul(out, lhsT=lhsT, rhs=rhs, start=start, stop=stop,
                                           skip_group_check=True), r=r, w=w)
        else:
            P.add("pe", lambda E: E.matmul(out, lhsT=lhsT, rhs=rhs, start=start, stop=stop), r=r, w=w)

    def act(out, in_, func, r, w, scale=None, bias=None):
        kw = {}
        if scale is not None:
            kw["scale"] = scale
        if bias is not None:
            kw["bias"] = bias
        P.add("act", lambda E: E.activation(out=out, in_=in_, func=func, **kw), r=r, w=w)

    def tt(out, in0, in1, op, r, w):
        P.add("dve", lambda E: E.tensor_tensor(out=out, in0=in0, in1=in1, op=op), r=r, w=w)

    def stt(out, in0, scalar, in1, op0, op1, r, w):
        P.add("dve", lambda E: E.scalar_tensor_tensor(out=out, in0=in0, scalar=scalar, in1=in1,
                                                      op0=op0, op1=op1), r=r, w=w)

    def ts(out, in0, s1, s2, op0, op1, r, w):
        P.add("dve", lambda E: E.tensor_scalar(out=out, in0=in0, scalar1=s1, scalar2=s2,
                                               op0=op0, op1=op1), r=r, w=w)

    def tss(out, in_, s, op, r, w):
        P.add("dve", lambda E: E.tensor_single_scalar(out=out, in_=in_, scalar=s, op=op), r=r, w=w)

    def vcopy(out, in_, r, w):
        P.add("dve", lambda E: E.tensor_copy(out=out, in_=in_), r=r, w=w)

    cp_state = [0]

    def anycopy(out, in_, r, w):
        cp_state[0] ^= 1
        if cp_state[0]:
            vcopy(out, in_, r, w)
        else:
            act(out, in_, AF.Copy, r, w)

    xtk = lambda c: ("xt", c)
    htk = lambda c: ("ht", c)

    dma("sp", GAIN[:], gains_in.get(), [], ["gain"], "ld")
    dma("sp", ROT[:], rot_in.get(), [], ["rot"], "ld")
    dma("sp", RSCAL[:], rscal_in.get(), [], ["rscal"], "ld")
    dma("sp", AMASK[:], amask_in.get(), [], ["amask"], "ld")
    dma("sp", IDF[:], ident_in.get(), [], ["idf"], "ld")
    vcopy(IDB[:], IDF[:], ["idf"], ["idb"])
    P.add("dve", lambda E: E.memset(ONES[:], 1.0), r=[], w=["ones"])

    def load_x_tile(t):
        for tc in range(4):
            gtc = t * 4 + tc
            st = tc % 2
            stg = HTraw[:, st * 4096:(st + 1) * 4096]
            keys = [htk(c) for c in range(16 * st, 16 * st + 16)]
            dma("sp", stg, x_in[gtc * 128:(gtc + 1) * 128, :], [], keys, ("in", st))
            for c0 in range(0, KC, 4):
                b = P.nextbank()
                for i in range(4):
                    c = c0 + i
                    P.add("pe", lambda E, b=b, i=i, c=c, stg=stg: E.transpose(
                        out=PS[b][:, i * 128:(i + 1) * 128], in_=stg[:, c * 128:(c + 1) * 128],
                        identity=IDF[:]), r=keys + ["idf"], w=[("ps", b)])
                anycopy(XT[:, c0:c0 + 4, tc * 128:(tc + 1) * 128],
                        PS[b][:, :].rearrange("p (a b) -> p a b", b=128),
                        [("ps", b)], [xtk(c) for c in range(c0, c0 + 4)])

    def rmsnorm_stats(nchunks, src_fn, src_keys_fn, dim):
        b = P.nextbank()
        for c in range(nchunks):
            s = c % 2
            act(SQ[:, s, :], src_fn(c), AF.Square, src_keys_fn(c), [("sq", s)])
            mm(PS[b][:, :], ONES[:, :], SQ[:, s, :], c == 0, c == nchunks - 1,
               [("sq", s), "ones"], [("ps", b)])
        ts(RSTD[:, :], PS[b][:, :], 1.0 / dim, EPS, ALU.mult, ALU.add, [("ps", b)], ["rstd"])
        act(RSTD[:, :], RSTD[:, :], AF.Sqrt, ["rstd"], ["rstd"])
        P.add("dve", lambda E: E.reciprocal(out=RSTD[:, :], in_=RSTD[:, :]), r=["rstd"], w=["rstd"])

    def rmsnorm_to_ht(gi):
        rmsnorm_stats(KC, lambda c: XT[:, c, :], lambda c: [xtk(c)], D)
        for c in range(KC):
            stt(HT[:, c, :], XT[:, c, :], GAIN[:, gi, c:c + 1], RSTD[:, :], ALU.mult, ALU.mult,
                [xtk(c), "rstd", "gain"], [htk(c)])

    def ffn(wg, wu, wdn):
        G = 2
        ngroups = NFF // G
        for gi in range(ngroups):
            ab = gi % 2
            dslots = []
            for jj in range(G):
                j = gi * G + jj
                sg, su, sd = rot_slot("gu", NGU), rot_slot("gu", NGU), rot_slot("wd", NWD)
                dslots.append(sd)
                dma("pool", wgu(sg), wg[:, j * 128:(j + 1) * 128].rearrange("(k p) n -> p k n", p=128),
                    [], [("gu", sg)], ("gu", sg))
                dma("pool", wgu(su), wu[:, j * 128:(j + 1) * 128].rearrange("(k p) n -> p k n", p=128),
                    [], [("gu", su)], ("gu", su))
                dma("pool", wd(sd), wdn[j * 128:(j + 1) * 128, :], [], [("wd", sd)], ("wd", sd))
                bg, bu = P.nextbank(), P.nextbank()
                for kc in range(KC):
                    mm(PS[bg][:, :], wgu(sg)[:, kc, :], HT[:, kc, :], kc == 0, kc == KC - 1,
                       [("gu", sg), htk(kc)], [("ps", bg)])
                for kc in range(KC):
                    mm(PS[bu][:, :], wgu(su)[:, kc, :], HT[:, kc, :], kc == 0, kc == KC - 1,
                       [("gu", su), htk(kc)], [("ps", bu)])
                act(SG[:, jj, :], PS[bg][:, :], AF.Silu, [("ps", bg)], [("sg", jj)])
                tt(ACTB[:, ab, jj, :], SG[:, jj, :], PS[bu][:, :], ALU.mult,
                   [("sg", jj), ("ps", bu)], [("actb", ab, jj)])
            for m in range(KC):
                bd = P.nextbank()
                for jj in range(G):
                    mm(PS[bd][:, :], wd(dslots[jj])[:, m * 128:(m + 1) * 128], ACTB[:, ab, jj, :],
                       jj == 0, jj == G - 1, [("wd", dslots[jj]), ("actb", ab, jj)], [("ps", bd)])
                stt(XT[:, m, :], PS[bd][:, :], 0.5, XT[:, m, :], ALU.mult, ALU.add,
                    [("ps", bd), xtk(m)], [xtk(m)])

    def store_bf(src, dst, rkeys, wkeys, slot):
        dma("sp", dst, src, rkeys, wkeys, ("ost", slot))

    def win_stage(t):
        tsl = slice(t * TT, (t + 1) * TT)
        col = 0
        for name, n in SEC:
            if name in ("qr", "kr"):
                for h in range(RH):
                    slots, banks = [], []
                    for i in range(2):
                        cc = col + 2 * h + i
                        s = rot_slot("gu", NGU)
                        dma("pool", wgu(s), w_in[:, cc * 128:(cc + 1) * 128].rearrange("(k p) n -> p k n", p=128),
                            [], [("gu", s)], ("gu", s))
                        b = P.nextbank()
                        for kc in range(KC):
                            mm(PS[b][:, :], wgu(s)[:, kc, :], HT[:, kc, :], kc == 0, kc == KC - 1,
                               [("gu", s), htk(kc)], [("ps", b)])
                        slots.append(s); banks.append(b)
                    b1, b2 = banks
                    cos, sin = ROT[:, 0, tsl], ROT[:, 1, tsl]
                    o1, o2 = rot_slot("ost", 8), rot_slot("ost", 8)
                    tt(TMP[:, 0, :], PS[b1][:, :], cos, ALU.mult, [("ps", b1), "rot"], [("tmp", 0)])
                    tt(TMP[:, 1, :], PS[b2][:, :], sin, ALU.mult, [("ps", b2), "rot"], [("tmp", 1)])
                    tt(OST[:, o1, :], TMP[:, 0, :], TMP[:, 1, :], ALU.subtract,
                       [("tmp", 0), ("tmp", 1)], [("ost", o1)])
                    tt(TMP[:, 2, :], PS[b2][:, :], cos, ALU.mult, [("ps", b2), "rot"], [("tmp", 2)])
                    tt(TMP[:, 3, :], PS[b1][:, :], sin, ALU.mult, [("ps", b1), "rot"], [("tmp", 3)])
                    tt(OST[:, o2, :], TMP[:, 2, :], TMP[:, 3, :], ALU.add,
                       [("tmp", 2), ("tmp", 3)], [("ost", o2)])
                    for oo, cx in ((o1, 2 * h), (o2, 2 * h + 1)):
                        dd = QRd[cx][:, tsl] if name == "qr" else KRm[t][cx]
                        store_bf(OST[:, oo, :], dd, [("ost", oo)], [(name, t, cx)], oo)
            elif name in ("gr", "qa", "ka", "ua", "ub"):
                dst = {"gr": GRd, "qa": QAd, "ka": None, "ua": UAd, "ub": UBd}[name]
                func = {"gr": AF.Silu, "qa": AF.Copy, "ka": AF.Copy, "ua": AF.Sigmoid, "ub": AF.Sigmoid}[name]
                for ci in range(n):
                    cc = col + ci
                    s = rot_slot("gu", NGU)
                    dma("pool", wgu(s), w_in[:, cc * 128:(cc + 1) * 128].rearrange("(k p) n -> p k n", p=128),
                        [], [("gu", s)], ("gu", s))
                    b = P.nextbank()
                    for kc in range(KC):
                        mm(PS[b][:, :], wgu(s)[:, kc, :], HT[:, kc, :], kc == 0, kc == KC - 1,
                           [("gu", s), htk(kc)], [("ps", b)])
                    o = rot_slot("ost", 8)
                    if func == AF.Copy:
                        vcopy(OST[:, o, :], PS[b][:, :], [("ps", b)], [("ost", o)])
                    else:
                        act(OST[:, o, :], PS[b][:, :], func, [("ps", b)], [("ost", o)])
                    dd = KAm[t][ci] if name == "ka" else dst[ci][:, tsl]
                    store_bf(OST[:, o, :], dd, [("ost", o)], [(name, t, ci)], o)
            else:
                dst = VRm[t] if name == "vr" else VAm[t]
                for ci in range(n):
                    cc = col + ci
                    s = rot_slot("gu", NGU)
                    dma("pool", wgu(s), w_in[:, cc * 128:(cc + 1) * 128].rearrange("(k p) n -> p k n", p=128),
                        [], [("gu", s)], ("gu", s))
                    b = P.nextbank()
                    for tc in range(4):
                        for kc in range(KC):
                            mm(PS[b][:, tc * 128:(tc + 1) * 128], HT[:, kc, tc * 128:(tc + 1) * 128],
                               wgu(s)[:, kc, :], kc == 0, kc == KC - 1,
                               [("gu", s), htk(kc)], [("ps", b)])
                    o = rot_slot("ost", 8)
                    anycopy(OST[:, o, :], PS[b][:, :], [("ps", b)], [("ost", o)])
                    store_bf(OST[:, o, :].rearrange("p (a e) -> p a e", e=128),
                             dst[:, ci * 128:(ci + 1) * 128].rearrange("(a p) e -> p a e", p=128),
                             [("ost", o)], [(name, t, ci)], o)
            col += n

    def emit_out(t):
        for tc in range(4):
            gtc = t * 4 + tc
            st = tc % 2
            stg = HTraw[:, st * 4096:(st + 1) * 4096]
            keys = [htk(c) for c in range(16 * st, 16 * st + 16)]
            for c0 in range(0, KC, 4):
                b = P.nextbank()
                for i in range(4):
                    c = c0 + i
                    P.add("pe", lambda E, b=b, i=i, c=c, tc=tc: E.transpose(
                        out=PS[b][:, i * 128:(i + 1) * 128], in_=XT[:, c, tc * 128:(tc + 1) * 128],
                        identity=IDF[:]), r=[xtk(c), "idf"], w=[("ps", b)])
                anycopy(stg[:, c0 * 128:(c0 + 4) * 128], PS[b][:, :], [("ps", b)], keys)
            dma("sp", y_out[gtc * 128:(gtc + 1) * 128, :], stg, keys, [("y", gtc)], ("in", st))

    def finish():
        P.add("sp", lambda E: E.nop(), r=[("y", g) for g in range(8)], w=[])
        P.used_inputs = list(used_inputs)
        return nc, P

    for t in range(NT):
        load_x_tile(t)
        if stop == 1:
            emit_out(t); return finish()
        rmsnorm_to_ht(0)
        ffn(wg1, wu1, wd1)
        if stop == 2:
            emit_out(t); return finish()
        rmsnorm_to_ht(1)
        win_stage(t)
        if stop == 3:
            emit_out(t); return finish()
        for nm, ncc in (("kr", 16), ("vr", 16), ("ka", 12), ("va", 12)):
            m_, g_ = CCT[(nm, t)]
            P.add("pool", lambda E, m_=m_, g_=g_: E.collective_compute(
                "AllGather", ALU.bypass, replica_groups=[[2 * i, 2 * i + 1] for i in range(ncores // 2)],
                ins=[m_.ap().opt()], outs=[g_.ap().opt()]),
                r=[(nm, t, c) for c in range(ncc)], w=[("g_" + nm, t)], dma_key=("cc", nm, t), inc=1)
        for q4 in range(4):
            dma("sp", XSd[t][:, q4 * 4096:(q4 + 1) * 4096], AR[:, q4 * 4096:(q4 + 1) * 4096],
                [xtk(c) for c in range(q4 * 8, q4 * 8 + 8)], [("xs", t, q4)], ("xs", q4))

    if stop == 4:
        P.fence(); emit_out(1); return finish()
    all_ar = [xtk(c) for c in range(KC)] + [htk(c) for c in range(KC)]
    cur = [0]

    def carve(nbytes, dt, shape_str=None, **kw):
        ncol = nbytes // 4
        v = AR[:, cur[0]:cur[0] + ncol]
        cur[0] += ncol
        assert cur[0] <= 24576
        if dt == BF16:
            v = v.bitcast(BF16)
        if shape_str:
            v = v.rearrange(shape_str, **kw)
        return v

    Rq = [carve(4096, BF16, "p (i t) -> p i t", t=TOK) for _ in range(2)]
    Rk = [carve(8192, BF16, "p (i t) -> p i t", t=2 * TOK) for _ in range(2)]
    Rv = [carve(8192, BF16, "p (j e) -> p j e", e=256) for _ in range(2)]
    Rg = [carve(4096, BF16, "p (i t) -> p i t", t=TOK) for _ in range(2)]
    Rtab = [carve(5632, F32) for _ in range(2)]
    PT = carve(4096, BF16, "p (i t) -> p i t", t=TT)
    YF = carve(4096, F32, "p (i t) -> p i t", t=TT)
    ret_end = cur[0]

    P.fence()

    def arw(keys):
        return list(keys)

    def ret_load(h):
        s = h % 2
        for i in range(2):
            c = 2 * h + i
            dma("sp", Rq[s][:, i, :], QRd[c], [("qr", c)], arw([("rq", s)]), "ld")
            for t_ in range(NT):
                dma("sp", Rk[s][:, i, t_ * TT:(t_ + 1) * TT], KRp[t_][c], [("g_kr", t_)], [("rk", s)], "ld")
                dma("sp", Rk[s][:, i, TOK + t_ * TT:TOK + (t_ + 1) * TT], KRm[t_][c], [], [("rk", s)], "ld")
            dma("sp", Rg[s][:, i, :], GRd[c], [("gr", c)], [("rg", s)], "ld")
        for t_ in range(NT):
            dma("sp", Rv[s][:, 4 * t_:4 * t_ + 4, :],
                VRp[t_][:, h * 256:(h + 1) * 256].rearrange("(j p) e -> p j e", p=128),
                [("g_vr", t_)], [("rv", s)], "ld")
            dma("sp", Rv[s][:, 8 + 4 * t_:12 + 4 * t_, :],
                VRm[t_][:, h * 256:(h + 1) * 256].rearrange("(j p) e -> p j e", p=128),
                [], [("rv", s)], "ld")
        dma("sp", Rtab[s][:, :], rtab_in[h], [], [("rtab", s)], "ld")

    def retention_head(h):
        s = h % 2
        q, k, v, g, tab = Rq[s], Rk[s], Rv[s], Rg[s], Rtab[s]
        for qb in range(2):
            qs = slice(qb * TT, (qb + 1) * TT)
            by = [P.nextbank(), P.nextbank()]
            jlist = list(range(8)) + [8 + j for j in range(4 * qb + 4)]
            for idx, jj in enumerate(jlist):
                bs = P.nextbank(avoid=by)
                for i in range(2):
                    mm(PS[bs][:, :], k[:, i, jj * 128:(jj + 1) * 128], q[:, i, qs], i == 0, i == 1,
                       [("rk", s), ("rq", s)], [("ps", bs)])
                j = jj - 8
                if jj < 8 or j < 4 * qb:
                    tv = tab[:, 0:512]
                else:
                    m = j - 4 * qb
                    tv = tab[:, 512 + 384 - 128 * m: 512 + 384 - 128 * m + 512]
                colx = (h * 16 + jj) * 2 + qb
                pi = idx % 4
                stt(PT[:, pi, :], PS[bs][:, :], RSCAL[:, colx:colx + 1], tv, ALU.mult, ALU.mult,
                    [("ps", bs), "rscal", ("rtab", s)], [("pt", pi)])
                for e in range(2):
                    mm(PS[by[e]][:, :], v[:, jj, e * 128:(e + 1) * 128], PT[:, pi, :],
                       idx == 0, idx == len(jlist) - 1, [("rv", s), ("pt", pi)], [("ps", by[e])])
            for e in range(2):
                act(YF[:, e, :], PS[by[e]][:, :], AF.Copy, [("ps", by[e])], [("yf", e)])
            rmsnorm_stats(2, lambda e: YF[:, e, :], lambda e: [("yf", e)], 256)
            for e in range(2):
                o = rot_slot("ost", 8)
                stt(TMP[:, e, :], YF[:, e, :], 1.0, RSTD[:, :], ALU.mult, ALU.mult,
                    [("yf", e), "rstd"], [("tmp", e)])
                tt(OST[:, o, :], TMP[:, e, :], g[:, e, qs], ALU.mult, [("tmp", e), ("rg", s)], [("ost", o)])
                store_bf(OST[:, o, :], YRd[2 * h + e][:, qs], [("ost", o)], [("yr", 2 * h + e, qb)], o)

    ret_load(0)
    for h in range(RH):
        if h + 1 < RH:
            ret_load(h + 1)
        retention_head(h)

    if stop == 5:
        P.fence()
        for q4 in range(4):
            dma("sp", AR[:, q4 * 4096:(q4 + 1) * 4096], XSd[1][:, q4 * 4096:(q4 + 1) * 4096],
                [("xs", 1, q4)], [xtk(c) for c in range(q4 * 8, q4 * 8 + 8)], ("xs", q4))
        emit_out(1); return finish()
    cur[0] = 0
    Aq = [carve(3 * 2048, BF16, "p (g t) -> p g t", t=TOK) for _ in range(2)]
    Ak = [carve(3 * 4096, BF16, "p (g t) -> p g t", t=2 * TOK) for _ in range(2)]
    Av0 = [carve(4096, BF16, "p (j e) -> p j e", e=128) for _ in range(2)]
    Av1 = [carve(4096, BF16, "p (c k e) -> p c k e", k=4, e=128) for _ in range(2)]
    Av2 = [carve(4096, BF16, "p (c e) -> p c e", e=128) for _ in range(2)]
    PA = carve(2048, BF16, "p (i t) -> p i t", t=128)
    SQW = carve(1024, BF16)
    ROW = carve(5 * TOK * 4 + 64, F32)
    ret_keys = [("rq", s) for s in range(2)] + [("rk", s) for s in range(2)] + [("rv", s) for s in range(2)] + \
               [("rg", s) for s in range(2)] + [("rtab", s) for s in range(2)] + \
               [("pt", i) for i in range(4)] + [("yf", e) for e in range(2)]
    P.fence()

    def atw(keys):
        return list(keys)

    def att_load(i):
        s = i % 2
        for g in range(3):
            c = 4 * g + i
            dma("sp", Aq[s][:, g, :], QAd[c], [("qa", c)], atw([("aq", s)]), "ld")
            for t_ in range(NT):
                dma("sp", Ak[s][:, g, t_ * TT:(t_ + 1) * TT], KAp[t_][c], [("g_ka", t_)], [("ak", s)], "ld")
                dma("sp", Ak[s][:, g, TOK + t_ * TT:TOK + (t_ + 1) * TT], KAm[t_][c], [], [("ak", s)], "ld")
        c0, c1, c2 = i * 128, (4 + i) * 128, (8 + i) * 128
        for t_ in range(NT):
            for (src, base, gk) in ((VAp[t_], 0, [("g_va", t_)]), (VAm[t_], 1, [])):
                dma("sp", Av0[s][:, 8 * base + 4 * t_:8 * base + 4 * t_ + 4, :],
                    src[:, c0:c0 + 128].rearrange("(j p) e -> p j e", p=128), gk, [("av0", s)], "ld")
                dma("sp", Av1[s][:, :, 2 * base + t_, :],
                    src[:, c1:c1 + 128].rearrange("(p c) e -> p c e", c=4), gk, [("av1", s)], "ld")
                dma("sp", Av2[s][64 * base + 32 * t_:64 * base + 32 * t_ + 32, :, :],
                    src[:, c2:c2 + 128].rearrange("(p c) e -> p c e", c=16), gk, [("av2", s)], "ld")

    def att_shift(i):
        s = i % 2
        QN = ROW[0:1, 0:3 * TOK].rearrange("p (g t) -> p g t", t=TOK)
        KN = ROW[0:1, 3 * TOK:5 * TOK]
        KM = ROW[0:1, 5 * TOK:5 * TOK + 4]
        for g in range(3):
            for (src, n, dstf) in ((Aq[s][:, g, :], 2, lambda blk: QN[:, g, blk * TT:(blk + 1) * TT]),
                                   (Ak[s][:, g, :], 4, lambda blk: KN[:, blk * TT:(blk + 1) * TT])):
                for blk in range(n):
                    act(SQW[:, 0:TT], src[:, blk * TT:(blk + 1) * TT], AF.Square,
                        [("aq", s), ("ak", s)], ["sqw"])
                    b = P.nextbank()
                    mm(PS[b][0:1, :], ONES[:, 0:1], SQW[:, 0:TT], True, True, ["sqw", "ones"], [("ps", b)])
                    vcopy(dstf(blk), PS[b][0:1, :], [("ps", b)], ["rowscr"])
            P.add("dve", lambda E, g=g: E.reduce_max(out=KM[:, g:g + 1], in_=KN[:, :],
                                                     axis=mybir.AxisListType.X), r=["rowscr"], w=["rowscr"])
            tss(QN[:, g, :], QN[:, g, :], KM[:, g:g + 1], ALU.mult, ["rowscr"], ["rowscr"])
        tt(QN[:, 0, :], QN[:, 0, :], QN[:, 1, :], ALU.max, ["rowscr"], ["rowscr"])
        tt(QN[:, 0, :], QN[:, 0, :], QN[:, 2, :], ALU.max, ["rowscr"], ["rowscr"])
        act(QN[:, 0, :], QN[:, 0, :], AF.Sqrt, ["rowscr"], ["rowscr"])
        tss(NEGC[:, :], QN[:, 0, :], -1.0, ALU.mult, ["rowscr"], ["negc"])

    SC = 128.0 ** -0.5

    def attention_head(i):
        s = i % 2
        for qb in range(2):
            bn, bd = P.nextbank(), P.nextbank()
            first = True
            cnt = 0
            work = []
            for g, r in enumerate((1, 4, 16)):
                nq = TT // r
                nsub = max(1, nq // 128)
                nqs = min(nq, 128)
                for c in range(r):
                    for sub in range(nsub):
                        uo0 = (TT * qb) // r + sub * 128
                        uq0 = TOK // r + uo0
                        if r == 16:
                            tiles = [(0, 3 + qb)]
                        else:
                            kt0 = uq0 // 128 - 1
                            prev = kt0 < (TOK // r) // 128
                            tiles = [(kt0, 0 if prev else 1), (kt0 + 1, 2)]
                        for kt, mk in tiles:
                            work.append((g, r, c, sub, nqs, uo0, kt, mk))
            nwork = len(work)
            for wi, (g, r, c, sub, nqs, uo0, kt, mk) in enumerate(work):
                q0 = c + r * uo0
                qsl = slice(q0, q0 + r * (nqs - 1) + 1, r)
                k0 = c + r * 128 * kt
                ksl = slice(k0, k0 + r * 127 + 1, r)
                bs = P.nextbank(avoid=(bn, bd))
                mm(PS[bs][:, 0:nqs], Ak[s][:, g, ksl], Aq[s][:, g, qsl], True, False,
                   [("ak", s), ("aq", s)], [("ps", bs)], skip=True)
                mm(PS[bs][:, 0:nqs], ONESROW[0:1, :], NEGC[0:1, qsl], False, False,
                   ["onesrow", "negc"], [("ps", bs)], skip=True)
                mm(PS[bs][:, 0:nqs], IDB[:, :], AMASK[:, mk, 0:nqs], False, True,
                   ["idb", "amask"], [("ps", bs)], skip=True)
                pi = cnt % 8
                cnt += 1
                act(PA[:, pi, 0:nqs], PS[bs][:, 0:nqs], AF.Exp, [("ps", bs)], [("pa", pi)], scale=SC)
                o0 = c + r * sub * 128
                osl = slice(o0, o0 + r * (nqs - 1) + 1, r)
                if g == 0:
                    vt = Av0[s][:, kt, :]
                    vk = ("av0", s)
                elif g == 1:
                    vt = Av1[s][:, c, kt, :]
                    vk = ("av1", s)
                else:
                    vt = Av2[s][:, c, :]
                    vk = ("av2", s)
                st = (wi == 0)
                mm(PS[bn][:, osl], vt, PA[:, pi, 0:nqs], st, wi == nwork - 1,
                   [vk, ("pa", pi)], [("ps", bn)], skip=True)
                mm(PS[bd][:, osl], ONES[:, :], PA[:, pi, 0:nqs], st, wi == nwork - 1,
                   ["ones", ("pa", pi)], [("ps", bd)], skip=True)
            P.add("dve", lambda E, bd=bd: E.reciprocal(out=TMP[:, 0, :], in_=PS[bd][:, :]),
                  r=[("ps", bd)], w=[("tmp", 0)])
            o = rot_slot("ost", 8)
            tt(OST[:, o, :], PS[bn][:, :], TMP[:, 0, :], ALU.mult, [("ps", bn), ("tmp", 0)], [("ost", o)])
            store_bf(OST[:, o, :], YAd[i][:, qb * TT:(qb + 1) * TT], [("ost", o)], [("ya", i, qb)], o)

    ONESROW = nc.alloc_sbuf_tensor([1, 128], BF16)
    P.add("dve", lambda E: E.memset(ONESROW[:], 1.0), r=[], w=["onesrow"])
    att_load(0)
    for i in range(4):
        if i + 1 < 4:
            att_load(i + 1)
        att_shift(i)
        attention_head(i)

    if stop == 6:
        P.fence()
        for q4 in range(4):
            dma("sp", AR[:, q4 * 4096:(q4 + 1) * 4096], XSd[1][:, q4 * 4096:(q4 + 1) * 4096],
                [("xs", 1, q4)], [xtk(c) for c in range(q4 * 8, q4 * 8 + 8)], ("xs", q4))
        emit_out(1); return finish()
    att_keys = [("aq", s) for s in range(2)] + [("ak", s) for s in range(2)] + \
               [("av0", s) for s in range(2)] + [("av1", s) for s in range(2)] + \
               [("av2", s) for s in range(2)] + [("pa", i) for i in range(8)] + ["sqw"]
    YRt = AR[:, 0:4096].bitcast(BF16).rearrange("p (c t) -> p c t", t=TT)
    YAt = AR[:, 4096:5120].bitcast(BF16).rearrange("p (c t) -> p c t", t=TT)
    P.fence()
    for t in range(NT):
        tsl = slice(t * TT, (t + 1) * TT)
        extra = []
        for c in range(16):
            dma("sp", YRt[:, c, :], YRd[c][:, tsl], [("yr", c, t)], [xtk(c // 2)] + (extra if c == 0 else []),
                "ld")
        for c in range(4):
            dma("sp", YAt[:, c, :], YAd[c][:, tsl], [("ya", c, t)], [xtk(8 + c // 2)], "ld")
        for m in range(KC):
            sa, sb = rot_slot("gu", NGU), rot_slot("gu", NGU)
            wa = WGU[:, sa, 0:2048].rearrange("p (k n) -> p k n", n=128)
            wb = WGU[:, sb, 0:512].rearrange("p (k n) -> p k n", n=128)
            dma("pool", wa, w_oa[:, m * 128:(m + 1) * 128].rearrange("(k p) n -> p k n", p=128),
                [], [("gu", sa)], ("gu", sa))
            dma("pool", wb, w_ob[:, m * 128:(m + 1) * 128].rearrange("(k p) n -> p k n", p=128),
                [], [("gu", sb)], ("gu", sb))
            us = rot_slot("ust", 2)
            dma("sp", UST[:, us, 0, :], UAd[m][:, tsl], [("ua", m)], [("ust", us)], "ld")
            dma("sp", UST[:, us, 1, :], UBd[m][:, tsl], [("ub", m)], [("ust", us)], "ld")
            ba, bb = P.nextbank(), P.nextbank()
            for kc in range(16):
                mm(PS[ba][:, :], wa[:, kc, :], YRt[:, kc, :], kc == 0, kc == 15,
                   [("gu", sa), xtk(kc // 2)], [("ps", ba)])
            for kc in range(4):
                mm(PS[bb][:, :], wb[:, kc, :], YAt[:, kc, :], kc == 0, kc == 3,
                   [("gu", sb), xtk(8 + kc // 2)], [("ps", bb)])
            tt(TMP[:, 0, :], PS[ba][:, :], UST[:, us, 0, :], ALU.mult, [("ps", ba), ("ust", us)], [("tmp", 0)])
            tt(TMP[:, 1, :], PS[bb][:, :], UST[:, us, 1, :], ALU.mult, [("ps", bb), ("ust", us)], [("tmp", 1)])
            tt(HT[:, m, :], TMP[:, 0, :], TMP[:, 1, :], ALU.add, [("tmp", 0), ("tmp", 1)], [htk(m)])
        for q4 in range(4):
            dma("sp", AR[:, q4 * 4096:(q4 + 1) * 4096], XSd[t][:, q4 * 4096:(q4 + 1) * 4096],
                [("xs", t, q4)], [xtk(c) for c in range(q4 * 8, q4 * 8 + 8)], ("xs", q4))
        for m in range(KC):
            s = rot_slot("gu", NGU)
            dma("pool", wgu(s), w_o[:, m * 128:(m + 1) * 128].rearrange("(k p) n -> p k n", p=128),
                [], [("gu", s)], ("gu", s))
            b = P.nextbank()
            for kc in range(KC):
                mm(PS[b][:, :], wgu(s)[:, kc, :], HT[:, kc, :], kc == 0, kc == KC - 1,
                   [("gu", s), htk(kc)], [("ps", b)])
            tt(XT[:, m, :], PS[b][:, :], XT[:, m, :], ALU.add, [("ps", b), xtk(m)], [xtk(m)])
        rmsnorm_to_ht(2)
        ffn(wg2, wu2, wd2)
        rmsnorm_stats(KC, lambda c: XT[:, c, :], lambda c: [xtk(c)], D)
        for c in range(KC):
            stt(XT[:, c, :], XT[:, c, :], GAIN[:, 3, c:c + 1], RSTD[:, :], ALU.mult, ALU.mult,
                [xtk(c), "rstd", "gain"], [xtk(c)])
        emit_out(t)
    return finish()


def finalize(nc, P):
    dkeys = list(P.dma_count.keys())
    sems = {}
    import contextlib
    with contextlib.ExitStack() as es:
        engs = {e: es.enter_context(nc.semaphore("e_" + e)) for e in ("pe", "act", "dve", "pool", "sp")}
        for i, k in enumerate(dkeys):
            sems[k] = es.enter_context(nc.semaphore("d%d" % i))
        block = es.enter_context(nc.Block())
        P.emit(nc, block, engs, sems)
    return nc


def _tables(half):
    pos = (np.arange(TOK, dtype=np.float32) + np.float32(half * TOK))
    inv = (np.float32(10000.0) ** (-np.arange(0, 256, 2, dtype=np.float32) / np.float32(256))).astype(np.float32)
    ang = pos[None, :] * inv[:, None]
    rot = np.stack([np.cos(ang), np.sin(ang)], axis=1).astype(np.float32)
    hh = np.arange(RH, dtype=np.float64)
    gam = 1.0 - 2.0 ** (-5.0 - hh)
    s = np.arange(128)[:, None].astype(np.float64)
    rtab = np.zeros((RH, 128, 1408), np.float32)
    rscal = np.zeros((RH, 16, 2), np.float64)
    for h in range(RH):
        tq = np.arange(512)[None, :].astype(np.float64)
        rtab[h, :, 0:512] = gam[h] ** (tq - s)
        v = np.arange(896)[None, :].astype(np.float64) - 384.0
        rtab[h, :, 512:] = np.where(v >= s, gam[h] ** np.maximum(v - s, 0.0), 0.0)
        for jj in range(16):
            for qb in range(2):
                j = jj - 8
                if jj < 8 or j < 4 * qb:
                    val = gam[h] ** (TOK + 512 * qb - 128 * jj) / 16.0
                    if jj < 8 and half == 0:
                        val = 0.0
                else:
                    val = 1.0 / 16.0
                rscal[h, jj, qb] = val
    rscal = np.broadcast_to(rscal.reshape(1, 256).astype(np.float32), (128, 256)).copy()
    sidx = np.arange(128)[:, None]
    tidx = np.arange(128)[None, :]
    am = np.zeros((128, 5, 128), np.float32)
    mu = np.where(sidx >= tidx, 0.0, NEGM)
    am[:, 0, :] = mu if half == 1 else NEGM
    am[:, 1, :] = mu
    am[:, 2, :] = np.where(sidx <= tidx, 0.0, NEGM)
    for b in range(2):
        m16 = np.where(sidx <= 64 + 32 * b + tidx, 0.0, NEGM)
        if half == 0:
            m16 = np.where(sidx < 64, NEGM, m16)
        am[:, 3 + b, :] = m16
    return rot, rtab, rscal, am.astype(ml_dtypes.bfloat16)


_CACHE = {}
_STOP = [99]


def kernel(x, ffn1_norm, ffn1_w_gate, ffn1_w_up, ffn1_w_down, mix_norm, w_in, w_out_ret, w_out_attn,
           w_out, ffn2_norm, ffn2_w_gate, ffn2_w_up, ffn2_w_down, final_norm):
    f = lambda a: np.ascontiguousarray(np.asarray(a, dtype=np.float32))
    x = f(x)
    gains = np.stack([f(ffn1_norm)[0], f(mix_norm)[0], f(ffn2_norm)[0], f(final_norm)], axis=0)
    gains = np.ascontiguousarray(gains.reshape(4, KC, 128).transpose(2, 0, 1))
    shared = {
        "ffn1_w_gate": f(ffn1_w_gate)[0], "ffn1_w_up": f(ffn1_w_up)[0], "ffn1_w_down": f(ffn1_w_down)[0],
        "ffn2_w_gate": f(ffn2_w_gate)[0], "ffn2_w_up": f(ffn2_w_up)[0], "ffn2_w_down": f(ffn2_w_down)[0],
        "w_in": f(w_in)[0], "w_out_ret": f(w_out_ret)[0], "w_out_attn": f(w_out_attn)[0], "w_out": f(w_out)[0],
        "gains": gains, "ident": np.eye(128, dtype=np.float32),
    }
    tabs = [_tables(0), _tables(1)]
    in_maps = []
    for c in range(NCORES):
        b, half = c // 2, c % 2
        rot, rtab, rscal, am = tabs[half]
        m = dict(shared)
        m["x"] = np.ascontiguousarray(x[b, half * TOK:(half + 1) * TOK, :])
        m["rot"] = rot; m["rtab"] = rtab; m["rscal"] = rscal; m["amask"] = am
        in_maps.append(m)
    if "nc" not in _CACHE:
        nc, P = build_program(stop=_STOP[0])
        finalize(nc, P)
        _CACHE["nc"] = nc
        _CACHE["used"] = set(P.used_inputs)
    in_maps = [{k: v for k, v in m.items() if k in _CACHE["used"]} for m in in_maps]
    res = run_bass_kernel_spmd(_CACHE["nc"], in_maps, core_ids=list(range(NCORES)))
    out = np.empty((4, SEQ, D), np.float32)
    for c in range(NCORES):
        b, half = c // 2, c % 2
        out[b, half * TOK:(half + 1) * TOK, :] = np.asarray(res.results[c]["y"], dtype=np.float32)
    return out
```

```python
import numpy as np
import ml_dtypes
import concourse.bass as bass
import concourse.mybir as mybir
from concourse.bass_utils import run_bass_kernel_spmd

F32 = mybir.dt.float32
BF16 = mybir.dt.bfloat16
AF = mybir.ActivationFunctionType
ALU = mybir.AluOpType

NCORES = 8
D = 4096
KC = 32
FF = 11008
NFF = FF // 128
TT = 512
NT = 2
TOK = TT * NT
SEQ = 2048
RH = 8
RW = 2048
AW = 1536
INW = 20992
EPS = 1e-6
NEGM = -30000.0

SEC = [("qr", 16), ("kr", 16), ("vr", 16), ("gr", 16), ("qa", 12), ("ka", 12), ("va", 12),
       ("ua", 32), ("ub", 32)]

OFF_KR = 0
OFF_VR = OFF_KR + 16 * 128 * TOK
OFF_KA = OFF_VR + TOK * RW
OFF_VA = OFF_KA + 12 * 128 * TOK
MINE_N = OFF_VA + TOK * AW
assert MINE_N % 1024 == 0
MINE_ROWS = MINE_N // 1024


class Op:
    __slots__ = ("eng", "fn", "deps", "sig", "is_dma", "key", "val", "need", "inc")

    def __init__(self, eng, fn, is_dma, key):
        self.eng = eng
        self.fn = fn
        self.deps = []
        self.sig = 0
        self.is_dma = is_dma
        self.key = key
        self.val = 0
        self.need = False
        self.inc = 16


class Prog:
    def __init__(self):
        self.ops = []
        self.last_w = {}
        self.readers = {}
        self.last_dma = {}
        self.dma_count = {}
        self.bank = 0
        self.default_w = None

    def nextbank(self, avoid=()):
        while self.bank in avoid:
            self.bank = (self.bank + 1) % 8
        b = self.bank
        self.bank = (self.bank + 1) % 8
        return b

    def fence(self):
        op = Op("sp", lambda E: E.nop(), False, None)
        last = {}
        for o in self.ops:
            if o.is_dma:
                last[("d", o.key)] = o
            else:
                last[("e", o.eng)] = o
        for o in last.values():
            o.need = True
            op.deps.append(o)
        op.need = True
        self.ops.append(op)
        self.last_w = {}
        self.readers = {}
        self.default_w = op

    def add(self, eng, fn, r=(), w=(), dma_key=None, inc=16):
        op = Op(eng, fn, dma_key is not None, dma_key)
        op.inc = inc
        deps = {}
        for k in r:
            lw = self.last_w.get(k, self.default_w)
            if lw is not None:
                deps[id(lw)] = (lw, True)
        for k in w:
            lw = self.last_w.get(k, self.default_w)
            if lw is not None and id(lw) not in deps:
                deps[id(lw)] = (lw, False)
            for rd in self.readers.get(k, ()):
                if id(rd) not in deps:
                    deps[id(rd)] = (rd, False)
        if self.default_w is not None and id(self.default_w) not in deps:
            deps[id(self.default_w)] = (self.default_w, True)
        if dma_key is not None:
            prev = self.last_dma.get(dma_key)
            if prev is not None:
                deps[id(prev)] = (prev, True)
            self.last_dma[dma_key] = op
            n = self.dma_count.get(dma_key, 0) + inc
            self.dma_count[dma_key] = n
            op.val = n
        for d, raw in deps.values():
            if d is op:
                continue
            if (not d.is_dma) and d.eng == eng:
                if eng == "pe":
                    continue
            d.need = True
            op.deps.append(d)
        for k in w:
            self.last_w[k] = op
            self.readers[k] = []
        for k in r:
            self.readers.setdefault(k, []).append(op)
        self.ops.append(op)
        return op

    def emit(self, nc, block, engsems, dmasems):
        cnt = {}
        for op in self.ops:
            if not op.is_dma and op.need:
                cnt[op.eng] = cnt.get(op.eng, 0) + 1
                op.sig = cnt[op.eng]
        by_eng = {}
        for op in self.ops:
            by_eng.setdefault(op.eng, []).append(op)

        def run(engname, E):
            waited = {}
            for op in by_eng.get(engname, []):
                for d in op.deps:
                    if d.is_dma:
                        sem, val, sk = dmasems[d.key], d.val, ("d", d.key)
                    else:
                        sem, val, sk = engsems[d.eng], d.sig, ("e", d.eng)
                    if waited.get(sk, 0) >= val:
                        continue
                    waited[sk] = val
                    E.wait_ge(sem, val)
                ins = op.fn(E)
                if op.is_dma:
                    ins.then_inc(dmasems[op.key], op.inc)
                elif op.need:
                    ins.then_inc(engsems[op.eng], 1)

        @block.tensor
        def _(E):
            run("pe", E)

        @block.scalar
        def _(E):
            run("act", E)

        @block.vector
        def _(E):
            run("dve", E)

        @block.gpsimd
        def _(E):
            run("pool", E)

        @block.sync
        def _(E):
            run("sp", E)


def build_program(dbg=None, ncores=NCORES, stop=99):
    nc = bass.Bass("TRN2", target_bir_lowering=False, num_devices=ncores)
    P = Prog()

    used_inputs = []

    class LazyIn:
        def __init__(self, name, shape, dt):
            self.name, self.shape, self.dt, self._ap = name, shape, dt, None

        def get(self):
            if self._ap is None:
                self._ap = nc.dram_tensor(self.name, list(self.shape), self.dt, kind="ExternalInput").ap()
                used_inputs.append(self.name)
            return self._ap

        def __getitem__(self, idx):
            return self.get()[idx]

    def din(name, shape, dt=F32):
        return LazyIn(name, shape, dt)

    x_in = din("x", [TOK, D])
    wg1 = din("ffn1_w_gate", [D, FF]); wu1 = din("ffn1_w_up", [D, FF]); wd1 = din("ffn1_w_down", [FF, D])
    wg2 = din("ffn2_w_gate", [D, FF]); wu2 = din("ffn2_w_up", [D, FF]); wd2 = din("ffn2_w_down", [FF, D])
    w_in = din("w_in", [D, INW])
    w_oa = din("w_out_ret", [RW, D]); w_ob = din("w_out_attn", [512, D]); w_o = din("w_out", [D, D])
    gains_in = din("gains", [128, 4, KC])
    rot_in = din("rot", [128, 2, TOK])
    rtab_in = din("rtab", [RH, 128, 1408])
    rscal_in = din("rscal", [128, 256])
    amask_in = din("amask", [128, 5, 128], BF16)
    ident_in = din("ident", [128, 128])
    y_out = nc.dram_tensor("y", [TOK, D], F32, kind="ExternalOutput").ap()

    def dscr(name, n, dt=BF16):
        return nc.dram_tensor(name, [n], dt).ap()

    QRd = dscr("QRd", 16 * 128 * TOK).rearrange("(c p t) -> c p t", p=128, t=TOK)
    GRd = dscr("GRd", 16 * 128 * TOK).rearrange("(c p t) -> c p t", p=128, t=TOK)
    QAd = dscr("QAd", 12 * 128 * TOK).rearrange("(c p t) -> c p t", p=128, t=TOK)
    UAd = dscr("UAd", 32 * 128 * TOK).rearrange("(c p t) -> c p t", p=128, t=TOK)
    UBd = dscr("UBd", 32 * 128 * TOK).rearrange("(c p t) -> c p t", p=128, t=TOK)
    YRd = dscr("YRd", 16 * 128 * TOK).rearrange("(c p t) -> c p t", p=128, t=TOK)
    YAd = dscr("YAd", 4 * 128 * TOK).rearrange("(c p t) -> c p t", p=128, t=TOK)
    XSd = dscr("XSd", NT * 128 * KC * TT, F32).rearrange("(n p f) -> n p f", p=128, f=KC * TT)
    SECROWS = {"kr": 1024, "vr": 1024, "ka": 768, "va": 768}
    CCT = {}
    for nm, rows in SECROWS.items():
        for t_ in range(NT):
            CCT[(nm, t_)] = (nc.dram_tensor("M_%s%d" % (nm, t_), [rows, 1024], BF16),
                             nc.dram_tensor("G_%s%d" % (nm, t_), [2 * rows, 1024], BF16))

    def secview(nm, t_, prev):
        rows = SECROWS[nm]
        m, g = CCT[(nm, t_)]
        flat = (g.ap()[0:rows, :] if prev else m.ap()).rearrange("r c -> (r c)")
        if nm in ("kr", "ka"):
            return flat.rearrange("(c p t) -> c p t", p=128, t=TT)
        return flat.rearrange("(t e) -> t e", e=(RW if nm == "vr" else AW))

    KRm = [secview("kr", t_, False) for t_ in range(NT)]; KRp = [secview("kr", t_, True) for t_ in range(NT)]
    VRm = [secview("vr", t_, False) for t_ in range(NT)]; VRp = [secview("vr", t_, True) for t_ in range(NT)]
    KAm = [secview("ka", t_, False) for t_ in range(NT)]; KAp = [secview("ka", t_, True) for t_ in range(NT)]
    VAm = [secview("va", t_, False) for t_ in range(NT)]; VAp = [secview("va", t_, True) for t_ in range(NT)]

    AR = nc.alloc_sbuf_tensor([128, 24576], F32)
    XT = AR[:, 0:16384].rearrange("p (c t) -> p c t", t=TT)
    HTraw = AR[:, 16384:24576]
    HT = HTraw.bitcast(BF16).rearrange("p (c t) -> p c t", t=TT)
    NGU, NWD = 4, 4
    WGU = nc.alloc_sbuf_tensor([128, NGU, 4096], BF16)
    WD = nc.alloc_sbuf_tensor([128, NWD, 4096], BF16)
    ACTB = nc.alloc_sbuf_tensor([128, 2, 2, TT], BF16)
    SG = nc.alloc_sbuf_tensor([128, 2, TT], F32)
    RSTD = nc.alloc_sbuf_tensor([128, TT], F32)
    SQ = nc.alloc_sbuf_tensor([128, 2, TT], BF16)
    TMP = nc.alloc_sbuf_tensor([128, 4, TT], F32)
    OST = nc.alloc_sbuf_tensor([128, 8, TT], BF16)
    UST = nc.alloc_sbuf_tensor([128, 2, 2, TT], BF16)
    GAIN = nc.alloc_sbuf_tensor([128, 4, KC], F32)
    ROT = nc.alloc_sbuf_tensor([128, 2, TOK], F32)
    RSCAL = nc.alloc_sbuf_tensor([128, 256], F32)
    AMASK = nc.alloc_sbuf_tensor([128, 5, 128], BF16)
    IDF = nc.alloc_sbuf_tensor([128, 128], F32)
    IDB = nc.alloc_sbuf_tensor([128, 128], BF16)
    ONES = nc.alloc_sbuf_tensor([128, 128], BF16)
    NEGC = nc.alloc_sbuf_tensor([1, TOK], BF16)
    PS = [nc.alloc_psum_tensor("psb%d" % _i, [128, 512], F32) for _i in range(8)]

    def wgu(s):
        return WGU[:, s, :].rearrange("p (k n) -> p k n", n=128)

    def wd(s):
        return WD[:, s, :]

    state = {"gu": 0, "wd": 0, "ost": 0, "ust": 0}

    def rot_slot(name, n):
        s = state[name]
        state[name] = (s + 1) % n
        return s

    ldn = [0]

    def dma(eng, out, in_, r, w, key):
        if key == "ld":
            key = ("ld", ldn[0] % 16)
            ldn[0] += 1
        P.add(eng, lambda E: E.dma_start(out=out, in_=in_), r=r, w=w, dma_key=key)

    def mm(out, lhsT, rhs, start, stop, r, w, skip=False):
        if skip:
            P.add("pe", lambda E: E.matmul(out, lhsT=lhsT, rhs=rhs, start=start, stop=stop,
                                           skip_group_check=True), r=r, w=w)
        else:
            P.add("pe", lambda E: E.matmul(out, lhsT=lhsT, rhs=rhs, start=start, stop=stop), r=r, w=w)

    def act(out, in_, func, r, w, scale=None, bias=None):
        kw = {}
        if scale is not None:
            kw["scale"] = scale
        if bias is not None:
            kw["bias"] = bias
        P.add("act", lambda E: E.activation(out=out, in_=in_, func=func, **kw), r=r, w=w)

    def tt(out, in0, in1, op, r, w):
        P.add("dve", lambda E: E.tensor_tensor(out=out, in0=in0, in1=in1, op=op), r=r, w=w)

    def stt(out, in0, scalar, in1, op0, op1, r, w):
        P.add("dve", lambda E: E.scalar_tensor_tensor(out=out, in0=in0, scalar=scalar, in1=in1,
                                                      op0=op0, op1=op1), r=r, w=w)

    def ts(out, in0, s1, s2, op0, op1, r, w):
        P.add("dve", lambda E: E.tensor_scalar(out=out, in0=in0, scalar1=s1, scalar2=s2,
                                               op0=op0, op1=op1), r=r, w=w)

    def tss(out, in_, s, op, r, w):
        P.add("dve", lambda E: E.tensor_single_scalar(out=out, in_=in_, scalar=s, op=op), r=r, w=w)

    def vcopy(out, in_, r, w):
        P.add("dve", lambda E: E.tensor_copy(out=out, in_=in_), r=r, w=w)

    cp_state = [0]

    def anycopy(out, in_, r, w):
        cp_state[0] ^= 1
        if cp_state[0]:
            vcopy(out, in_, r, w)
        else:
            act(out, in_, AF.Copy, r, w)

    xtk = lambda c: ("xt", c)
    htk = lambda c: ("ht", c)

    dma("sp", GAIN[:], gains_in.get(), [], ["gain"], "ld")
    dma("sp", ROT[:], rot_in.get(), [], ["rot"], "ld")
    dma("sp", RSCAL[:], rscal_in.get(), [], ["rscal"], "ld")
    dma("sp", AMASK[:], amask_in.get(), [], ["amask"], "ld")
    dma("sp", IDF[:], ident_in.get(), [], ["idf"], "ld")
    vcopy(IDB[:], IDF[:], ["idf"], ["idb"])
    P.add("dve", lambda E: E.memset(ONES[:], 1.0), r=[], w=["ones"])

    def load_x_tile(t):
        for tc in range(4):
            gtc = t * 4 + tc
            st = tc % 2
            stg = HTraw[:, st * 4096:(st + 1) * 4096]
            keys = [htk(c) for c in range(16 * st, 16 * st + 16)]
            dma("sp", stg, x_in[gtc * 128:(gtc + 1) * 128, :], [], keys, ("in", st))
            for c0 in range(0, KC, 4):
                b = P.nextbank()
                for i in range(4):
                    c = c0 + i
                    P.add("pe", lambda E, b=b, i=i, c=c, stg=stg: E.transpose(
                        out=PS[b][:, i * 128:(i + 1) * 128], in_=stg[:, c * 128:(c + 1) * 128],
                        identity=IDF[:]), r=keys + ["idf"], w=[("ps", b)])
                anycopy(XT[:, c0:c0 + 4, tc * 128:(tc + 1) * 128],
                        PS[b][:, :].rearrange("p (a b) -> p a b", b=128),
                        [("ps", b)], [xtk(c) for c in range(c0, c0 + 4)])

    def rmsnorm_stats(nchunks, src_fn, src_keys_fn, dim):
        b = P.nextbank()
        for c in range(nchunks):
            s = c % 2
            act(SQ[:, s, :], src_fn(c), AF.Square, src_keys_fn(c), [("sq", s)])
            mm(PS[b][:, :], ONES[:, :], SQ[:, s, :], c == 0, c == nchunks - 1,
               [("sq", s), "ones"], [("ps", b)])
        ts(RSTD[:, :], PS[b][:, :], 1.0 / dim, EPS, ALU.mult, ALU.add, [("ps", b)], ["rstd"])
        act(RSTD[:, :], RSTD[:, :], AF.Sqrt, ["rstd"], ["rstd"])
        P.add("dve", lambda E: E.reciprocal(out=RSTD[:, :], in_=RSTD[:, :]), r=["rstd"], w=["rstd"])

    def rmsnorm_to_ht(gi):
        rmsnorm_stats(KC, lambda c: XT[:, c, :], lambda c: [xtk(c)], D)
        for c in range(KC):
            stt(HT[:, c, :], XT[:, c, :], GAIN[:, gi, c:c + 1], RSTD[:, :], ALU.mult, ALU.mult,
                [xtk(c), "rstd", "gain"], [htk(c)])

    def ffn(wg, wu, wdn):
        G = 2
        ngroups = NFF // G
        for gi in range(ngroups):
            ab = gi % 2
            dslots = []
            for jj in range(G):
                j = gi * G + jj
                sg, su, sd = rot_slot("gu", NGU), rot_slot("gu", NGU), rot_slot("wd", NWD)
                dslots.append(sd)
                dma("pool", wgu(sg), wg[:, j * 128:(j + 1) * 128].rearrange("(k p) n -> p k n", p=128),
                    [], [("gu", sg)], ("gu", sg))
                dma("pool", wgu(su), wu[:, j * 128:(j + 1) * 128].rearrange("(k p) n -> p k n", p=128),
                    [], [("gu", su)], ("gu", su))
                dma("pool", wd(sd), wdn[j * 128:(j + 1) * 128, :], [], [("wd", sd)], ("wd", sd))
                bg, bu = P.nextbank(), P.nextbank()
                for kc in range(KC):
                    mm(PS[bg][:, :], wgu(sg)[:, kc, :], HT[:, kc, :], kc == 0, kc == KC - 1,
                       [("gu", sg), htk(kc)], [("ps", bg)])
                for kc in range(KC):
                    mm(PS[bu][:, :], wgu(su)[:, kc, :], HT[:, kc, :], kc == 0, kc == KC - 1,
                       [("gu", su), htk(kc)], [("ps", bu)])
                act(SG[:, jj, :], PS[bg][:, :], AF.Silu, [("ps", bg)], [("sg", jj)])
                tt(ACTB[:, ab, jj, :], SG[:, jj, :], PS[bu][:, :], ALU.mult,
                   [("sg", jj), ("ps", bu)], [("actb", ab, jj)])
            for m in range(KC):
                bd = P.nextbank()
                for jj in range(G):
                    mm(PS[bd][:, :], wd(dslots[jj])[:, m * 128:(m + 1) * 128], ACTB[:, ab, jj, :],
                       jj == 0, jj == G - 1, [("wd", dslots[jj]), ("actb", ab, jj)], [("ps", bd)])
                stt(XT[:, m, :], PS[bd][:, :], 0.5, XT[:, m, :], ALU.mult, ALU.add,
                    [("ps", bd), xtk(m)], [xtk(m)])

    def store_bf(src, dst, rkeys, wkeys, slot):
        dma("sp", dst, src, rkeys, wkeys, ("ost", slot))

    def win_stage(t):
        tsl = slice(t * TT, (t + 1) * TT)
        col = 0
        for name, n in SEC:
            if name in ("qr", "kr"):
                for h in range(RH):
                    slots, banks = [], []
                    for i in range(2):
                        cc = col + 2 * h + i
                        s = rot_slot("gu", NGU)
                        dma("pool", wgu(s), w_in[:, cc * 128:(cc + 1) * 128].rearrange("(k p) n -> p k n", p=128),
                            [], [("gu", s)], ("gu", s))
                        b = P.nextbank()
                        for kc in range(KC):
                            mm(PS[b][:, :], wgu(s)[:, kc, :], HT[:, kc, :], kc == 0, kc == KC - 1,
                               [("gu", s), htk(kc)], [("ps", b)])
                        slots.append(s); banks.append(b)
                    b1, b2 = banks
                    cos, sin = ROT[:, 0, tsl], ROT[:, 1, tsl]
                    o1, o2 = rot_slot("ost", 8), rot_slot("ost", 8)
                    tt(TMP[:, 0, :], PS[b1][:, :], cos, ALU.mult, [("ps", b1), "rot"], [("tmp", 0)])
                    tt(TMP[:, 1, :], PS[b2][:, :], sin, ALU.mult, [("ps", b2), "rot"], [("tmp", 1)])
                    tt(OST[:, o1, :], TMP[:, 0, :], TMP[:, 1, :], ALU.subtract,
                       [("tmp", 0), ("tmp", 1)], [("ost", o1)])
                    tt(TMP[:, 2, :], PS[b2][:, :], cos, ALU.mult, [("ps", b2), "rot"], [("tmp", 2)])
                    tt(TMP[:, 3, :], PS[b1][:, :], sin, ALU.mult, [("ps", b1), "rot"], [("tmp", 3)])
                    tt(OST[:, o2, :], TMP[:, 2, :], TMP[:, 3, :], ALU.add,
                       [("tmp", 2), ("tmp", 3)], [("ost", o2)])
                    for oo, cx in ((o1, 2 * h), (o2, 2 * h + 1)):
                        dd = QRd[cx][:, tsl] if name == "qr" else KRm[t][cx]
                        store_bf(OST[:, oo, :], dd, [("ost", oo)], [(name, t, cx)], oo)
            elif name in ("gr", "qa", "ka", "ua", "ub"):
                dst = {"gr": GRd, "qa": QAd, "ka": None, "ua": UAd, "ub": UBd}[name]
                func = {"gr": AF.Silu, "qa": AF.Copy, "ka": AF.Copy, "ua": AF.Sigmoid, "ub": AF.Sigmoid}[name]
                for ci in range(n):
                    cc = col + ci
                    s = rot_slot("gu", NGU)
                    dma("pool", wgu(s), w_in[:, cc * 128:(cc + 1) * 128].rearrange("(k p) n -> p k n", p=128),
                        [], [("gu", s)], ("gu", s))
                    b = P.nextbank()
                    for kc in range(KC):
                        mm(PS[b][:, :], wgu(s)[:, kc, :], HT[:, kc, :], kc == 0, kc == KC - 1,
                           [("gu", s), htk(kc)], [("ps", b)])
                    o = rot_slot("ost", 8)
                    if func == AF.Copy:
                        vcopy(OST[:, o, :], PS[b][:, :], [("ps", b)], [("ost", o)])
                    else:
                        act(OST[:, o, :], PS[b][:, :], func, [("ps", b)], [("ost", o)])
                    dd = KAm[t][ci] if name == "ka" else dst[ci][:, tsl]
                    store_bf(OST[:, o, :], dd, [("ost", o)], [(name, t, ci)], o)
            else:
                dst = VRm[t] if name == "vr" else VAm[t]
                for ci in range(n):
                    cc = col + ci
                    s = rot_slot("gu", NGU)
                    dma("pool", wgu(s), w_in[:, cc * 128:(cc + 1) * 128].rearrange("(k p) n -> p k n", p=128),
                        [], [("gu", s)], ("gu", s))
                    b = P.nextbank()
                    for tc in range(4):
                        for kc in range(KC):
                            mm(PS[b][:, tc * 128:(tc + 1) * 128], HT[:, kc, tc * 128:(tc + 1) * 128],
                               wgu(s)[:, kc, :], kc == 0, kc == KC - 1,
                               [("gu", s), htk(kc)], [("ps", b)])
                    o = rot_slot("ost", 8)
                    anycopy(OST[:, o, :], PS[b][:, :], [("ps", b)], [("ost", o)])
                    store_bf(OST[:, o, :].rearrange("p (a e) -> p a e", e=128),
                             dst[:, ci * 128:(ci + 1) * 128].rearrange("(a p) e -> p a e", p=128),
                             [("ost", o)], [(name, t, ci)], o)
            col += n
            if name in SECROWS:
                m_, g_ = CCT[(name, t)]
                P.add("pool", lambda E, m_=m_, g_=g_: E.collective_compute(
                    "AllGather", ALU.bypass, replica_groups=[[2 * i, 2 * i + 1] for i in range(ncores // 2)],
                    ins=[m_.ap().opt()], outs=[g_.ap().opt()]),
                    r=[(name, t, c) for c in range(n)], w=[("g_" + name, t)], dma_key=("cc", name, t), inc=1)

    def emit_out(t):
        for tc in range(4):
            gtc = t * 4 + tc
            st = tc % 2
            stg = HTraw[:, st * 4096:(st + 1) * 4096]
            keys = [htk(c) for c in range(16 * st, 16 * st + 16)]
            for c0 in range(0, KC, 4):
                b = P.nextbank()
                for i in range(4):
                    c = c0 + i
                    P.add("pe", lambda E, b=b, i=i, c=c, tc=tc: E.transpose(
                        out=PS[b][:, i * 128:(i + 1) * 128], in_=XT[:, c, tc * 128:(tc + 1) * 128],
                        identity=IDF[:]), r=[xtk(c), "idf"], w=[("ps", b)])
                anycopy(stg[:, c0 * 128:(c0 + 4) * 128], PS[b][:, :], [("ps", b)], keys)
            dma("sp", y_out[gtc * 128:(gtc + 1) * 128, :], stg, keys, [("y", gtc)], ("in", st))

    def finish():
        P.add("sp", lambda E: E.nop(), r=[("y", g) for g in range(8)], w=[])
        P.used_inputs = list(used_inputs)
        return nc, P

    for t in range(NT):
        load_x_tile(t)
        if stop == 1:
            emit_out(t); return finish()
        rmsnorm_to_ht(0)
        ffn(wg1, wu1, wd1)
        if stop == 2:
            emit_out(t); return finish()
        rmsnorm_to_ht(1)
        win_stage(t)
        if stop == 3:
            emit_out(t); return finish()
        for q4 in range(4):
            dma("sp", XSd[t][:, q4 * 4096:(q4 + 1) * 4096], AR[:, q4 * 4096:(q4 + 1) * 4096],
                [xtk(c) for c in range(q4 * 8, q4 * 8 + 8)], [("xs", t, q4)], ("xs", q4))
    if stop == 4:
        P.fence(); emit_out(1); return finish()
    all_ar = [xtk(c) for c in range(KC)] + [htk(c) for c in range(KC)]
    cur = [0]

    def carve(nbytes, dt, shape_str=None, **kw):
        ncol = nbytes // 4
        v = AR[:, cur[0]:cur[0] + ncol]
        cur[0] += ncol
        assert cur[0] <= 24576
        if dt == BF16:
            v = v.bitcast(BF16)
        if shape_str:
            v = v.rearrange(shape_str, **kw)
        return v

    Rq = [carve(4096, BF16, "p (i t) -> p i t", t=TOK) for _ in range(2)]
    Rk = [carve(8192, BF16, "p (i t) -> p i t", t=2 * TOK) for _ in range(2)]
    Rv = [carve(8192, BF16, "p (j e) -> p j e", e=256) for _ in range(2)]
    Rg = [carve(4096, BF16, "p (i t) -> p i t", t=TOK) for _ in range(2)]
    Rtab = [carve(5632, F32) for _ in range(2)]
    PT = carve(4096, BF16, "p (i t) -> p i t", t=TT)
    YF = carve(4096, F32, "p (i t) -> p i t", t=TT)
    ret_end = cur[0]

    P.fence()

    def arw(keys):
        return list(keys)

    def ret_load(h):
        s = h % 2
        for i in range(2):
            c = 2 * h + i
            dma("sp", Rq[s][:, i, :], QRd[c], [("qr", c)], arw([("rq", s)]), "ld")
            for t_ in range(NT):
                dma("sp", Rk[s][:, i, t_ * TT:(t_ + 1) * TT], KRp[t_][c], [("g_kr", t_)], [("rk", s)], "ld")
                dma("sp", Rk[s][:, i, TOK + t_ * TT:TOK + (t_ + 1) * TT], KRm[t_][c], [], [("rk", s)], "ld")
            dma("sp", Rg[s][:, i, :], GRd[c], [("gr", c)], [("rg", s)], "ld")
        for t_ in range(NT):
            dma("sp", Rv[s][:, 4 * t_:4 * t_ + 4, :],
                VRp[t_][:, h * 256:(h + 1) * 256].rearrange("(j p) e -> p j e", p=128),
                [("g_vr", t_)], [("rv", s)], "ld")
            dma("sp", Rv[s][:, 8 + 4 * t_:12 + 4 * t_, :],
                VRm[t_][:, h * 256:(h + 1) * 256].rearrange("(j p) e -> p j e", p=128),
                [], [("rv", s)], "ld")
        dma("sp", Rtab[s][:, :], rtab_in[h], [], [("rtab", s)], "ld")

    def retention_head(h):
        s = h % 2
        q, k, v, g, tab = Rq[s], Rk[s], Rv[s], Rg[s], Rtab[s]
        for qb in range(2):
            qs = slice(qb * TT, (qb + 1) * TT)
            by = [P.nextbank(), P.nextbank()]
            jlist = list(range(8)) + [8 + j for j in range(4 * qb + 4)]
            def r_scores(idx, jj):
                bs = P.nextbank(avoid=by)
                for i in range(2):
                    mm(PS[bs][:, :], k[:, i, jj * 128:(jj + 1) * 128], q[:, i, qs], i == 0, i == 1,
                       [("rk", s), ("rq", s)], [("ps", bs)])
                j = jj - 8
                if jj < 8 or j < 4 * qb:
                    tv = tab[:, 0:512]
                else:
                    m = j - 4 * qb
                    tv = tab[:, 512 + 384 - 128 * m: 512 + 384 - 128 * m + 512]
                colx = (h * 16 + jj) * 2 + qb
                pi = idx % 4
                stt(PT[:, pi, :], PS[bs][:, :], RSCAL[:, colx:colx + 1], tv, ALU.mult, ALU.mult,
                    [("ps", bs), "rscal", ("rtab", s)], [("pt", pi)])

            def r_pv(idx, jj):
                pi = idx % 4
                for e in range(2):
                    mm(PS[by[e]][:, :], v[:, jj, e * 128:(e + 1) * 128], PT[:, pi, :],
                       idx == 0, idx == len(jlist) - 1, [("rv", s), ("pt", pi)], [("ps", by[e])])

            for idx in range(len(jlist) + 1):
                if idx < len(jlist):
                    r_scores(idx, jlist[idx])
                if idx >= 1:
                    r_pv(idx - 1, jlist[idx - 1])
            for e in range(2):
                act(YF[:, e, :], PS[by[e]][:, :], AF.Copy, [("ps", by[e])], [("yf", e)])
            rmsnorm_stats(2, lambda e: YF[:, e, :], lambda e: [("yf", e)], 256)
            for e in range(2):
                o = rot_slot("ost", 8)
                stt(TMP[:, e, :], YF[:, e, :], 1.0, RSTD[:, :], ALU.mult, ALU.mult,
                    [("yf", e), "rstd"], [("tmp", e)])
                tt(OST[:, o, :], TMP[:, e, :], g[:, e, qs], ALU.mult, [("tmp", e), ("rg", s)], [("ost", o)])
                store_bf(OST[:, o, :], YRd[2 * h + e][:, qs], [("ost", o)], [("yr", 2 * h + e, qb)], o)

    ret_load(0)
    for h in range(RH):
        if h + 1 < RH:
            ret_load(h + 1)
        retention_head(h)

    if stop == 5:
        P.fence()
        for q4 in range(4):
            dma("sp", AR[:, q4 * 4096:(q4 + 1) * 4096], XSd[1][:, q4 * 4096:(q4 + 1) * 4096],
                [("xs", 1, q4)], [xtk(c) for c in range(q4 * 8, q4 * 8 + 8)], ("xs", q4))
        emit_out(1); return finish()
    cur[0] = 0
    Aq = [carve(3 * 2048, BF16, "p (g t) -> p g t", t=TOK) for _ in range(2)]
    Ak = [carve(3 * 4096, BF16, "p (g t) -> p g t", t=2 * TOK) for _ in range(2)]
    Av0 = [carve(4096, BF16, "p (j e) -> p j e", e=128) for _ in range(2)]
    Av1 = [carve(4096, BF16, "p (c k e) -> p c k e", k=4, e=128) for _ in range(2)]
    Av2 = [carve(4096, BF16, "p (c e) -> p c e", e=128) for _ in range(2)]
    PA = carve(2048, BF16, "p (i t) -> p i t", t=128)
    SQW = carve(1024, BF16)
    ROW = carve(5 * TOK * 4 + 64, F32)
    ret_keys = [("rq", s) for s in range(2)] + [("rk", s) for s in range(2)] + [("rv", s) for s in range(2)] + \
               [("rg", s) for s in range(2)] + [("rtab", s) for s in range(2)] + \
               [("pt", i) for i in range(4)] + [("yf", e) for e in range(2)]
    P.fence()

    def atw(keys):
        return list(keys)

    def att_load(i):
        s = i % 2
        for g in range(3):
            c = 4 * g + i
            dma("sp", Aq[s][:, g, :], QAd[c], [("qa", c)], atw([("aq", s)]), "ld")
            for t_ in range(NT):
                dma("sp", Ak[s][:, g, t_ * TT:(t_ + 1) * TT], KAp[t_][c], [("g_ka", t_)], [("ak", s)], "ld")
                dma("sp", Ak[s][:, g, TOK + t_ * TT:TOK + (t_ + 1) * TT], KAm[t_][c], [], [("ak", s)], "ld")
        c0, c1, c2 = i * 128, (4 + i) * 128, (8 + i) * 128
        for t_ in range(NT):
            for (src, base, gk) in ((VAp[t_], 0, [("g_va", t_)]), (VAm[t_], 1, [])):
                dma("sp", Av0[s][:, 8 * base + 4 * t_:8 * base + 4 * t_ + 4, :],
                    src[:, c0:c0 + 128].rearrange("(j p) e -> p j e", p=128), gk, [("av0", s)], "ld")
                dma("sp", Av1[s][:, :, 2 * base + t_, :],
                    src[:, c1:c1 + 128].rearrange("(p c) e -> p c e", c=4), gk, [("av1", s)], "ld")
                dma("sp", Av2[s][64 * base + 32 * t_:64 * base + 32 * t_ + 32, :, :],
                    src[:, c2:c2 + 128].rearrange("(p c) e -> p c e", c=16), gk, [("av2", s)], "ld")

    def att_shift(i):
        s = i % 2
        QN = ROW[0:1, 0:3 * TOK].rearrange("p (g t) -> p g t", t=TOK)
        KN = ROW[0:1, 3 * TOK:5 * TOK]
        KM = ROW[0:1, 5 * TOK:5 * TOK + 4]
        for g in range(3):
            for (src, n, dstf) in ((Aq[s][:, g, :], 2, lambda blk: QN[:, g, blk * TT:(blk + 1) * TT]),
                                   (Ak[s][:, g, :], 4, lambda blk: KN[:, blk * TT:(blk + 1) * TT])):
                for blk in range(n):
                    act(SQW[:, 0:TT], src[:, blk * TT:(blk + 1) * TT], AF.Square,
                        [("aq", s), ("ak", s)], ["sqw"])
                    b = P.nextbank()
                    mm(PS[b][0:1, :], ONES[:, 0:1], SQW[:, 0:TT], True, True, ["sqw", "ones"], [("ps", b)])
                    vcopy(dstf(blk), PS[b][0:1, :], [("ps", b)], ["rowscr"])
            P.add("dve", lambda E, g=g: E.reduce_max(out=KM[:, g:g + 1], in_=KN[:, :],
                                                     axis=mybir.AxisListType.X), r=["rowscr"], w=["rowscr"])
            tss(QN[:, g, :], QN[:, g, :], KM[:, g:g + 1], ALU.mult, ["rowscr"], ["rowscr"])
        tt(QN[:, 0, :], QN[:, 0, :], QN[:, 1, :], ALU.max, ["rowscr"], ["rowscr"])
        tt(QN[:, 0, :], QN[:, 0, :], QN[:, 2, :], ALU.max, ["rowscr"], ["rowscr"])
        act(QN[:, 0, :], QN[:, 0, :], AF.Sqrt, ["rowscr"], ["rowscr"])
        tss(NEGC[:, :], QN[:, 0, :], -1.0, ALU.mult, ["rowscr"], ["negc"])

    SC = 128.0 ** -0.5

    def attention_head(i):
        s = i % 2
        for qb in range(2):
            bn, bd = P.nextbank(), P.nextbank()
            first = True
            cnt = 0
            work = []
            for g, r in enumerate((1, 4, 16)):
                nq = TT // r
                nsub = max(1, nq // 128)
                nqs = min(nq, 128)
                for c in range(r):
                    for sub in range(nsub):
                        uo0 = (TT * qb) // r + sub * 128
                        uq0 = TOK // r + uo0
                        if r == 16:
                            tiles = [(0, 3 + qb)]
                        else:
                            kt0 = uq0 // 128 - 1
                            prev = kt0 < (TOK // r) // 128
                            tiles = [(kt0, 0 if prev else 1), (kt0 + 1, 2)]
                        for kt, mk in tiles:
                            work.append((g, r, c, sub, nqs, uo0, kt, mk))
            nwork = len(work)
            def a_scores(wi):
                g, r, c, sub, nqs, uo0, kt, mk = work[wi]
                q0 = c + r * uo0
                qsl = slice(q0, q0 + r * (nqs - 1) + 1, r)
                k0 = c + r * 128 * kt
                ksl = slice(k0, k0 + r * 127 + 1, r)
                bs = P.nextbank(avoid=(bn, bd))
                mm(PS[bs][:, 0:nqs], Ak[s][:, g, ksl], Aq[s][:, g, qsl], True, False,
                   [("ak", s), ("aq", s)], [("ps", bs)], skip=True)
                mm(PS[bs][:, 0:nqs], ONESROW[0:1, :], NEGC[0:1, qsl], False, False,
                   ["onesrow", "negc"], [("ps", bs)], skip=True)
                mm(PS[bs][:, 0:nqs], IDB[:, :], AMASK[:, mk, 0:nqs], False, True,
                   ["idb", "amask"], [("ps", bs)], skip=True)
                pi = wi % 8
                act(PA[:, pi, 0:nqs], PS[bs][:, 0:nqs], AF.Exp, [("ps", bs)], [("pa", pi)], scale=SC)

            def a_pv(wi):
                g, r, c, sub, nqs, uo0, kt, mk = work[wi]
                pi = wi % 8
                o0 = c + r * sub * 128
                osl = slice(o0, o0 + r * (nqs - 1) + 1, r)
                if g == 0:
                    vt = Av0[s][:, kt, :]
                    vk = ("av0", s)
                elif g == 1:
                    vt = Av1[s][:, c, kt, :]
                    vk = ("av1", s)
                else:
                    vt = Av2[s][:, c, :]
                    vk = ("av2", s)
                st = (wi == 0)
                mm(PS[bn][:, osl], vt, PA[:, pi, 0:nqs], st, wi == nwork - 1,
                   [vk, ("pa", pi)], [("ps", bn)], skip=True)
                mm(PS[bd][:, osl], ONES[:, :], PA[:, pi, 0:nqs], st, wi == nwork - 1,
                   ["ones", ("pa", pi)], [("ps", bd)], skip=True)

            for wi in range(nwork + 2):
                if wi < nwork:
                    a_scores(wi)
                if wi >= 2:
                    a_pv(wi - 2)
            P.add("dve", lambda E, bd=bd: E.reciprocal(out=TMP[:, 0, :], in_=PS[bd][:, :]),
                  r=[("ps", bd)], w=[("tmp", 0)])
            o = rot_slot("ost", 8)
            tt(OST[:, o, :], PS[bn][:, :], TMP[:, 0, :], ALU.mult, [("ps", bn), ("tmp", 0)], [("ost", o)])
            store_bf(OST[:, o, :], YAd[i][:, qb * TT:(qb + 1) * TT], [("ost", o)], [("ya", i, qb)], o)

    ONESROW = nc.alloc_sbuf_tensor([1, 128], BF16)
    P.add("dve", lambda E: E.memset(ONESROW[:], 1.0), r=[], w=["onesrow"])
    att_load(0)
    for i in range(4):
        if i + 1 < 4:
            att_load(i + 1)
        att_shift(i)
        attention_head(i)

    if stop == 6:
        P.fence()
        for q4 in range(4):
            dma("sp", AR[:, q4 * 4096:(q4 + 1) * 4096], XSd[1][:, q4 * 4096:(q4 + 1) * 4096],
                [("xs", 1, q4)], [xtk(c) for c in range(q4 * 8, q4 * 8 + 8)], ("xs", q4))
        emit_out(1); return finish()
    att_keys = [("aq", s) for s in range(2)] + [("ak", s) for s in range(2)] + \
               [("av0", s) for s in range(2)] + [("av1", s) for s in range(2)] + \
               [("av2", s) for s in range(2)] + [("pa", i) for i in range(8)] + ["sqw"]
    YRt = AR[:, 0:4096].bitcast(BF16).rearrange("p (c t) -> p c t", t=TT)
    YAt = AR[:, 4096:5120].bitcast(BF16).rearrange("p (c t) -> p c t", t=TT)
    P.fence()
    for t in range(NT):
        tsl = slice(t * TT, (t + 1) * TT)
        extra = []
        for c in range(16):
            dma("sp", YRt[:, c, :], YRd[c][:, tsl], [("yr", c, t)], [xtk(c // 2)] + (extra if c == 0 else []),
                "ld")
        for c in range(4):
            dma("sp", YAt[:, c, :], YAd[c][:, tsl], [("ya", c, t)], [xtk(8 + c // 2)], "ld")
        for m in range(KC):
            sa, sb = rot_slot("gu", NGU), rot_slot("gu", NGU)
            wa = WGU[:, sa, 0:2048].rearrange("p (k n) -> p k n", n=128)
            wb = WGU[:, sb, 0:512].rearrange("p (k n) -> p k n", n=128)
            dma("pool", wa, w_oa[:, m * 128:(m + 1) * 128].rearrange("(k p) n -> p k n", p=128),
                [], [("gu", sa)], ("gu", sa))
            dma("pool", wb, w_ob[:, m * 128:(m + 1) * 128].rearrange("(k p) n -> p k n", p=128),
                [], [("gu", sb)], ("gu", sb))
            us = rot_slot("ust", 2)
            dma("sp", UST[:, us, 0, :], UAd[m][:, tsl], [("ua", m)], [("ust", us)], "ld")
            dma("sp", UST[:, us, 1, :], UBd[m][:, tsl], [("ub", m)], [("ust", us)], "ld")
            ba, bb = P.nextbank(), P.nextbank()
            for kc in range(16):
                mm(PS[ba][:, :], wa[:, kc, :], YRt[:, kc, :], kc == 0, kc == 15,
                   [("gu", sa), xtk(kc // 2)], [("ps", ba)])
            for kc in range(4):
                mm(PS[bb][:, :], wb[:, kc, :], YAt[:, kc, :], kc == 0, kc == 3,
                   [("gu", sb), xtk(8 + kc // 2)], [("ps", bb)])
            tt(TMP[:, 0, :], PS[ba][:, :], UST[:, us, 0, :], ALU.mult, [("ps", ba), ("ust", us)], [("tmp", 0)])
            tt(TMP[:, 1, :], PS[bb][:, :], UST[:, us, 1, :], ALU.mult, [("ps", bb), ("ust", us)], [("tmp", 1)])
            tt(HT[:, m, :], TMP[:, 0, :], TMP[:, 1, :], ALU.add, [("tmp", 0), ("tmp", 1)], [htk(m)])
        for q4 in range(4):
            dma("sp", AR[:, q4 * 4096:(q4 + 1) * 4096], XSd[t][:, q4 * 4096:(q4 + 1) * 4096],
                [("xs", t, q4)], [xtk(c) for c in range(q4 * 8, q4 * 8 + 8)], ("xs", q4))
        for m in range(KC):
            s = rot_slot("gu", NGU)
            dma("pool", wgu(s), w_o[:, m * 128:(m + 1) * 128].rearrange("(k p) n -> p k n", p=128),
                [], [("gu", s)], ("gu", s))
            b = P.nextbank()
            for kc in range(KC):
                mm(PS[b][:, :], wgu(s)[:, kc, :], HT[:, kc, :], kc == 0, kc == KC - 1,
                   [("gu", s), htk(kc)], [("ps", b)])
            tt(XT[:, m, :], PS[b][:, :], XT[:, m, :], ALU.add, [("ps", b), xtk(m)], [xtk(m)])
        rmsnorm_to_ht(2)
        ffn(wg2, wu2, wd2)
        rmsnorm_stats(KC, lambda c: XT[:, c, :], lambda c: [xtk(c)], D)
        for c in range(KC):
            stt(XT[:, c, :], XT[:, c, :], GAIN[:, 3, c:c + 1], RSTD[:, :], ALU.mult, ALU.mult,
                [xtk(c), "rstd", "gain"], [xtk(c)])
        emit_out(t)
    return finish()


def finalize(nc, P):
    dkeys = list(P.dma_count.keys())
    sems = {}
    import contextlib
    with contextlib.ExitStack() as es:
        engs = {e: es.enter_context(nc.semaphore("e_" + e)) for e in ("pe", "act", "dve", "pool", "sp")}
        for i, k in enumerate(dkeys):
            sems[k] = es.enter_context(nc.semaphore("d%d" % i))
        block = es.enter_context(nc.Block())
        P.emit(nc, block, engs, sems)
    return nc


def _tables(half):
    pos = (np.arange(TOK, dtype=np.float32) + np.float32(half * TOK))
    inv = (np.float32(10000.0) ** (-np.arange(0, 256, 2, dtype=np.float32) / np.float32(256))).astype(np.float32)
    ang = pos[None, :] * inv[:, None]
    rot = np.stack([np.cos(ang), np.sin(ang)], axis=1).astype(np.float32)
    hh = np.arange(RH, dtype=np.float64)
    gam = 1.0 - 2.0 ** (-5.0 - hh)
    s = np.arange(128)[:, None].astype(np.float64)
    rtab = np.zeros((RH, 128, 1408), np.float32)
    rscal = np.zeros((RH, 16, 2), np.float64)
    for h in range(RH):
        tq = np.arange(512)[None, :].astype(np.float64)
        rtab[h, :, 0:512] = gam[h] ** (tq - s)
        v = np.arange(896)[None, :].astype(np.float64) - 384.0
        rtab[h, :, 512:] = np.where(v >= s, gam[h] ** np.maximum(v - s, 0.0), 0.0)
        for jj in range(16):
            for qb in range(2):
                j = jj - 8
                if jj < 8 or j < 4 * qb:
                    val = gam[h] ** (TOK + 512 * qb - 128 * jj) / 16.0
                    if jj < 8 and half == 0:
                        val = 0.0
                else:
                    val = 1.0 / 16.0
                rscal[h, jj, qb] = val
    rscal = np.broadcast_to(rscal.reshape(1, 256).astype(np.float32), (128, 256)).copy()
    sidx = np.arange(128)[:, None]
    tidx = np.arange(128)[None, :]
    am = np.zeros((128, 5, 128), np.float32)
    mu = np.where(sidx >= tidx, 0.0, NEGM)
    am[:, 0, :] = mu if half == 1 else NEGM
    am[:, 1, :] = mu
    am[:, 2, :] = np.where(sidx <= tidx, 0.0, NEGM)
    for b in range(2):
        m16 = np.where(sidx <= 64 + 32 * b + tidx, 0.0, NEGM)
        if half == 0:
            m16 = np.where(sidx < 64, NEGM, m16)
        am[:, 3 + b, :] = m16
    return rot, rtab, rscal, am.astype(ml_dtypes.bfloat16)


_CACHE = {}
_STOP = [99]


def kernel(x, ffn1_norm, ffn1_w_gate, ffn1_w_up, ffn1_w_down, mix_norm, w_in, w_out_ret, w_out_attn,
           w_out, ffn2_norm, ffn2_w_gate, ffn2_w_up, ffn2_w_down, final_norm):
    f = lambda a: np.ascontiguousarray(np.asarray(a, dtype=np.float32))
    x = f(x)
    gains = np.stack([f(ffn1_norm)[0], f(mix_norm)[0], f(ffn2_norm)[0], f(final_norm)], axis=0)
    gains = np.ascontiguousarray(gains.reshape(4, KC, 128).transpose(2, 0, 1))
    shared = {
        "ffn1_w_gate": f(ffn1_w_gate)[0], "ffn1_w_up": f(ffn1_w_up)[0], "ffn1_w_down": f(ffn1_w_down)[0],
        "ffn2_w_gate": f(ffn2_w_gate)[0], "ffn2_w_up": f(ffn2_w_up)[0], "ffn2_w_down": f(ffn2_w_down)[0],
        "w_in": f(w_in)[0], "w_out_ret": f(w_out_ret)[0], "w_out_attn": f(w_out_attn)[0], "w_out": f(w_out)[0],
        "gains": gains, "ident": np.eye(128, dtype=np.float32),
    }
    tabs = [_tables(0), _tables(1)]
    in_maps = []
    for c in range(NCORES):
        b, half = c // 2, c % 2
        rot, rtab, rscal, am = tabs[half]
        m = dict(shared)
        m["x"] = np.ascontiguousarray(x[b, half * TOK:(half + 1) * TOK, :])
        m["rot"] = rot; m["rtab"] = rtab; m["rscal"] = rscal; m["amask"] = am
        in_maps.append(m)
    if "nc" not in _CACHE:
        nc, P = build_program(stop=_STOP[0])
        finalize(nc, P)
        _CACHE["nc"] = nc
        _CACHE["used"] = set(P.used_inputs)
    in_maps = [{k: v for k, v in m.items() if k in _CACHE["used"]} for m in in_maps]
    res = run_bass_kernel_spmd(_CACHE["nc"], in_maps, core_ids=list(range(NCORES)))
    out = np.empty((4, SEQ, D), np.float32)
    for c in range(NCORES):
        b, half = c // 2, c % 2
        out[b, half * TOK:(half + 1) * TOK, :] = np.asarray(res.results[c]["y"], dtype=np.float32)
    return out
```

```python
import numpy as np
import ml_dtypes
import concourse.bass as bass
import concourse.mybir as mybir
from concourse.bass_utils import run_bass_kernel_spmd

F32 = mybir.dt.float32
BF16 = mybir.dt.bfloat16
AF = mybir.ActivationFunctionType
ALU = mybir.AluOpType

NCORES = 8
D = 4096
KC = 32
FF = 11008
NFF = FF // 128
TT = 512
NT = 2
TOK = TT * NT
SEQ = 2048
RH = 8
RW = 2048
AW = 1536
INW = 20992
EPS = 1e-6
NEGM = -30000.0

SEC = [("qr", 16), ("kr", 16), ("vr", 16), ("gr", 16), ("qa", 12), ("ka", 12), ("va", 12),
       ("ua", 32), ("ub", 32)]

OFF_KR = 0
OFF_VR = OFF_KR + 16 * 128 * TOK
OFF_KA = OFF_VR + TOK * RW
OFF_VA = OFF_KA + 12 * 128 * TOK
MINE_N = OFF_VA + TOK * AW
assert MINE_N % 1024 == 0
MINE_ROWS = MINE_N // 1024


class Op:
    __slots__ = ("eng", "fn", "deps", "sig", "is_dma", "key", "val", "need", "inc")

    def __init__(self, eng, fn, is_dma, key):
        self.eng = eng
        self.fn = fn
        self.deps = []
        self.sig = 0
        self.is_dma = is_dma
        self.key = key
        self.val = 0
        self.need = False
        self.inc = 16


class Prog:
    def __init__(self):
        self.ops = []
        self.last_w = {}
        self.readers = {}
        self.last_dma = {}
        self.dma_count = {}
        self.bank = 0
        self.default_w = None

    def nextbank(self, avoid=()):
        while self.bank in avoid:
            self.bank = (self.bank + 1) % 8
        b = self.bank
        self.bank = (self.bank + 1) % 8
        return b

    def fence(self):
        op = Op("sp", lambda E: E.nop(), False, None)
        last = {}
        for o in self.ops:
            if o.is_dma:
                last[("d", o.key)] = o
            else:
                last[("e", o.eng)] = o
        for o in last.values():
            o.need = True
            op.deps.append(o)
        op.need = True
        self.ops.append(op)
        self.last_w = {}
        self.readers = {}
        self.default_w = op

    def add(self, eng, fn, r=(), w=(), dma_key=None, inc=16):
        op = Op(eng, fn, dma_key is not None, dma_key)
        op.inc = inc
        deps = {}
        for k in r:
            lw = self.last_w.get(k, self.default_w)
            if lw is not None:
                deps[id(lw)] = (lw, True)
        for k in w:
            lw = self.last_w.get(k, self.default_w)
            if lw is not None and id(lw) not in deps:
                deps[id(lw)] = (lw, False)
            for rd in self.readers.get(k, ()):
                if id(rd) not in deps:
                    deps[id(rd)] = (rd, False)
        if self.default_w is not None and id(self.default_w) not in deps:
            deps[id(self.default_w)] = (self.default_w, True)
        if dma_key is not None:
            prev = self.last_dma.get(dma_key)
            if prev is not None:
                deps[id(prev)] = (prev, True)
            self.last_dma[dma_key] = op
            n = self.dma_count.get(dma_key, 0) + inc
            self.dma_count[dma_key] = n
            op.val = n
        for d, raw in deps.values():
            if d is op:
                continue
            if (not d.is_dma) and d.eng == eng:
                if eng == "pe":
                    continue
            d.need = True
            op.deps.append(d)
        for k in w:
            self.last_w[k] = op
            self.readers[k] = []
        for k in r:
            self.readers.setdefault(k, []).append(op)
        self.ops.append(op)
        return op

    def emit(self, nc, block, engsems, dmasems):
        cnt = {}
        for op in self.ops:
            if not op.is_dma and op.need:
                cnt[op.eng] = cnt.get(op.eng, 0) + 1
                op.sig = cnt[op.eng]
        by_eng = {}
        for op in self.ops:
            by_eng.setdefault(op.eng, []).append(op)

        def run(engname, E):
            waited = {}
            for op in by_eng.get(engname, []):
                for d in op.deps:
                    if d.is_dma:
                        sem, val, sk = dmasems[d.key], d.val, ("d", d.key)
                    else:
                        sem, val, sk = engsems[d.eng], d.sig, ("e", d.eng)
                    if waited.get(sk, 0) >= val:
                        continue
                    waited[sk] = val
                    E.wait_ge(sem, val)
                ins = op.fn(E)
                if op.is_dma:
                    ins.then_inc(dmasems[op.key], op.inc)
                elif op.need:
                    ins.then_inc(engsems[op.eng], 1)

        @block.tensor
        def _(E):
            run("pe", E)

        @block.scalar
        def _(E):
            run("act", E)

        @block.vector
        def _(E):
            run("dve", E)

        @block.gpsimd
        def _(E):
            run("pool", E)

        @block.sync
        def _(E):
            run("sp", E)


def build_program(dbg=None, ncores=NCORES, stop=99):
    nc = bass.Bass("TRN2", target_bir_lowering=False, num_devices=ncores)
    P = Prog()

    used_inputs = []

    class LazyIn:
        def __init__(self, name, shape, dt):
            self.name, self.shape, self.dt, self._ap = name, shape, dt, None

        def get(self):
            if self._ap is None:
                self._ap = nc.dram_tensor(self.name, list(self.shape), self.dt, kind="ExternalInput").ap()
                used_inputs.append(self.name)
            return self._ap

        def __getitem__(self, idx):
            return self.get()[idx]

    def din(name, shape, dt=F32):
        return LazyIn(name, shape, dt)

    x_in = din("x", [TOK, D])
    wg1 = din("ffn1_w_gate", [D, FF]); wu1 = din("ffn1_w_up", [D, FF]); wd1 = din("ffn1_w_down", [FF, D])
    wg2 = din("ffn2_w_gate", [D, FF]); wu2 = din("ffn2_w_up", [D, FF]); wd2 = din("ffn2_w_down", [FF, D])
    w_in = din("w_in", [D, INW])
    w_oa = din("w_out_ret", [RW, D]); w_ob = din("w_out_attn", [512, D]); w_o = din("w_out", [D, D])
    gains_in = din("gains", [128, 4, KC])
    rot_in = din("rot", [128, 2, TOK])
    rtab_in = din("rtab", [RH, 128, 1408])
    rscal_in = din("rscal", [128, 256])
    amask_in = din("amask", [128, 5, 128], BF16)
    ident_in = din("ident", [128, 128])
    y_out = nc.dram_tensor("y", [TOK, D], F32, kind="ExternalOutput").ap()

    def dscr(name, n, dt=BF16):
        return nc.dram_tensor(name, [n], dt).ap()

    QRd = dscr("QRd", 16 * 128 * TOK).rearrange("(c p t) -> c p t", p=128, t=TOK)
    GRd = dscr("GRd", 16 * 128 * TOK).rearrange("(c p t) -> c p t", p=128, t=TOK)
    QAd = dscr("QAd", 12 * 128 * TOK).rearrange("(c p t) -> c p t", p=128, t=TOK)
    UAd = dscr("UAd", 32 * 128 * TOK).rearrange("(c p t) -> c p t", p=128, t=TOK)
    UBd = dscr("UBd", 32 * 128 * TOK).rearrange("(c p t) -> c p t", p=128, t=TOK)
    YRd = dscr("YRd", 16 * 128 * TOK).rearrange("(c p t) -> c p t", p=128, t=TOK)
    YAd = dscr("YAd", 4 * 128 * TOK).rearrange("(c p t) -> c p t", p=128, t=TOK)
    XSd = dscr("XSd", NT * 128 * KC * TT, F32).rearrange("(n p f) -> n p f", p=128, f=KC * TT)
    SECROWS = {"kr": 1024, "vr": 1024, "ka": 768, "va": 768}
    CCT = {}
    for nm, rows in SECROWS.items():
        for t_ in range(NT):
            CCT[(nm, t_)] = (nc.dram_tensor("M_%s%d" % (nm, t_), [rows, 1024], BF16),
                             nc.dram_tensor("G_%s%d" % (nm, t_), [2 * rows, 1024], BF16))

    def secview(nm, t_, prev):
        rows = SECROWS[nm]
        m, g = CCT[(nm, t_)]
        flat = (g.ap()[0:rows, :] if prev else m.ap()).rearrange("r c -> (r c)")
        if nm in ("kr", "ka"):
            return flat.rearrange("(c p t) -> c p t", p=128, t=TT)
        return flat.rearrange("(t e) -> t e", e=(RW if nm == "vr" else AW))

    KRm = [secview("kr", t_, False) for t_ in range(NT)]; KRp = [secview("kr", t_, True) for t_ in range(NT)]
    VRm = [secview("vr", t_, False) for t_ in range(NT)]; VRp = [secview("vr", t_, True) for t_ in range(NT)]
    KAm = [secview("ka", t_, False) for t_ in range(NT)]; KAp = [secview("ka", t_, True) for t_ in range(NT)]
    VAm = [secview("va", t_, False) for t_ in range(NT)]; VAp = [secview("va", t_, True) for t_ in range(NT)]

    AR = nc.alloc_sbuf_tensor([128, 24576], F32)
    XT = AR[:, 0:16384].rearrange("p (c t) -> p c t", t=TT)
    HTraw = AR[:, 16384:24576]
    HT = HTraw.bitcast(BF16).rearrange("p (c t) -> p c t", t=TT)
    NW = 4
    W8 = nc.alloc_sbuf_tensor([128, NW, 8192], BF16)
    ACTB = nc.alloc_sbuf_tensor([128, 2, 2, TT], BF16)
    SG = nc.alloc_sbuf_tensor([128, 2, TT], F32)
    RSTD = nc.alloc_sbuf_tensor([128, TT], F32)
    SQ = nc.alloc_sbuf_tensor([128, 2, TT], BF16)
    TMP = nc.alloc_sbuf_tensor([128, 4, TT], F32)
    OST = nc.alloc_sbuf_tensor([128, 8, TT], BF16)
    UST = nc.alloc_sbuf_tensor([128, 2, 2, TT], BF16)
    GAIN = nc.alloc_sbuf_tensor([128, 4, KC], F32)
    ROT = nc.alloc_sbuf_tensor([128, 2, TOK], F32)
    RSCAL = nc.alloc_sbuf_tensor([128, 256], F32)
    AMASK = nc.alloc_sbuf_tensor([128, 5, 128], BF16)
    IDF = nc.alloc_sbuf_tensor([128, 128], F32)
    IDB = nc.alloc_sbuf_tensor([128, 128], BF16)
    ONES = nc.alloc_sbuf_tensor([128, 128], BF16)
    NEGC = nc.alloc_sbuf_tensor([1, TOK], BF16)
    PS = [nc.alloc_psum_tensor("psb%d" % _i, [128, 512], F32) for _i in range(8)]

    def wcol(d, nk=KC):
        return W8[:, d, 0:nk * 256].rearrange("p (k n) -> p k n", n=256)

    def wrow(d):
        return W8[:, d, :].rearrange("p (j n) -> p j n", n=4096)

    def wload_cols(d, wsrc, c0, nk=KC, off=0):
        dst = W8[:, d, off:off + nk * 256].rearrange("p (k n) -> p k n", n=256)
        dma("pool", dst, wsrc[:, c0 * 128:c0 * 128 + 256].rearrange("(k p) n -> p k n", p=128),
            [], [("w", d)], ("w", d))
        return dst

    state = {"w": 0, "ost": 0, "ust": 0}

    def rot_slot(name, n):
        s = state[name]
        state[name] = (s + 1) % n
        return s

    ldn = [0]

    def dma(eng, out, in_, r, w, key):
        if key == "ld":
            key = ("ld", ldn[0] % 16)
            ldn[0] += 1
        P.add(eng, lambda E: E.dma_start(out=out, in_=in_), r=r, w=w, dma_key=key)

    def mm(out, lhsT, rhs, start, stop, r, w, skip=False):
        if skip:
            P.add("pe", lambda E: E.matmul(out, lhsT=lhsT, rhs=rhs, start=start, stop=stop,
                                           skip_group_check=True), r=r, w=w)
        else:
            P.add("pe", lambda E: E.matmul(out, lhsT=lhsT, rhs=rhs, start=start, stop=stop), r=r, w=w)

    def act(out, in_, func, r, w, scale=None, bias=None):
        kw = {}
        if scale is not None:
            kw["scale"] = scale
        if bias is not None:
            kw["bias"] = bias
        P.add("act", lambda E: E.activation(out=out, in_=in_, func=func, **kw), r=r, w=w)

    def tt(out, in0, in1, op, r, w):
        P.add("dve", lambda E: E.tensor_tensor(out=out, in0=in0, in1=in1, op=op), r=r, w=w)

    def stt(out, in0, scalar, in1, op0, op1, r, w):
        P.add("dve", lambda E: E.scalar_tensor_tensor(out=out, in0=in0, scalar=scalar, in1=in1,
                                                      op0=op0, op1=op1), r=r, w=w)

    def ts(out, in0, s1, s2, op0, op1, r, w):
        P.add("dve", lambda E: E.tensor_scalar(out=out, in0=in0, scalar1=s1, scalar2=s2,
                                               op0=op0, op1=op1), r=r, w=w)

    def tss(out, in_, s, op, r, w):
        P.add("dve", lambda E: E.tensor_single_scalar(out=out, in_=in_, scalar=s, op=op), r=r, w=w)

    def vcopy(out, in_, r, w):
        P.add("dve", lambda E: E.tensor_copy(out=out, in_=in_), r=r, w=w)

    cp_state = [0]

    def anycopy(out, in_, r, w):
        cp_state[0] ^= 1
        if cp_state[0]:
            vcopy(out, in_, r, w)
        else:
            act(out, in_, AF.Copy, r, w)

    xtk = lambda c: ("xt", c)
    htk = lambda c: ("ht", c)

    dma("sp", GAIN[:], gains_in.get(), [], ["gain"], "ld")
    dma("sp", ROT[:], rot_in.get(), [], ["rot"], "ld")
    dma("sp", RSCAL[:], rscal_in.get(), [], ["rscal"], "ld")
    dma("sp", AMASK[:], amask_in.get(), [], ["amask"], "ld")
    dma("sp", IDF[:], ident_in.get(), [], ["idf"], "ld")
    vcopy(IDB[:], IDF[:], ["idf"], ["idb"])
    P.add("dve", lambda E: E.memset(ONES[:], 1.0), r=[], w=["ones"])

    def load_x_tile(t):
        for tc in range(4):
            gtc = t * 4 + tc
            st = tc % 2
            stg = HTraw[:, st * 4096:(st + 1) * 4096]
            keys = [htk(c) for c in range(16 * st, 16 * st + 16)]
            dma("sp", stg, x_in[gtc * 128:(gtc + 1) * 128, :], [], keys, ("in", st))
            for c0 in range(0, KC, 4):
                b = P.nextbank()
                for i in range(4):
                    c = c0 + i
                    P.add("pe", lambda E, b=b, i=i, c=c, stg=stg: E.transpose(
                        out=PS[b][:, i * 128:(i + 1) * 128], in_=stg[:, c * 128:(c + 1) * 128],
                        identity=IDF[:]), r=keys + ["idf"], w=[("ps", b)])
                anycopy(XT[:, c0:c0 + 4, tc * 128:(tc + 1) * 128],
                        PS[b][:, :].rearrange("p (a b) -> p a b", b=128),
                        [("ps", b)], [xtk(c) for c in range(c0, c0 + 4)])

    def rmsnorm_stats(nchunks, src_fn, src_keys_fn, dim):
        b = P.nextbank()
        for c in range(nchunks):
            s = c % 2
            act(SQ[:, s, :], src_fn(c), AF.Square, src_keys_fn(c), [("sq", s)])
            mm(PS[b][:, :], ONES[:, :], SQ[:, s, :], c == 0, c == nchunks - 1,
               [("sq", s), "ones"], [("ps", b)])
        ts(RSTD[:, :], PS[b][:, :], 1.0 / dim, EPS, ALU.mult, ALU.add, [("ps", b)], ["rstd"])
        act(RSTD[:, :], RSTD[:, :], AF.Sqrt, ["rstd"], ["rstd"])
        P.add("dve", lambda E: E.reciprocal(out=RSTD[:, :], in_=RSTD[:, :]), r=["rstd"], w=["rstd"])

    def rmsnorm_to_ht(gi):
        rmsnorm_stats(KC, lambda c: XT[:, c, :], lambda c: [xtk(c)], D)
        for c in range(KC):
            stt(HT[:, c, :], XT[:, c, :], GAIN[:, gi, c:c + 1], RSTD[:, :], ALU.mult, ALU.mult,
                [xtk(c), "rstd", "gain"], [htk(c)])

    def ffn(wg, wu, wdn):
        G = 2
        ngroups = NFF // G
        for gi in range(ngroups):
            ab = gi % 2
            j0 = gi * G
            dg, du, dd = rot_slot("w", NW), rot_slot("w", NW), rot_slot("w", NW)
            gv = wload_cols(dg, wg, j0)
            uv = wload_cols(du, wu, j0)
            dma("pool", wrow(dd), wdn[j0 * 128:(j0 + 2) * 128, :].rearrange("(j p) n -> p j n", p=128),
                [], [("w", dd)], ("w", dd))
            dv = wrow(dd)
            for jj in range(G):
                bg, bu = P.nextbank(), P.nextbank()
                for kc in range(KC):
                    mm(PS[bg][:, :], gv[:, kc, jj * 128:(jj + 1) * 128], HT[:, kc, :], kc == 0, kc == KC - 1,
                       [("w", dg), htk(kc)], [("ps", bg)])
                for kc in range(KC):
                    mm(PS[bu][:, :], uv[:, kc, jj * 128:(jj + 1) * 128], HT[:, kc, :], kc == 0, kc == KC - 1,
                       [("w", du), htk(kc)], [("ps", bu)])
                act(SG[:, jj, :], PS[bg][:, :], AF.Silu, [("ps", bg)], [("sg", jj)])
                tt(ACTB[:, ab, jj, :], SG[:, jj, :], PS[bu][:, :], ALU.mult,
                   [("sg", jj), ("ps", bu)], [("actb", ab, jj)])
            for m in range(KC):
                bd = P.nextbank()
                for jj in range(G):
                    mm(PS[bd][:, :], dv[:, jj, m * 128:(m + 1) * 128], ACTB[:, ab, jj, :],
                       jj == 0, jj == G - 1, [("w", dd), ("actb", ab, jj)], [("ps", bd)])
                stt(XT[:, m, :], PS[bd][:, :], 0.5, XT[:, m, :], ALU.mult, ALU.add,
                    [("ps", bd), xtk(m)], [xtk(m)])

    def store_bf(src, dst, rkeys, wkeys, slot):
        dma("sp", dst, src, rkeys, wkeys, ("ost", slot))

    def win_stage(t):
        tsl = slice(t * TT, (t + 1) * TT)
        col = 0
        for name, n in SEC:
            if name in ("qr", "kr"):
                for h in range(RH):
                    banks = []
                    d = rot_slot("w", NW)
                    wv = wload_cols(d, w_in, col + 2 * h)
                    for i in range(2):
                        b = P.nextbank()
                        for kc in range(KC):
                            mm(PS[b][:, :], wv[:, kc, i * 128:(i + 1) * 128], HT[:, kc, :], kc == 0, kc == KC - 1,
                               [("w", d), htk(kc)], [("ps", b)])
                        banks.append(b)
                    b1, b2 = banks
                    cos, sin = ROT[:, 0, tsl], ROT[:, 1, tsl]
                    o1, o2 = rot_slot("ost", 8), rot_slot("ost", 8)
                    tt(TMP[:, 0, :], PS[b1][:, :], cos, ALU.mult, [("ps", b1), "rot"], [("tmp", 0)])
                    tt(TMP[:, 1, :], PS[b2][:, :], sin, ALU.mult, [("ps", b2), "rot"], [("tmp", 1)])
                    tt(OST[:, o1, :], TMP[:, 0, :], TMP[:, 1, :], ALU.subtract,
                       [("tmp", 0), ("tmp", 1)], [("ost", o1)])
                    tt(TMP[:, 2, :], PS[b2][:, :], cos, ALU.mult, [("ps", b2), "rot"], [("tmp", 2)])
                    tt(TMP[:, 3, :], PS[b1][:, :], sin, ALU.mult, [("ps", b1), "rot"], [("tmp", 3)])
                    tt(OST[:, o2, :], TMP[:, 2, :], TMP[:, 3, :], ALU.add,
                       [("tmp", 2), ("tmp", 3)], [("ost", o2)])
                    for oo, cx in ((o1, 2 * h), (o2, 2 * h + 1)):
                        dd = QRd[cx][:, tsl] if name == "qr" else KRm[t][cx]
                        store_bf(OST[:, oo, :], dd, [("ost", oo)], [(name, t, cx)], oo)
            elif name in ("gr", "qa", "ka", "ua", "ub"):
                dst = {"gr": GRd, "qa": QAd, "ka": None, "ua": UAd, "ub": UBd}[name]
                func = {"gr": AF.Silu, "qa": AF.Copy, "ka": AF.Copy, "ua": AF.Sigmoid, "ub": AF.Sigmoid}[name]
                for ci in range(n):
                    if ci % 2 == 0:
                        d = rot_slot("w", NW)
                        wv = wload_cols(d, w_in, col + ci)
                    u = ci % 2
                    b = P.nextbank()
                    for kc in range(KC):
                        mm(PS[b][:, :], wv[:, kc, u * 128:(u + 1) * 128], HT[:, kc, :], kc == 0, kc == KC - 1,
                           [("w", d), htk(kc)], [("ps", b)])
                    o = rot_slot("ost", 8)
                    if func == AF.Copy:
                        vcopy(OST[:, o, :], PS[b][:, :], [("ps", b)], [("ost", o)])
                    else:
                        act(OST[:, o, :], PS[b][:, :], func, [("ps", b)], [("ost", o)])
                    dd = KAm[t][ci] if name == "ka" else dst[ci][:, tsl]
                    store_bf(OST[:, o, :], dd, [("ost", o)], [(name, t, ci)], o)
            else:
                dst = VRm[t] if name == "vr" else VAm[t]
                for ci in range(n):
                    if ci % 2 == 0:
                        d = rot_slot("w", NW)
                        wv = wload_cols(d, w_in, col + ci)
                    u = ci % 2
                    b = P.nextbank()
                    for tc in range(4):
                        for kc in range(KC):
                            mm(PS[b][:, tc * 128:(tc + 1) * 128], HT[:, kc, tc * 128:(tc + 1) * 128],
                               wv[:, kc, u * 128:(u + 1) * 128], kc == 0, kc == KC - 1,
                               [("w", d), htk(kc)], [("ps", b)])
                    o = rot_slot("ost", 8)
                    anycopy(OST[:, o, :], PS[b][:, :], [("ps", b)], [("ost", o)])
                    store_bf(OST[:, o, :].rearrange("p (a e) -> p a e", e=128),
                             dst[:, ci * 128:(ci + 1) * 128].rearrange("(a p) e -> p a e", p=128),
                             [("ost", o)], [(name, t, ci)], o)
            col += n
            if name in SECROWS:
                m_, g_ = CCT[(name, t)]
                P.add("pool", lambda E, m_=m_, g_=g_: E.collective_compute(
                    "AllGather", ALU.bypass, replica_groups=[[2 * i, 2 * i + 1] for i in range(ncores // 2)],
                    ins=[m_.ap().opt()], outs=[g_.ap().opt()]),
                    r=[(name, t, c) for c in range(n)], w=[("g_" + name, t)], dma_key=("cc", name, t), inc=1)

    def emit_out(t):
        for tc in range(4):
            gtc = t * 4 + tc
            st = tc % 2
            stg = HTraw[:, st * 4096:(st + 1) * 4096]
            keys = [htk(c) for c in range(16 * st, 16 * st + 16)]
            for c0 in range(0, KC, 4):
                b = P.nextbank()
                for i in range(4):
                    c = c0 + i
                    P.add("pe", lambda E, b=b, i=i, c=c, tc=tc: E.transpose(
                        out=PS[b][:, i * 128:(i + 1) * 128], in_=XT[:, c, tc * 128:(tc + 1) * 128],
                        identity=IDF[:]), r=[xtk(c), "idf"], w=[("ps", b)])
                anycopy(stg[:, c0 * 128:(c0 + 4) * 128], PS[b][:, :], [("ps", b)], keys)
            dma("sp", y_out[gtc * 128:(gtc + 1) * 128, :], stg, keys, [("y", gtc)], ("in", st))

    def finish():
        P.add("sp", lambda E: E.nop(), r=[("y", g) for g in range(8)], w=[])
        P.used_inputs = list(used_inputs)
        return nc, P

    for t in range(NT):
        load_x_tile(t)
        if stop == 1:
            emit_out(t); return finish()
        rmsnorm_to_ht(0)
        ffn(wg1, wu1, wd1)
        if stop == 2:
            emit_out(t); return finish()
        rmsnorm_to_ht(1)
        win_stage(t)
        if stop == 3:
            emit_out(t); return finish()
        for q4 in range(4):
            dma("sp", XSd[t][:, q4 * 4096:(q4 + 1) * 4096], AR[:, q4 * 4096:(q4 + 1) * 4096],
                [xtk(c) for c in range(q4 * 8, q4 * 8 + 8)], [("xs", t, q4)], ("xs", q4))
    if stop == 4:
        P.fence(); emit_out(1); return finish()
    all_ar = [xtk(c) for c in range(KC)] + [htk(c) for c in range(KC)]
    cur = [0]

    def carve(nbytes, dt, shape_str=None, **kw):
        ncol = nbytes // 4
        v = AR[:, cur[0]:cur[0] + ncol]
        cur[0] += ncol
        assert cur[0] <= 24576
        if dt == BF16:
            v = v.bitcast(BF16)
        if shape_str:
            v = v.rearrange(shape_str, **kw)
        return v

    Rq = [carve(4096, BF16, "p (i t) -> p i t", t=TOK) for _ in range(2)]
    Rk = [carve(8192, BF16, "p (i t) -> p i t", t=2 * TOK) for _ in range(2)]
    Rv = [carve(8192, BF16, "p (j e) -> p j e", e=256) for _ in range(2)]
    Rg = [carve(4096, BF16, "p (i t) -> p i t", t=TOK) for _ in range(2)]
    Rtab = [carve(5632, F32) for _ in range(2)]
    PT = carve(4096, BF16, "p (i t) -> p i t", t=TT)
    YF = carve(4096, F32, "p (i t) -> p i t", t=TT)
    ret_end = cur[0]

    P.fence()

    def arw(keys):
        return list(keys)

    def ret_load(h):
        s = h % 2
        for i in range(2):
            c = 2 * h + i
            dma("sp", Rq[s][:, i, :], QRd[c], [("qr", c)], arw([("rq", s)]), "ld")
            for t_ in range(NT):
                dma("sp", Rk[s][:, i, t_ * TT:(t_ + 1) * TT], KRp[t_][c], [("g_kr", t_)], [("rk", s)], "ld")
                dma("sp", Rk[s][:, i, TOK + t_ * TT:TOK + (t_ + 1) * TT], KRm[t_][c], [], [("rk", s)], "ld")
            dma("sp", Rg[s][:, i, :], GRd[c], [("gr", c)], [("rg", s)], "ld")
        for t_ in range(NT):
            dma("sp", Rv[s][:, 4 * t_:4 * t_ + 4, :],
                VRp[t_][:, h * 256:(h + 1) * 256].rearrange("(j p) e -> p j e", p=128),
                [("g_vr", t_)], [("rv", s)], "ld")
            dma("sp", Rv[s][:, 8 + 4 * t_:12 + 4 * t_, :],
                VRm[t_][:, h * 256:(h + 1) * 256].rearrange("(j p) e -> p j e", p=128),
                [], [("rv", s)], "ld")
        dma("sp", Rtab[s][:, :], rtab_in[h], [], [("rtab", s)], "ld")

    def retention_head(h):
        s = h % 2
        q, k, v, g, tab = Rq[s], Rk[s], Rv[s], Rg[s], Rtab[s]
        for qb in range(2):
            qs = slice(qb * TT, (qb + 1) * TT)
            by = [P.nextbank(), P.nextbank()]
            jlist = list(range(8)) + [8 + j for j in range(4 * qb + 4)]
            def r_scores(idx, jj):
                bs = P.nextbank(avoid=by)
                for i in range(2):
                    mm(PS[bs][:, :], k[:, i, jj * 128:(jj + 1) * 128], q[:, i, qs], i == 0, i == 1,
                       [("rk", s), ("rq", s)], [("ps", bs)])
                j = jj - 8
                if jj < 8 or j < 4 * qb:
                    tv = tab[:, 0:512]
                else:
                    m = j - 4 * qb
                    tv = tab[:, 512 + 384 - 128 * m: 512 + 384 - 128 * m + 512]
                colx = (h * 16 + jj) * 2 + qb
                pi = idx % 4
                stt(PT[:, pi, :], PS[bs][:, :], RSCAL[:, colx:colx + 1], tv, ALU.mult, ALU.mult,
                    [("ps", bs), "rscal", ("rtab", s)], [("pt", pi)])

            def r_pv(idx, jj):
                pi = idx % 4
                for e in range(2):
                    mm(PS[by[e]][:, :], v[:, jj, e * 128:(e + 1) * 128], PT[:, pi, :],
                       idx == 0, idx == len(jlist) - 1, [("rv", s), ("pt", pi)], [("ps", by[e])])

            for idx in range(len(jlist) + 1):
                if idx < len(jlist):
                    r_scores(idx, jlist[idx])
                if idx >= 1:
                    r_pv(idx - 1, jlist[idx - 1])
            for e in range(2):
                act(YF[:, e, :], PS[by[e]][:, :], AF.Copy, [("ps", by[e])], [("yf", e)])
            rmsnorm_stats(2, lambda e: YF[:, e, :], lambda e: [("yf", e)], 256)
            for e in range(2):
                o = rot_slot("ost", 8)
                stt(TMP[:, e, :], YF[:, e, :], 1.0, RSTD[:, :], ALU.mult, ALU.mult,
                    [("yf", e), "rstd"], [("tmp", e)])
                tt(OST[:, o, :], TMP[:, e, :], g[:, e, qs], ALU.mult, [("tmp", e), ("rg", s)], [("ost", o)])
                store_bf(OST[:, o, :], YRd[2 * h + e][:, qs], [("ost", o)], [("yr", 2 * h + e, qb)], o)

    ret_load(0)
    for h in range(RH):
        if h + 1 < RH:
            ret_load(h + 1)
        retention_head(h)

    if stop == 5:
        P.fence()
        for q4 in range(4):
            dma("sp", AR[:, q4 * 4096:(q4 + 1) * 4096], XSd[1][:, q4 * 4096:(q4 + 1) * 4096],
                [("xs", 1, q4)], [xtk(c) for c in range(q4 * 8, q4 * 8 + 8)], ("xs", q4))
        emit_out(1); return finish()
    cur[0] = 0
    Aq = [carve(3 * 2048, BF16, "p (g t) -> p g t", t=TOK) for _ in range(2)]
    Ak = [carve(3 * 4096, BF16, "p (g t) -> p g t", t=2 * TOK) for _ in range(2)]
    Av0 = [carve(4096, BF16, "p (j e) -> p j e", e=128) for _ in range(2)]
    Av1 = [carve(4096, BF16, "p (c k e) -> p c k e", k=4, e=128) for _ in range(2)]
    Av2 = [carve(4096, BF16, "p (c e) -> p c e", e=128) for _ in range(2)]
    PA = carve(2048, BF16, "p (i t) -> p i t", t=128)
    SQW = carve(1024, BF16)
    ROW = carve(5 * TOK * 4 + 64, F32)
    ret_keys = [("rq", s) for s in range(2)] + [("rk", s) for s in range(2)] + [("rv", s) for s in range(2)] + \
               [("rg", s) for s in range(2)] + [("rtab", s) for s in range(2)] + \
               [("pt", i) for i in range(4)] + [("yf", e) for e in range(2)]
    P.fence()

    def atw(keys):
        return list(keys)

    def att_load(i):
        s = i % 2
        for g in range(3):
            c = 4 * g + i
            dma("sp", Aq[s][:, g, :], QAd[c], [("qa", c)], atw([("aq", s)]), "ld")
            for t_ in range(NT):
                dma("sp", Ak[s][:, g, t_ * TT:(t_ + 1) * TT], KAp[t_][c], [("g_ka", t_)], [("ak", s)], "ld")
                dma("sp", Ak[s][:, g, TOK + t_ * TT:TOK + (t_ + 1) * TT], KAm[t_][c], [], [("ak", s)], "ld")
        c0, c1, c2 = i * 128, (4 + i) * 128, (8 + i) * 128
        for t_ in range(NT):
            for (src, base, gk) in ((VAp[t_], 0, [("g_va", t_)]), (VAm[t_], 1, [])):
                dma("sp", Av0[s][:, 8 * base + 4 * t_:8 * base + 4 * t_ + 4, :],
                    src[:, c0:c0 + 128].rearrange("(j p) e -> p j e", p=128), gk, [("av0", s)], "ld")
                dma("sp", Av1[s][:, :, 2 * base + t_, :],
                    src[:, c1:c1 + 128].rearrange("(p c) e -> p c e", c=4), gk, [("av1", s)], "ld")
                dma("sp", Av2[s][64 * base + 32 * t_:64 * base + 32 * t_ + 32, :, :],
                    src[:, c2:c2 + 128].rearrange("(p c) e -> p c e", c=16), gk, [("av2", s)], "ld")

    def att_shift(i):
        s = i % 2
        QN = ROW[0:1, 0:3 * TOK].rearrange("p (g t) -> p g t", t=TOK)
        KN = ROW[0:1, 3 * TOK:5 * TOK]
        KM = ROW[0:1, 5 * TOK:5 * TOK + 4]
        for g in range(3):
            for (src, n, dstf) in ((Aq[s][:, g, :], 2, lambda blk: QN[:, g, blk * TT:(blk + 1) * TT]),
                                   (Ak[s][:, g, :], 4, lambda blk: KN[:, blk * TT:(blk + 1) * TT])):
                for blk in range(n):
                    act(SQW[:, 0:TT], src[:, blk * TT:(blk + 1) * TT], AF.Square,
                        [("aq", s), ("ak", s)], ["sqw"])
                    b = P.nextbank()
                    mm(PS[b][0:1, :], ONES[:, 0:1], SQW[:, 0:TT], True, True, ["sqw", "ones"], [("ps", b)])
                    vcopy(dstf(blk), PS[b][0:1, :], [("ps", b)], ["rowscr"])
            P.add("dve", lambda E, g=g: E.reduce_max(out=KM[:, g:g + 1], in_=KN[:, :],
                                                     axis=mybir.AxisListType.X), r=["rowscr"], w=["rowscr"])
            tss(QN[:, g, :], QN[:, g, :], KM[:, g:g + 1], ALU.mult, ["rowscr"], ["rowscr"])
        tt(QN[:, 0, :], QN[:, 0, :], QN[:, 1, :], ALU.max, ["rowscr"], ["rowscr"])
        tt(QN[:, 0, :], QN[:, 0, :], QN[:, 2, :], ALU.max, ["rowscr"], ["rowscr"])
        act(QN[:, 0, :], QN[:, 0, :], AF.Sqrt, ["rowscr"], ["rowscr"])
        tss(NEGC[:, :], QN[:, 0, :], -1.0, ALU.mult, ["rowscr"], ["negc"])

    SC = 128.0 ** -0.5

    def attention_head(i):
        s = i % 2
        for qb in range(2):
            bn, bd = P.nextbank(), P.nextbank()
            first = True
            cnt = 0
            work = []
            for g, r in enumerate((1, 4, 16)):
                nq = TT // r
                nsub = max(1, nq // 128)
                nqs = min(nq, 128)
                for c in range(r):
                    for sub in range(nsub):
                        uo0 = (TT * qb) // r + sub * 128
                        uq0 = TOK // r + uo0
                        if r == 16:
                            tiles = [(0, 3 + qb)]
                        else:
                            kt0 = uq0 // 128 - 1
                            prev = kt0 < (TOK // r) // 128
                            tiles = [(kt0, 0 if prev else 1), (kt0 + 1, 2)]
                        for kt, mk in tiles:
                            work.append((g, r, c, sub, nqs, uo0, kt, mk))
            nwork = len(work)
            def a_scores(wi):
                g, r, c, sub, nqs, uo0, kt, mk = work[wi]
                q0 = c + r * uo0
                qsl = slice(q0, q0 + r * (nqs - 1) + 1, r)
                k0 = c + r * 128 * kt
                ksl = slice(k0, k0 + r * 127 + 1, r)
                bs = P.nextbank(avoid=(bn, bd))
                mm(PS[bs][:, 0:nqs], Ak[s][:, g, ksl], Aq[s][:, g, qsl], True, False,
                   [("ak", s), ("aq", s)], [("ps", bs)], skip=True)
                mm(PS[bs][:, 0:nqs], ONESROW[0:1, :], NEGC[0:1, qsl], False, False,
                   ["onesrow", "negc"], [("ps", bs)], skip=True)
                mm(PS[bs][:, 0:nqs], IDB[:, :], AMASK[:, mk, 0:nqs], False, True,
                   ["idb", "amask"], [("ps", bs)], skip=True)
                pi = wi % 8
                act(PA[:, pi, 0:nqs], PS[bs][:, 0:nqs], AF.Exp, [("ps", bs)], [("pa", pi)], scale=SC)

            def a_pv(wi):
                g, r, c, sub, nqs, uo0, kt, mk = work[wi]
                pi = wi % 8
                o0 = c + r * sub * 128
                osl = slice(o0, o0 + r * (nqs - 1) + 1, r)
                if g == 0:
                    vt = Av0[s][:, kt, :]
                    vk = ("av0", s)
                elif g == 1:
                    vt = Av1[s][:, c, kt, :]
                    vk = ("av1", s)
                else:
                    vt = Av2[s][:, c, :]
                    vk = ("av2", s)
                st = (wi == 0)
                mm(PS[bn][:, osl], vt, PA[:, pi, 0:nqs], st, wi == nwork - 1,
                   [vk, ("pa", pi)], [("ps", bn)], skip=True)
                mm(PS[bd][:, osl], ONES[:, :], PA[:, pi, 0:nqs], st, wi == nwork - 1,
                   ["ones", ("pa", pi)], [("ps", bd)], skip=True)

            for wi in range(nwork + 2):
                if wi < nwork:
                    a_scores(wi)
                if wi >= 2:
                    a_pv(wi - 2)
            P.add("dve", lambda E, bd=bd: E.reciprocal(out=TMP[:, 0, :], in_=PS[bd][:, :]),
                  r=[("ps", bd)], w=[("tmp", 0)])
            o = rot_slot("ost", 8)
            tt(OST[:, o, :], PS[bn][:, :], TMP[:, 0, :], ALU.mult, [("ps", bn), ("tmp", 0)], [("ost", o)])
            store_bf(OST[:, o, :], YAd[i][:, qb * TT:(qb + 1) * TT], [("ost", o)], [("ya", i, qb)], o)

    ONESROW = nc.alloc_sbuf_tensor([1, 128], BF16)
    P.add("dve", lambda E: E.memset(ONESROW[:], 1.0), r=[], w=["onesrow"])
    att_load(0)
    for i in range(4):
        if i + 1 < 4:
            att_load(i + 1)
        att_shift(i)
        attention_head(i)

    if stop == 6:
        P.fence()
        for q4 in range(4):
            dma("sp", AR[:, q4 * 4096:(q4 + 1) * 4096], XSd[1][:, q4 * 4096:(q4 + 1) * 4096],
                [("xs", 1, q4)], [xtk(c) for c in range(q4 * 8, q4 * 8 + 8)], ("xs", q4))
        emit_out(1); return finish()
    att_keys = [("aq", s) for s in range(2)] + [("ak", s) for s in range(2)] + \
               [("av0", s) for s in range(2)] + [("av1", s) for s in range(2)] + \
               [("av2", s) for s in range(2)] + [("pa", i) for i in range(8)] + ["sqw"]
    YRt = AR[:, 0:4096].bitcast(BF16).rearrange("p (c t) -> p c t", t=TT)
    YAt = AR[:, 4096:5120].bitcast(BF16).rearrange("p (c t) -> p c t", t=TT)
    P.fence()
    for t in range(NT):
        tsl = slice(t * TT, (t + 1) * TT)
        extra = []
        for c in range(16):
            dma("sp", YRt[:, c, :], YRd[c][:, tsl], [("yr", c, t)], [xtk(c // 2)] + (extra if c == 0 else []),
                "ld")
        for c in range(4):
            dma("sp", YAt[:, c, :], YAd[c][:, tsl], [("ya", c, t)], [xtk(8 + c // 2)], "ld")
        for m in range(KC):
            if m % 2 == 0:
                d3 = rot_slot("w", NW)
                wa2 = wload_cols(d3, w_oa, m, nk=16, off=0)
                wb2 = wload_cols(d3, w_ob, m, nk=4, off=4096)
            um = m % 2
            wa = wa2[:, :, um * 128:(um + 1) * 128]
            wb = wb2[:, :, um * 128:(um + 1) * 128]
            sa = sb = d3
            us = rot_slot("ust", 2)
            dma("sp", UST[:, us, 0, :], UAd[m][:, tsl], [("ua", m)], [("ust", us)], "ld")
            dma("sp", UST[:, us, 1, :], UBd[m][:, tsl], [("ub", m)], [("ust", us)], "ld")
            ba, bb = P.nextbank(), P.nextbank()
            for kc in range(16):
                mm(PS[ba][:, :], wa[:, kc, :], YRt[:, kc, :], kc == 0, kc == 15,
                   [("w", sa), xtk(kc // 2)], [("ps", ba)])
            for kc in range(4):
                mm(PS[bb][:, :], wb[:, kc, :], YAt[:, kc, :], kc == 0, kc == 3,
                   [("w", sb), xtk(8 + kc // 2)], [("ps", bb)])
            tt(TMP[:, 0, :], PS[ba][:, :], UST[:, us, 0, :], ALU.mult, [("ps", ba), ("ust", us)], [("tmp", 0)])
            tt(TMP[:, 1, :], PS[bb][:, :], UST[:, us, 1, :], ALU.mult, [("ps", bb), ("ust", us)], [("tmp", 1)])
            tt(HT[:, m, :], TMP[:, 0, :], TMP[:, 1, :], ALU.add, [("tmp", 0), ("tmp", 1)], [htk(m)])
        for q4 in range(4):
            dma("sp", AR[:, q4 * 4096:(q4 + 1) * 4096], XSd[t][:, q4 * 4096:(q4 + 1) * 4096],
                [("xs", t, q4)], [xtk(c) for c in range(q4 * 8, q4 * 8 + 8)], ("xs", q4))
        for m in range(KC):
            if m % 2 == 0:
                d4 = rot_slot("w", NW)
                wv4 = wload_cols(d4, w_o, m)
            um = m % 2
            b = P.nextbank()
            for kc in range(KC):
                mm(PS[b][:, :], wv4[:, kc, um * 128:(um + 1) * 128], HT[:, kc, :], kc == 0, kc == KC - 1,
                   [("w", d4), htk(kc)], [("ps", b)])
            tt(XT[:, m, :], PS[b][:, :], XT[:, m, :], ALU.add, [("ps", b), xtk(m)], [xtk(m)])
        rmsnorm_to_ht(2)
        ffn(wg2, wu2, wd2)
        rmsnorm_stats(KC, lambda c: XT[:, c, :], lambda c: [xtk(c)], D)
        for c in range(KC):
            stt(XT[:, c, :], XT[:, c, :], GAIN[:, 3, c:c + 1], RSTD[:, :], ALU.mult, ALU.mult,
                [xtk(c), "rstd", "gain"], [xtk(c)])
        emit_out(t)
    return finish()


def finalize(nc, P):
    dkeys = list(P.dma_count.keys())
    sems = {}
    import contextlib
    with contextlib.ExitStack() as es:
        engs = {e: es.enter_context(nc.semaphore("e_" + e)) for e in ("pe", "act", "dve", "pool", "sp")}
        for i, k in enumerate(dkeys):
            sems[k] = es.enter_context(nc.semaphore("d%d" % i))
        block = es.enter_context(nc.Block())
        P.emit(nc, block, engs, sems)
    return nc


def _tables(half):
    pos = (np.arange(TOK, dtype=np.float32) + np.float32(half * TOK))
    inv = (np.float32(10000.0) ** (-np.arange(0, 256, 2, dtype=np.float32) / np.float32(256))).astype(np.float32)
    ang = pos[None, :] * inv[:, None]
    rot = np.stack([np.cos(ang), np.sin(ang)], axis=1).astype(np.float32)
    hh = np.arange(RH, dtype=np.float64)
    gam = 1.0 - 2.0 ** (-5.0 - hh)
    s = np.arange(128)[:, None].astype(np.float64)
    rtab = np.zeros((RH, 128, 1408), np.float32)
    rscal = np.zeros((RH, 16, 2), np.float64)
    for h in range(RH):
        tq = np.arange(512)[None, :].astype(np.float64)
        rtab[h, :, 0:512] = gam[h] ** (tq - s)
        v = np.arange(896)[None, :].astype(np.float64) - 384.0
        rtab[h, :, 512:] = np.where(v >= s, gam[h] ** np.maximum(v - s, 0.0), 0.0)
        for jj in range(16):
            for qb in range(2):
                j = jj - 8
                if jj < 8 or j < 4 * qb:
                    val = gam[h] ** (TOK + 512 * qb - 128 * jj) / 16.0
                    if jj < 8 and half == 0:
                        val = 0.0
                else:
                    val = 1.0 / 16.0
                rscal[h, jj, qb] = val
    rscal = np.broadcast_to(rscal.reshape(1, 256).astype(np.float32), (128, 256)).copy()
    sidx = np.arange(128)[:, None]
    tidx = np.arange(128)[None, :]
    am = np.zeros((128, 5, 128), np.float32)
    mu = np.where(sidx >= tidx, 0.0, NEGM)
    am[:, 0, :] = mu if half == 1 else NEGM
    am[:, 1, :] = mu
    am[:, 2, :] = np.where(sidx <= tidx, 0.0, NEGM)
    for b in range(2):
        m16 = np.where(sidx <= 64 + 32 * b + tidx, 0.0, NEGM)
        if half == 0:
            m16 = np.where(sidx < 64, NEGM, m16)
        am[:, 3 + b, :] = m16
    return rot, rtab, rscal, am.astype(ml_dtypes.bfloat16)


_CACHE = {}
_STOP = [99]


def kernel(x, ffn1_norm, ffn1_w_gate, ffn1_w_up, ffn1_w_down, mix_norm, w_in, w_out_ret, w_out_attn,
           w_out, ffn2_norm, ffn2_w_gate, ffn2_w_up, ffn2_w_down, final_norm):
    f = lambda a: np.ascontiguousarray(np.asarray(a, dtype=np.float32))
    x = f(x)
    gains = np.stack([f(ffn1_norm)[0], f(mix_norm)[0], f(ffn2_norm)[0], f(final_norm)], axis=0)
    gains = np.ascontiguousarray(gains.reshape(4, KC, 128).transpose(2, 0, 1))
    shared = {
        "ffn1_w_gate": f(ffn1_w_gate)[0], "ffn1_w_up": f(ffn1_w_up)[0], "ffn1_w_down": f(ffn1_w_down)[0],
        "ffn2_w_gate": f(ffn2_w_gate)[0], "ffn2_w_up": f(ffn2_w_up)[0], "ffn2_w_down": f(ffn2_w_down)[0],
        "w_in": f(w_in)[0], "w_out_ret": f(w_out_ret)[0], "w_out_attn": f(w_out_attn)[0], "w_out": f(w_out)[0],
        "gains": gains, "ident": np.eye(128, dtype=np.float32),
    }
    tabs = [_tables(0), _tables(1)]
    in_maps = []
    for c in range(NCORES):
        b, half = c // 2, c % 2
        rot, rtab, rscal, am = tabs[half]
        m = dict(shared)
        m["x"] = np.ascontiguousarray(x[b, half * TOK:(half + 1) * TOK, :])
        m["rot"] = rot; m["rtab"] = rtab; m["rscal"] = rscal; m["amask"] = am
        in_maps.append(m)
    if "nc" not in _CACHE:
        nc, P = build_program(stop=_STOP[0])
        finalize(nc, P)
        _CACHE["nc"] = nc
        _CACHE["used"] = set(P.used_inputs)
    in_maps = [{k: v for k, v in m.items() if k in _CACHE["used"]} for m in in_maps]
    res = run_bass_kernel_spmd(_CACHE["nc"], in_maps, core_ids=list(range(NCORES)))
    out = np.empty((4, SEQ, D), np.float32)
    for c in range(NCORES):
        b, half = c // 2, c % 2
        out[b, half * TOK:(half + 1) * TOK, :] = np.asarray(res.results[c]["y"], dtype=np.float32)
    return out
```

```python
import numpy as np
import ml_dtypes
import concourse.bass as bass
import concourse.mybir as mybir
from concourse.bass_utils import run_bass_kernel_spmd

F32 = mybir.dt.float32
BF16 = mybir.dt.bfloat16
AF = mybir.ActivationFunctionType
ALU = mybir.AluOpType

NCORES = 8
D = 4096
KC = 32
FF = 11008
NFF = FF // 128
TT = 512
NT = 2
TOK = TT * NT
SEQ = 2048
RH = 8
RW = 2048
AW = 1536
INW = 20992
EPS = 1e-6
NEGM = -30000.0

SEC = [("qr", 16), ("kr", 16), ("vr", 16), ("gr", 16), ("qa", 12), ("ka", 12), ("va", 12),
       ("ua", 32), ("ub", 32)]

OFF_KR = 0
OFF_VR = OFF_KR + 16 * 128 * TOK
OFF_KA = OFF_VR + TOK * RW
OFF_VA = OFF_KA + 12 * 128 * TOK
MINE_N = OFF_VA + TOK * AW
assert MINE_N % 1024 == 0
MINE_ROWS = MINE_N // 1024


class Op:
    __slots__ = ("eng", "fn", "deps", "sig", "is_dma", "key", "val", "need", "inc")

    def __init__(self, eng, fn, is_dma, key):
        self.eng = eng
        self.fn = fn
        self.deps = []
        self.sig = 0
        self.is_dma = is_dma
        self.key = key
        self.val = 0
        self.need = False
        self.inc = 16


class Prog:
    def __init__(self):
        self.ops = []
        self.last_w = {}
        self.readers = {}
        self.last_dma = {}
        self.dma_count = {}
        self.bank = 0
        self.default_w = None

    def nextbank(self, avoid=()):
        while self.bank in avoid:
            self.bank = (self.bank + 1) % 8
        b = self.bank
        self.bank = (self.bank + 1) % 8
        return b

    def fence(self):
        op = Op("sp", lambda E: E.nop(), False, None)
        last = {}
        for o in self.ops:
            if o.is_dma:
                last[("d", o.key)] = o
            else:
                last[("e", o.eng)] = o
        for o in last.values():
            o.need = True
            op.deps.append(o)
        op.need = True
        self.ops.append(op)
        self.last_w = {}
        self.readers = {}
        self.default_w = op

    def add(self, eng, fn, r=(), w=(), dma_key=None, inc=16):
        op = Op(eng, fn, dma_key is not None, dma_key)
        op.inc = inc
        deps = {}
        for k in r:
            lw = self.last_w.get(k, self.default_w)
            if lw is not None:
                deps[id(lw)] = (lw, True)
        for k in w:
            lw = self.last_w.get(k, self.default_w)
            if lw is not None and id(lw) not in deps:
                deps[id(lw)] = (lw, False)
            for rd in self.readers.get(k, ()):
                if id(rd) not in deps:
                    deps[id(rd)] = (rd, False)
        if self.default_w is not None and id(self.default_w) not in deps:
            deps[id(self.default_w)] = (self.default_w, True)
        if dma_key is not None:
            prev = self.last_dma.get(dma_key)
            if prev is not None:
                deps[id(prev)] = (prev, True)
            self.last_dma[dma_key] = op
            n = self.dma_count.get(dma_key, 0) + inc
            self.dma_count[dma_key] = n
            op.val = n
        for d, raw in deps.values():
            if d is op:
                continue
            if (not d.is_dma) and d.eng == eng:
                if eng == "pe":
                    continue
            d.need = True
            op.deps.append(d)
        for k in w:
            self.last_w[k] = op
            self.readers[k] = []
        for k in r:
            self.readers.setdefault(k, []).append(op)
        self.ops.append(op)
        return op

    def emit(self, nc, block, engsems, dmasems):
        cnt = {}
        for op in self.ops:
            if not op.is_dma and op.need:
                cnt[op.eng] = cnt.get(op.eng, 0) + 1
                op.sig = cnt[op.eng]
        by_eng = {}
        for op in self.ops:
            by_eng.setdefault(op.eng, []).append(op)

        def run(engname, E):
            waited = {}
            for op in by_eng.get(engname, []):
                for d in op.deps:
                    if d.is_dma:
                        sem, val, sk = dmasems[d.key], d.val, ("d", d.key)
                    else:
                        sem, val, sk = engsems[d.eng], d.sig, ("e", d.eng)
                    if waited.get(sk, 0) >= val:
                        continue
                    waited[sk] = val
                    E.wait_ge(sem, val)
                ins = op.fn(E)
                if op.is_dma:
                    ins.then_inc(dmasems[op.key], op.inc)
                elif op.need:
                    ins.then_inc(engsems[op.eng], 1)

        @block.tensor
        def _(E):
            run("pe", E)

        @block.scalar
        def _(E):
            run("act", E)

        @block.vector
        def _(E):
            run("dve", E)

        @block.gpsimd
        def _(E):
            run("pool", E)

        @block.sync
        def _(E):
            run("sp", E)


def build_program(dbg=None, ncores=NCORES, stop=99):
    nc = bass.Bass("TRN2", target_bir_lowering=False, num_devices=ncores)
    P = Prog()

    used_inputs = []

    class LazyIn:
        def __init__(self, name, shape, dt):
            self.name, self.shape, self.dt, self._ap = name, shape, dt, None

        def get(self):
            if self._ap is None:
                self._ap = nc.dram_tensor(self.name, list(self.shape), self.dt, kind="ExternalInput").ap()
                used_inputs.append(self.name)
            return self._ap

        def __getitem__(self, idx):
            return self.get()[idx]

    def din(name, shape, dt=F32):
        return LazyIn(name, shape, dt)

    x_in = din("x", [TOK, D])
    wg1 = din("ffn1_w_gate", [D, FF]); wu1 = din("ffn1_w_up", [D, FF]); wd1 = din("ffn1_w_down", [FF, D])
    wg2 = din("ffn2_w_gate", [D, FF]); wu2 = din("ffn2_w_up", [D, FF]); wd2 = din("ffn2_w_down", [FF, D])
    w_in = din("w_in", [D, INW])
    w_oa = din("w_out_ret", [RW, D]); w_ob = din("w_out_attn", [512, D]); w_o = din("w_out", [D, D])
    gains_in = din("gains", [128, 4, KC])
    rot_in = din("rot", [128, 2, TOK])
    rtab_in = din("rtab", [RH, 128, 1408])
    rscal_in = din("rscal", [128, 256])
    amask_in = din("amask", [128, 5, 128], BF16)
    ident_in = din("ident", [128, 128])
    y_out = nc.dram_tensor("y", [TOK, D], F32, kind="ExternalOutput").ap()

    def dscr(name, n, dt=BF16):
        return nc.dram_tensor(name, [n], dt).ap()

    QRd = dscr("QRd", 16 * 128 * TOK).rearrange("(c p t) -> c p t", p=128, t=TOK)
    GRd = dscr("GRd", 16 * 128 * TOK).rearrange("(c p t) -> c p t", p=128, t=TOK)
    QAd = dscr("QAd", 12 * 128 * TOK).rearrange("(c p t) -> c p t", p=128, t=TOK)
    UAd = dscr("UAd", 32 * 128 * TOK).rearrange("(c p t) -> c p t", p=128, t=TOK)
    UBd = dscr("UBd", 32 * 128 * TOK).rearrange("(c p t) -> c p t", p=128, t=TOK)
    YRd = dscr("YRd", 16 * 128 * TOK).rearrange("(c p t) -> c p t", p=128, t=TOK)
    YAd = dscr("YAd", 4 * 128 * TOK).rearrange("(c p t) -> c p t", p=128, t=TOK)
    XSd = dscr("XSd", NT * 128 * KC * TT, F32).rearrange("(n p f) -> n p f", p=128, f=KC * TT)
    SECROWS = {"kr": 1024, "vr": 1024, "ka": 768, "va": 768}
    CCT = {}
    for nm, rows in SECROWS.items():
        for t_ in range(NT):
            CCT[(nm, t_)] = (nc.dram_tensor("M_%s%d" % (nm, t_), [rows, 1024], BF16),
                             nc.dram_tensor("G_%s%d" % (nm, t_), [2 * rows, 1024], BF16))

    def secview(nm, t_, prev):
        rows = SECROWS[nm]
        m, g = CCT[(nm, t_)]
        flat = (g.ap()[0:rows, :] if prev else m.ap()).rearrange("r c -> (r c)")
        if nm in ("kr", "ka"):
            return flat.rearrange("(c p t) -> c p t", p=128, t=TT)
        return flat.rearrange("(t e) -> t e", e=(RW if nm == "vr" else AW))

    KRm = [secview("kr", t_, False) for t_ in range(NT)]; KRp = [secview("kr", t_, True) for t_ in range(NT)]
    VRm = [secview("vr", t_, False) for t_ in range(NT)]; VRp = [secview("vr", t_, True) for t_ in range(NT)]
    KAm = [secview("ka", t_, False) for t_ in range(NT)]; KAp = [secview("ka", t_, True) for t_ in range(NT)]
    VAm = [secview("va", t_, False) for t_ in range(NT)]; VAp = [secview("va", t_, True) for t_ in range(NT)]

    AR = nc.alloc_sbuf_tensor([128, 24576], F32)
    XT = AR[:, 0:16384].rearrange("p (c t) -> p c t", t=TT)
    HTraw = AR[:, 16384:24576]
    HT = HTraw.bitcast(BF16).rearrange("p (c t) -> p c t", t=TT)
    NW = 4
    W8 = nc.alloc_sbuf_tensor([128, NW, 8192], BF16)
    ACTB = nc.alloc_sbuf_tensor([128, 2, 2, TT], BF16)
    SG = nc.alloc_sbuf_tensor([128, 2, TT], F32)
    RSTD = nc.alloc_sbuf_tensor([128, TT], F32)
    SQ = nc.alloc_sbuf_tensor([128, 2, TT], BF16)
    TMP = nc.alloc_sbuf_tensor([128, 4, TT], F32)
    OST = nc.alloc_sbuf_tensor([128, 8, TT], BF16)
    UST = nc.alloc_sbuf_tensor([128, 2, 2, TT], BF16)
    GAIN = nc.alloc_sbuf_tensor([128, 4, KC], F32)
    ROT = nc.alloc_sbuf_tensor([128, 2, TOK], F32)
    RSCAL = nc.alloc_sbuf_tensor([128, 256], F32)
    AMASK = nc.alloc_sbuf_tensor([128, 5, 128], BF16)
    IDF = nc.alloc_sbuf_tensor([128, 128], F32)
    IDB = nc.alloc_sbuf_tensor([128, 128], BF16)
    ONES = nc.alloc_sbuf_tensor([128, 128], BF16)
    NEGC = nc.alloc_sbuf_tensor([1, TOK], BF16)
    PS = [nc.alloc_psum_tensor("psb%d" % _i, [128, 512], F32) for _i in range(8)]

    def wcol(d, nk=KC):
        return W8[:, d, 0:nk * 256].rearrange("p (k n) -> p k n", n=256)

    def wrow(d):
        return W8[:, d, :].rearrange("p (j n) -> p j n", n=4096)

    def wload_cols(d, wsrc, c0, nk=KC, off=0):
        dst = W8[:, d, off:off + nk * 256].rearrange("p (k n) -> p k n", n=256)
        dma("pool", dst, wsrc[:, c0 * 128:c0 * 128 + 256].rearrange("(k p) n -> p k n", p=128),
            [], [("w", d)], ("w", d))
        return dst

    state = {"w": 0, "ost": 0, "ust": 0}

    def rot_slot(name, n):
        s = state[name]
        state[name] = (s + 1) % n
        return s

    ldn = [0]

    def dma(eng, out, in_, r, w, key):
        if key == "ld":
            key = ("ld", ldn[0] % 16)
            ldn[0] += 1
        P.add(eng, lambda E: E.dma_start(out=out, in_=in_), r=r, w=w, dma_key=key)

    def mm(out, lhsT, rhs, start, stop, r, w, skip=False):
        if skip:
            P.add("pe", lambda E: E.matmul(out, lhsT=lhsT, rhs=rhs, start=start, stop=stop,
                                           skip_group_check=True), r=r, w=w)
        else:
            P.add("pe", lambda E: E.matmul(out, lhsT=lhsT, rhs=rhs, start=start, stop=stop), r=r, w=w)

    def act(out, in_, func, r, w, scale=None, bias=None):
        kw = {}
        if scale is not None:
            kw["scale"] = scale
        if bias is not None:
            kw["bias"] = bias
        P.add("act", lambda E: E.activation(out=out, in_=in_, func=func, **kw), r=r, w=w)

    def tt(out, in0, in1, op, r, w):
        P.add("dve", lambda E: E.tensor_tensor(out=out, in0=in0, in1=in1, op=op), r=r, w=w)

    def stt(out, in0, scalar, in1, op0, op1, r, w):
        P.add("dve", lambda E: E.scalar_tensor_tensor(out=out, in0=in0, scalar=scalar, in1=in1,
                                                      op0=op0, op1=op1), r=r, w=w)

    def ts(out, in0, s1, s2, op0, op1, r, w):
        P.add("dve", lambda E: E.tensor_scalar(out=out, in0=in0, scalar1=s1, scalar2=s2,
                                               op0=op0, op1=op1), r=r, w=w)

    def tss(out, in_, s, op, r, w):
        P.add("dve", lambda E: E.tensor_single_scalar(out=out, in_=in_, scalar=s, op=op), r=r, w=w)

    def vcopy(out, in_, r, w):
        P.add("dve", lambda E: E.tensor_copy(out=out, in_=in_), r=r, w=w)

    cp_state = [0]

    def anycopy(out, in_, r, w):
        cp_state[0] ^= 1
        if cp_state[0]:
            vcopy(out, in_, r, w)
        else:
            act(out, in_, AF.Copy, r, w)

    xtk = lambda c: ("xt", c)
    htk = lambda c: ("ht", c)

    dma("sp", GAIN[:], gains_in.get(), [], ["gain"], "ld")
    dma("sp", ROT[:], rot_in.get(), [], ["rot"], "ld")
    dma("sp", RSCAL[:], rscal_in.get(), [], ["rscal"], "ld")
    dma("sp", AMASK[:], amask_in.get(), [], ["amask"], "ld")
    dma("sp", IDF[:], ident_in.get(), [], ["idf"], "ld")
    vcopy(IDB[:], IDF[:], ["idf"], ["idb"])
    P.add("dve", lambda E: E.memset(ONES[:], 1.0), r=[], w=["ones"])

    def load_x_tile(t):
        for tc in range(4):
            gtc = t * 4 + tc
            st = tc % 2
            stg = HTraw[:, st * 4096:(st + 1) * 4096]
            keys = [htk(c) for c in range(16 * st, 16 * st + 16)]
            dma("sp", stg, x_in[gtc * 128:(gtc + 1) * 128, :], [], keys, ("in", st))
            for c0 in range(0, KC, 4):
                b = P.nextbank()
                for i in range(4):
                    c = c0 + i
                    P.add("pe", lambda E, b=b, i=i, c=c, stg=stg: E.transpose(
                        out=PS[b][:, i * 128:(i + 1) * 128], in_=stg[:, c * 128:(c + 1) * 128],
                        identity=IDF[:]), r=keys + ["idf"], w=[("ps", b)])
                anycopy(XT[:, c0:c0 + 4, tc * 128:(tc + 1) * 128],
                        PS[b][:, :].rearrange("p (a b) -> p a b", b=128),
                        [("ps", b)], [xtk(c) for c in range(c0, c0 + 4)])

    def rmsnorm_stats(nchunks, src_fn, src_keys_fn, dim):
        b = P.nextbank()
        for c in range(nchunks):
            s = c % 2
            act(SQ[:, s, :], src_fn(c), AF.Square, src_keys_fn(c), [("sq", s)])
            mm(PS[b][:, :], ONES[:, :], SQ[:, s, :], c == 0, c == nchunks - 1,
               [("sq", s), "ones"], [("ps", b)])
        ts(RSTD[:, :], PS[b][:, :], 1.0 / dim, EPS, ALU.mult, ALU.add, [("ps", b)], ["rstd"])
        act(RSTD[:, :], RSTD[:, :], AF.Sqrt, ["rstd"], ["rstd"])
        P.add("dve", lambda E: E.reciprocal(out=RSTD[:, :], in_=RSTD[:, :]), r=["rstd"], w=["rstd"])

    def rmsnorm_to_ht(gi):
        rmsnorm_stats(KC, lambda c: XT[:, c, :], lambda c: [xtk(c)], D)
        for c in range(KC):
            stt(HT[:, c, :], XT[:, c, :], GAIN[:, gi, c:c + 1], RSTD[:, :], ALU.mult, ALU.mult,
                [xtk(c), "rstd", "gain"], [htk(c)])

    def ffn(wg, wu, wdn):
        G = 2
        ngroups = NFF // G
        for gi in range(ngroups):
            ab = gi % 2
            j0 = gi * G
            dg, du, dd = rot_slot("w", NW), rot_slot("w", NW), rot_slot("w", NW)
            gv = wload_cols(dg, wg, j0)
            uv = wload_cols(du, wu, j0)
            dma("pool", wrow(dd), wdn[j0 * 128:(j0 + 2) * 128, :].rearrange("(j p) n -> p j n", p=128),
                [], [("w", dd)], ("w", dd))
            dv = wrow(dd)
            for jj in range(G):
                bg, bu = P.nextbank(), P.nextbank()
                for kc in range(KC):
                    mm(PS[bg][:, :], gv[:, kc, jj * 128:(jj + 1) * 128], HT[:, kc, :], kc == 0, kc == KC - 1,
                       [("w", dg), htk(kc)], [("ps", bg)])
                for kc in range(KC):
                    mm(PS[bu][:, :], uv[:, kc, jj * 128:(jj + 1) * 128], HT[:, kc, :], kc == 0, kc == KC - 1,
                       [("w", du), htk(kc)], [("ps", bu)])
                act(SG[:, jj, :], PS[bg][:, :], AF.Silu, [("ps", bg)], [("sg", jj)])
                tt(ACTB[:, ab, jj, :], SG[:, jj, :], PS[bu][:, :], ALU.mult,
                   [("sg", jj), ("ps", bu)], [("actb", ab, jj)])
            for m in range(KC):
                bd = P.nextbank()
                for jj in range(G):
                    mm(PS[bd][:, :], dv[:, jj, m * 128:(m + 1) * 128], ACTB[:, ab, jj, :],
                       jj == 0, jj == G - 1, [("w", dd), ("actb", ab, jj)], [("ps", bd)])
                stt(XT[:, m, :], PS[bd][:, :], 0.5, XT[:, m, :], ALU.mult, ALU.add,
                    [("ps", bd), xtk(m)], [xtk(m)])

    def store_bf(src, dst, rkeys, wkeys, slot):
        dma("sp", dst, src, rkeys, wkeys, ("ost", slot))

    def win_stage(t):
        tsl = slice(t * TT, (t + 1) * TT)
        col = 0
        for name, n in SEC:
            if name in ("qr", "kr"):
                for h in range(RH):
                    banks = []
                    d = rot_slot("w", NW)
                    wv = wload_cols(d, w_in, col + 2 * h)
                    for i in range(2):
                        b = P.nextbank()
                        for kc in range(KC):
                            mm(PS[b][:, :], wv[:, kc, i * 128:(i + 1) * 128], HT[:, kc, :], kc == 0, kc == KC - 1,
                               [("w", d), htk(kc)], [("ps", b)])
                        banks.append(b)
                    b1, b2 = banks
                    cos, sin = ROT[:, 0, tsl], ROT[:, 1, tsl]
                    o1, o2 = rot_slot("ost", 8), rot_slot("ost", 8)
                    tt(TMP[:, 0, :], PS[b1][:, :], cos, ALU.mult, [("ps", b1), "rot"], [("tmp", 0)])
                    tt(TMP[:, 1, :], PS[b2][:, :], sin, ALU.mult, [("ps", b2), "rot"], [("tmp", 1)])
                    tt(OST[:, o1, :], TMP[:, 0, :], TMP[:, 1, :], ALU.subtract,
                       [("tmp", 0), ("tmp", 1)], [("ost", o1)])
                    tt(TMP[:, 2, :], PS[b2][:, :], cos, ALU.mult, [("ps", b2), "rot"], [("tmp", 2)])
                    tt(TMP[:, 3, :], PS[b1][:, :], sin, ALU.mult, [("ps", b1), "rot"], [("tmp", 3)])
                    tt(OST[:, o2, :], TMP[:, 2, :], TMP[:, 3, :], ALU.add,
                       [("tmp", 2), ("tmp", 3)], [("ost", o2)])
                    for oo, cx in ((o1, 2 * h), (o2, 2 * h + 1)):
                        dd = QRd[cx][:, tsl] if name == "qr" else KRm[t][cx]
                        store_bf(OST[:, oo, :], dd, [("ost", oo)], [(name, t, cx)], oo)
            elif name in ("gr", "qa", "ka", "ua", "ub"):
                dst = {"gr": GRd, "qa": QAd, "ka": None, "ua": UAd, "ub": UBd}[name]
                func = {"gr": AF.Silu, "qa": AF.Copy, "ka": AF.Copy, "ua": AF.Sigmoid, "ub": AF.Sigmoid}[name]
                for ci in range(n):
                    if ci % 2 == 0:
                        d = rot_slot("w", NW)
                        wv = wload_cols(d, w_in, col + ci)
                    u = ci % 2
                    b = P.nextbank()
                    for kc in range(KC):
                        mm(PS[b][:, :], wv[:, kc, u * 128:(u + 1) * 128], HT[:, kc, :], kc == 0, kc == KC - 1,
                           [("w", d), htk(kc)], [("ps", b)])
                    o = rot_slot("ost", 8)
                    if func == AF.Copy:
                        vcopy(OST[:, o, :], PS[b][:, :], [("ps", b)], [("ost", o)])
                    else:
                        act(OST[:, o, :], PS[b][:, :], func, [("ps", b)], [("ost", o)])
                    dd = KAm[t][ci] if name == "ka" else dst[ci][:, tsl]
                    store_bf(OST[:, o, :], dd, [("ost", o)], [(name, t, ci)], o)
            else:
                dst = VRm[t] if name == "vr" else VAm[t]
                for ci in range(0, n, 2):
                    d = rot_slot("w", NW)
                    wv = wload_cols(d, w_in, col + ci)
                    for half in range(2):
                        b = P.nextbank()
                        for a2 in range(2):
                            tc = 2 * half + a2
                            for kc in range(KC):
                                mm(PS[b][:, a2 * 256:(a2 + 1) * 256], HT[:, kc, tc * 128:(tc + 1) * 128],
                                   wv[:, kc, :], kc == 0, kc == KC - 1,
                                   [("w", d), htk(kc)], [("ps", b)])
                        o = rot_slot("ost", 8)
                        anycopy(OST[:, o, :], PS[b][:, :], [("ps", b)], [("ost", o)])
                        store_bf(OST[:, o, :].rearrange("p (a e) -> p a e", e=256),
                                 dst[half * 256:(half + 1) * 256, ci * 128:ci * 128 + 256].rearrange(
                                     "(a p) e -> p a e", p=128),
                                 [("ost", o)], [(name, t, ci), (name, t, ci + 1)], o)
            col += n
            if name in SECROWS:
                m_, g_ = CCT[(name, t)]
                P.add("pool", lambda E, m_=m_, g_=g_: E.collective_compute(
                    "AllGather", ALU.bypass, replica_groups=[[2 * i, 2 * i + 1] for i in range(ncores // 2)],
                    ins=[m_.ap().opt()], outs=[g_.ap().opt()]),
                    r=[(name, t, c) for c in range(n)], w=[("g_" + name, t)], dma_key=("cc", name, t), inc=1)

    def emit_out(t):
        for tc in range(4):
            gtc = t * 4 + tc
            st = tc % 2
            stg = HTraw[:, st * 4096:(st + 1) * 4096]
            keys = [htk(c) for c in range(16 * st, 16 * st + 16)]
            for c0 in range(0, KC, 4):
                b = P.nextbank()
                for i in range(4):
                    c = c0 + i
                    P.add("pe", lambda E, b=b, i=i, c=c, tc=tc: E.transpose(
                        out=PS[b][:, i * 128:(i + 1) * 128], in_=XT[:, c, tc * 128:(tc + 1) * 128],
                        identity=IDF[:]), r=[xtk(c), "idf"], w=[("ps", b)])
                anycopy(stg[:, c0 * 128:(c0 + 4) * 128], PS[b][:, :], [("ps", b)], keys)
            dma("sp", y_out[gtc * 128:(gtc + 1) * 128, :], stg, keys, [("y", gtc)], ("in", st))

    def finish():
        P.add("sp", lambda E: E.nop(), r=[("y", g) for g in range(8)], w=[])
        P.used_inputs = list(used_inputs)
        return nc, P

    for t in range(NT):
        load_x_tile(t)
        if stop == 1:
            emit_out(t); return finish()
        rmsnorm_to_ht(0)
        ffn(wg1, wu1, wd1)
        if stop == 2:
            emit_out(t); return finish()
        rmsnorm_to_ht(1)
        win_stage(t)
        if stop == 3:
            emit_out(t); return finish()
        for q4 in range(4):
            dma("sp", XSd[t][:, q4 * 4096:(q4 + 1) * 4096], AR[:, q4 * 4096:(q4 + 1) * 4096],
                [xtk(c) for c in range(q4 * 8, q4 * 8 + 8)], [("xs", t, q4)], ("xs", q4))
    if stop == 4:
        P.fence(); emit_out(1); return finish()
    all_ar = [xtk(c) for c in range(KC)] + [htk(c) for c in range(KC)]
    cur = [0]

    def carve(nbytes, dt, shape_str=None, **kw):
        ncol = nbytes // 4
        v = AR[:, cur[0]:cur[0] + ncol]
        cur[0] += ncol
        assert cur[0] <= 24576
        if dt == BF16:
            v = v.bitcast(BF16)
        if shape_str:
            v = v.rearrange(shape_str, **kw)
        return v

    Rq = [carve(4096, BF16, "p (i t) -> p i t", t=TOK) for _ in range(2)]
    Rk = [carve(8192, BF16, "p (i t) -> p i t", t=2 * TOK) for _ in range(2)]
    Rv = [carve(8192, BF16, "p (j e) -> p j e", e=256) for _ in range(2)]
    Rg = [carve(4096, BF16, "p (i t) -> p i t", t=TOK) for _ in range(2)]
    Rtab = [carve(5632, F32) for _ in range(2)]
    PT = carve(4096, BF16, "p (i t) -> p i t", t=TT)
    YF = carve(4096, F32, "p (i t) -> p i t", t=TT)
    ret_end = cur[0]

    P.fence()

    def arw(keys):
        return list(keys)

    def ret_load(h):
        s = h % 2
        for i in range(2):
            c = 2 * h + i
            dma("sp", Rq[s][:, i, :], QRd[c], [("qr", c)], arw([("rq", s)]), "ld")
            for t_ in range(NT):
                dma("sp", Rk[s][:, i, t_ * TT:(t_ + 1) * TT], KRp[t_][c], [("g_kr", t_)], [("rk", s)], "ld")
                dma("sp", Rk[s][:, i, TOK + t_ * TT:TOK + (t_ + 1) * TT], KRm[t_][c], [], [("rk", s)], "ld")
            dma("sp", Rg[s][:, i, :], GRd[c], [("gr", c)], [("rg", s)], "ld")
        for t_ in range(NT):
            dma("sp", Rv[s][:, 4 * t_:4 * t_ + 4, :],
                VRp[t_][:, h * 256:(h + 1) * 256].rearrange("(j p) e -> p j e", p=128),
                [("g_vr", t_)], [("rv", s)], "ld")
            dma("sp", Rv[s][:, 8 + 4 * t_:12 + 4 * t_, :],
                VRm[t_][:, h * 256:(h + 1) * 256].rearrange("(j p) e -> p j e", p=128),
                [], [("rv", s)], "ld")
        dma("sp", Rtab[s][:, :], rtab_in[h], [], [("rtab", s)], "ld")

    def retention_head(h):
        s = h % 2
        q, k, v, g, tab = Rq[s], Rk[s], Rv[s], Rg[s], Rtab[s]
        for qb in range(2):
            qs = slice(qb * TT, (qb + 1) * TT)
            by = [P.nextbank(), P.nextbank()]
            jlist = list(range(8)) + [8 + j for j in range(4 * qb + 4)]
            def r_scores(idx, jj):
                bs = P.nextbank(avoid=by)
                for i in range(2):
                    mm(PS[bs][:, :], k[:, i, jj * 128:(jj + 1) * 128], q[:, i, qs], i == 0, i == 1,
                       [("rk", s), ("rq", s)], [("ps", bs)])
                j = jj - 8
                if jj < 8 or j < 4 * qb:
                    tv = tab[:, 0:512]
                else:
                    m = j - 4 * qb
                    tv = tab[:, 512 + 384 - 128 * m: 512 + 384 - 128 * m + 512]
                colx = (h * 16 + jj) * 2 + qb
                pi = idx % 4
                stt(PT[:, pi, :], PS[bs][:, :], RSCAL[:, colx:colx + 1], tv, ALU.mult, ALU.mult,
                    [("ps", bs), "rscal", ("rtab", s)], [("pt", pi)])

            def r_pv(idx, jj):
                pi = idx % 4
                for e in range(2):
                    mm(PS[by[e]][:, :], v[:, jj, e * 128:(e + 1) * 128], PT[:, pi, :],
                       idx == 0, idx == len(jlist) - 1, [("rv", s), ("pt", pi)], [("ps", by[e])])

            for idx in range(len(jlist) + 1):
                if idx < len(jlist):
                    r_scores(idx, jlist[idx])
                if idx >= 1:
                    r_pv(idx - 1, jlist[idx - 1])
            for e in range(2):
                act(YF[:, e, :], PS[by[e]][:, :], AF.Copy, [("ps", by[e])], [("yf", e)])
            rmsnorm_stats(2, lambda e: YF[:, e, :], lambda e: [("yf", e)], 256)
            for e in range(2):
                o = rot_slot("ost", 8)
                stt(TMP[:, e, :], YF[:, e, :], 1.0, RSTD[:, :], ALU.mult, ALU.mult,
                    [("yf", e), "rstd"], [("tmp", e)])
                tt(OST[:, o, :], TMP[:, e, :], g[:, e, qs], ALU.mult, [("tmp", e), ("rg", s)], [("ost", o)])
                store_bf(OST[:, o, :], YRd[2 * h + e][:, qs], [("ost", o)], [("yr", 2 * h + e, qb)], o)

    ret_load(0)
    for h in range(RH):
        if h + 1 < RH:
            ret_load(h + 1)
        retention_head(h)

    if stop == 5:
        P.fence()
        for q4 in range(4):
            dma("sp", AR[:, q4 * 4096:(q4 + 1) * 4096], XSd[1][:, q4 * 4096:(q4 + 1) * 4096],
                [("xs", 1, q4)], [xtk(c) for c in range(q4 * 8, q4 * 8 + 8)], ("xs", q4))
        emit_out(1); return finish()
    cur[0] = 0
    Aq = [carve(3 * 2048, BF16, "p (g t) -> p g t", t=TOK) for _ in range(2)]
    Ak = [carve(3 * 4096, BF16, "p (g t) -> p g t", t=2 * TOK) for _ in range(2)]
    Av0 = [carve(4096, BF16, "p (j e) -> p j e", e=128) for _ in range(2)]
    Av1 = [carve(4096, BF16, "p (c k e) -> p c k e", k=4, e=128) for _ in range(2)]
    Av2 = [carve(4096, BF16, "p (c e) -> p c e", e=128) for _ in range(2)]
    PA = carve(2048, BF16, "p (i t) -> p i t", t=128)
    SQW = carve(4096, BF16)
    ROW = carve(5 * TOK * 4 + 64, F32)
    ret_keys = [("rq", s) for s in range(2)] + [("rk", s) for s in range(2)] + [("rv", s) for s in range(2)] + \
               [("rg", s) for s in range(2)] + [("rtab", s) for s in range(2)] + \
               [("pt", i) for i in range(4)] + [("yf", e) for e in range(2)]
    P.fence()

    def atw(keys):
        return list(keys)

    def att_load(i):
        s = i % 2
        for g in range(3):
            c = 4 * g + i
            dma("sp", Aq[s][:, g, :], QAd[c], [("qa", c)], atw([("aq", s)]), "ld")
            for t_ in range(NT):
                dma("sp", Ak[s][:, g, t_ * TT:(t_ + 1) * TT], KAp[t_][c], [("g_ka", t_)], [("ak", s)], "ld")
                dma("sp", Ak[s][:, g, TOK + t_ * TT:TOK + (t_ + 1) * TT], KAm[t_][c], [], [("ak", s)], "ld")
        c0, c1, c2 = i * 128, (4 + i) * 128, (8 + i) * 128
        for t_ in range(NT):
            for (src, base, gk) in ((VAp[t_], 0, [("g_va", t_)]), (VAm[t_], 1, [])):
                dma("sp", Av0[s][:, 8 * base + 4 * t_:8 * base + 4 * t_ + 4, :],
                    src[:, c0:c0 + 128].rearrange("(j p) e -> p j e", p=128), gk, [("av0", s)], "ld")
                dma("sp", Av1[s][:, :, 2 * base + t_, :],
                    src[:, c1:c1 + 128].rearrange("(p c) e -> p c e", c=4), gk, [("av1", s)], "ld")
                dma("sp", Av2[s][64 * base + 32 * t_:64 * base + 32 * t_ + 32, :, :],
                    src[:, c2:c2 + 128].rearrange("(p c) e -> p c e", c=16), gk, [("av2", s)], "ld")

    def att_shift(i):
        s = i % 2
        QN = ROW[0:1, 0:3 * TOK].rearrange("p (g t) -> p g t", t=TOK)
        KN = ROW[0:1, 3 * TOK:5 * TOK]
        KM = ROW[0:1, 5 * TOK:5 * TOK + 4]
        for g in range(3):
            for (src, n, dstf) in ((Aq[s][:, g, :], 2, lambda blk: QN[:, g, blk * TT:(blk + 1) * TT]),
                                   (Ak[s][:, g, :], 4, lambda blk: KN[:, blk * TT:(blk + 1) * TT])):
                act(SQW[:, 0:n * TT], src, AF.Square, [("aq", s), ("ak", s)], ["sqw"])
                for blk in range(n):
                    b = P.nextbank()
                    mm(PS[b][0:1, :], ONES[:, 0:1], SQW[:, blk * TT:(blk + 1) * TT], True, True,
                       ["sqw", "ones"], [("ps", b)])
                    vcopy(dstf(blk), PS[b][0:1, :], [("ps", b)],
                          [("rowscr_q", g, blk) if n == 2 else ("rowscr_k", blk)])
            P.add("dve", lambda E, g=g: E.reduce_max(out=KM[:, g:g + 1], in_=KN[:, :],
                                                     axis=mybir.AxisListType.X),
                  r=["rowscr"] + [("rowscr_k", blk) for blk in range(4)], w=["rowscr"])
            tss(QN[:, g, :], QN[:, g, :], KM[:, g:g + 1], ALU.mult,
                ["rowscr"] + [("rowscr_q", g, blk) for blk in range(2)],
                ["rowscr"] + [("rowscr_q", g, blk) for blk in range(2)])
        tt(QN[:, 0, :], QN[:, 0, :], QN[:, 1, :], ALU.max, ["rowscr"], ["rowscr"])
        tt(QN[:, 0, :], QN[:, 0, :], QN[:, 2, :], ALU.max, ["rowscr"], ["rowscr"])
        act(QN[:, 0, :], QN[:, 0, :], AF.Sqrt, ["rowscr"], ["rowscr"])
        tss(NEGC[:, :], QN[:, 0, :], -1.0, ALU.mult, ["rowscr"], ["negc"])

    SC = 128.0 ** -0.5

    def attention_head(i):
        s = i % 2
        for qb in range(2):
            bn, bd = P.nextbank(), P.nextbank()
            first = True
            cnt = 0
            work = []
            for g, r in enumerate((1, 4, 16)):
                nq = TT // r
                nsub = max(1, nq // 128)
                nqs = min(nq, 128)
                for c in range(r):
                    for sub in range(nsub):
                        uo0 = (TT * qb) // r + sub * 128
                        uq0 = TOK // r + uo0
                        if r == 16:
                            tiles = [(0, 3 + qb)]
                        else:
                            kt0 = uq0 // 128 - 1
                            prev = kt0 < (TOK // r) // 128
                            tiles = [(kt0, 0 if prev else 1), (kt0 + 1, 2)]
                        for kt, mk in tiles:
                            work.append((g, r, c, sub, nqs, uo0, kt, mk))
            nwork = len(work)
            def a_scores(wi):
                g, r, c, sub, nqs, uo0, kt, mk = work[wi]
                q0 = c + r * uo0
                qsl = slice(q0, q0 + r * (nqs - 1) + 1, r)
                k0 = c + r * 128 * kt
                ksl = slice(k0, k0 + r * 127 + 1, r)
                bs = P.nextbank(avoid=(bn, bd))
                mm(PS[bs][:, 0:nqs], Ak[s][:, g, ksl], Aq[s][:, g, qsl], True, False,
                   [("ak", s), ("aq", s)], [("ps", bs)], skip=True)
                mm(PS[bs][:, 0:nqs], ONESROW[0:1, :], NEGC[0:1, qsl], False, False,
                   ["onesrow", "negc"], [("ps", bs)], skip=True)
                mm(PS[bs][:, 0:nqs], IDB[:, :], AMASK[:, mk, 0:nqs], False, True,
                   ["idb", "amask"], [("ps", bs)], skip=True)
                pi = wi % 8
                act(PA[:, pi, 0:nqs], PS[bs][:, 0:nqs], AF.Exp, [("ps", bs)], [("pa", pi)], scale=SC)

            def a_pv(wi):
                g, r, c, sub, nqs, uo0, kt, mk = work[wi]
                pi = wi % 8
                o0 = c + r * sub * 128
                osl = slice(o0, o0 + r * (nqs - 1) + 1, r)
                if g == 0:
                    vt = Av0[s][:, kt, :]
                    vk = ("av0", s)
                elif g == 1:
                    vt = Av1[s][:, c, kt, :]
                    vk = ("av1", s)
                else:
                    vt = Av2[s][:, c, :]
                    vk = ("av2", s)
                st = (wi == 0)
                mm(PS[bn][:, osl], vt, PA[:, pi, 0:nqs], st, wi == nwork - 1,
                   [vk, ("pa", pi)], [("ps", bn)], skip=True)
                mm(PS[bd][:, osl], ONES[:, :], PA[:, pi, 0:nqs], st, wi == nwork - 1,
                   ["ones", ("pa", pi)], [("ps", bd)], skip=True)

            for wi in range(nwork + 2):
                if wi < nwork:
                    a_scores(wi)
                if wi >= 2:
                    a_pv(wi - 2)
            P.add("dve", lambda E, bd=bd: E.reciprocal(out=TMP[:, 0, :], in_=PS[bd][:, :]),
                  r=[("ps", bd)], w=[("tmp", 0)])
            o = rot_slot("ost", 8)
            tt(OST[:, o, :], PS[bn][:, :], TMP[:, 0, :], ALU.mult, [("ps", bn), ("tmp", 0)], [("ost", o)])
            store_bf(OST[:, o, :], YAd[i][:, qb * TT:(qb + 1) * TT], [("ost", o)], [("ya", i, qb)], o)

    ONESROW = nc.alloc_sbuf_tensor([1, 128], BF16)
    P.add("dve", lambda E: E.memset(ONESROW[:], 1.0), r=[], w=["onesrow"])
    att_load(0)
    for i in range(4):
        if i + 1 < 4:
            att_load(i + 1)
        att_shift(i)
        attention_head(i)

    if stop == 6:
        P.fence()
        for q4 in range(4):
            dma("sp", AR[:, q4 * 4096:(q4 + 1) * 4096], XSd[1][:, q4 * 4096:(q4 + 1) * 4096],
                [("xs", 1, q4)], [xtk(c) for c in range(q4 * 8, q4 * 8 + 8)], ("xs", q4))
        emit_out(1); return finish()
    att_keys = [("aq", s) for s in range(2)] + [("ak", s) for s in range(2)] + \
               [("av0", s) for s in range(2)] + [("av1", s) for s in range(2)] + \
               [("av2", s) for s in range(2)] + [("pa", i) for i in range(8)] + ["sqw"]
    YRt = AR[:, 0:4096].bitcast(BF16).rearrange("p (c t) -> p c t", t=TT)
    YAt = AR[:, 4096:5120].bitcast(BF16).rearrange("p (c t) -> p c t", t=TT)
    P.fence()
    for t in range(NT):
        tsl = slice(t * TT, (t + 1) * TT)
        extra = []
        for c in range(16):
            dma("sp", YRt[:, c, :], YRd[c][:, tsl], [("yr", c, t)], [xtk(c // 2)] + (extra if c == 0 else []),
                "ld")
        for c in range(4):
            dma("sp", YAt[:, c, :], YAd[c][:, tsl], [("ya", c, t)], [xtk(8 + c // 2)], "ld")
        for m in range(KC):
            if m % 2 == 0:
                d3 = rot_slot("w", NW)
                wa2 = wload_cols(d3, w_oa, m, nk=16, off=0)
                wb2 = wload_cols(d3, w_ob, m, nk=4, off=4096)
            um = m % 2
            wa = wa2[:, :, um * 128:(um + 1) * 128]
            wb = wb2[:, :, um * 128:(um + 1) * 128]
            sa = sb = d3
            us = rot_slot("ust", 2)
            dma("sp", UST[:, us, 0, :], UAd[m][:, tsl], [("ua", m)], [("ust", us)], "ld")
            dma("sp", UST[:, us, 1, :], UBd[m][:, tsl], [("ub", m)], [("ust", us)], "ld")
            ba, bb = P.nextbank(), P.nextbank()
            for kc in range(16):
                mm(PS[ba][:, :], wa[:, kc, :], YRt[:, kc, :], kc == 0, kc == 15,
                   [("w", sa), xtk(kc // 2)], [("ps", ba)])
            for kc in range(4):
                mm(PS[bb][:, :], wb[:, kc, :], YAt[:, kc, :], kc == 0, kc == 3,
                   [("w", sb), xtk(8 + kc // 2)], [("ps", bb)])
            tt(TMP[:, 0, :], PS[ba][:, :], UST[:, us, 0, :], ALU.mult, [("ps", ba), ("ust", us)], [("tmp", 0)])
            tt(TMP[:, 1, :], PS[bb][:, :], UST[:, us, 1, :], ALU.mult, [("ps", bb), ("ust", us)], [("tmp", 1)])
            tt(HT[:, m, :], TMP[:, 0, :], TMP[:, 1, :], ALU.add, [("tmp", 0), ("tmp", 1)], [htk(m)])
        for q4 in range(4):
            dma("sp", AR[:, q4 * 4096:(q4 + 1) * 4096], XSd[t][:, q4 * 4096:(q4 + 1) * 4096],
                [("xs", t, q4)], [xtk(c) for c in range(q4 * 8, q4 * 8 + 8)], ("xs", q4))
        for m in range(KC):
            if m % 2 == 0:
                d4 = rot_slot("w", NW)
                wv4 = wload_cols(d4, w_o, m)
            um = m % 2
            b = P.nextbank()
            for kc in range(KC):
                mm(PS[b][:, :], wv4[:, kc, um * 128:(um + 1) * 128], HT[:, kc, :], kc == 0, kc == KC - 1,
                   [("w", d4), htk(kc)], [("ps", b)])
            tt(XT[:, m, :], PS[b][:, :], XT[:, m, :], ALU.add, [("ps", b), xtk(m)], [xtk(m)])
        rmsnorm_to_ht(2)
        ffn(wg2, wu2, wd2)
        rmsnorm_stats(KC, lambda c: XT[:, c, :], lambda c: [xtk(c)], D)
        for c in range(KC):
            stt(XT[:, c, :], XT[:, c, :], GAIN[:, 3, c:c + 1], RSTD[:, :], ALU.mult, ALU.mult,
                [xtk(c), "rstd", "gain"], [xtk(c)])
        emit_out(t)
    return finish()


def finalize(nc, P):
    dkeys = list(P.dma_count.keys())
    sems = {}
    import contextlib
    with contextlib.ExitStack() as es:
        engs = {e: es.enter_context(nc.semaphore("e_" + e)) for e in ("pe", "act", "dve", "pool", "sp")}
        for i, k in enumerate(dkeys):
            sems[k] = es.enter_context(nc.semaphore("d%d" % i))
        block = es.enter_context(nc.Block())
        P.emit(nc, block, engs, sems)
    return nc


def _tables(half):
    pos = (np.arange(TOK, dtype=np.float32) + np.float32(half * TOK))
    inv = (np.float32(10000.0) ** (-np.arange(0, 256, 2, dtype=np.float32) / np.float32(256))).astype(np.float32)
    ang = pos[None, :] * inv[:, None]
    rot = np.stack([np.cos(ang), np.sin(ang)], axis=1).astype(np.float32)
    hh = np.arange(RH, dtype=np.float64)
    gam = 1.0 - 2.0 ** (-5.0 - hh)
    s = np.arange(128)[:, None].astype(np.float64)
    rtab = np.zeros((RH, 128, 1408), np.float32)
    rscal = np.zeros((RH, 16, 2), np.float64)
    for h in range(RH):
        tq = np.arange(512)[None, :].astype(np.float64)
        rtab[h, :, 0:512] = gam[h] ** (tq - s)
        v = np.arange(896)[None, :].astype(np.float64) - 384.0
        rtab[h, :, 512:] = np.where(v >= s, gam[h] ** np.maximum(v - s, 0.0), 0.0)
        for jj in range(16):
            for qb in range(2):
                j = jj - 8
                if jj < 8 or j < 4 * qb:
                    val = gam[h] ** (TOK + 512 * qb - 128 * jj) / 16.0
                    if jj < 8 and half == 0:
                        val = 0.0
                else:
                    val = 1.0 / 16.0
                rscal[h, jj, qb] = val
    rscal = np.broadcast_to(rscal.reshape(1, 256).astype(np.float32), (128, 256)).copy()
    sidx = np.arange(128)[:, None]
    tidx = np.arange(128)[None, :]
    am = np.zeros((128, 5, 128), np.float32)
    mu = np.where(sidx >= tidx, 0.0, NEGM)
    am[:, 0, :] = mu if half == 1 else NEGM
    am[:, 1, :] = mu
    am[:, 2, :] = np.where(sidx <= tidx, 0.0, NEGM)
    for b in range(2):
        m16 = np.where(sidx <= 64 + 32 * b + tidx, 0.0, NEGM)
        if half == 0:
            m16 = np.where(sidx < 64, NEGM, m16)
        am[:, 3 + b, :] = m16
    return rot, rtab, rscal, am.astype(ml_dtypes.bfloat16)


_CACHE = {}
_STOP = [99]


def kernel(x, ffn1_norm, ffn1_w_gate, ffn1_w_up, ffn1_w_down, mix_norm, w_in, w_out_ret, w_out_attn,
           w_out, ffn2_norm, ffn2_w_gate, ffn2_w_up, ffn2_w_down, final_norm):
    f = lambda a: np.ascontiguousarray(np.asarray(a, dtype=np.float32))
    x = f(x)
    gains = np.stack([f(ffn1_norm)[0], f(mix_norm)[0], f(ffn2_norm)[0], f(final_norm)], axis=0)
    gains = np.ascontiguousarray(gains.reshape(4, KC, 128).transpose(2, 0, 1))
    shared = {
        "ffn1_w_gate": f(ffn1_w_gate)[0], "ffn1_w_up": f(ffn1_w_up)[0], "ffn1_w_down": f(ffn1_w_down)[0],
        "ffn2_w_gate": f(ffn2_w_gate)[0], "ffn2_w_up": f(ffn2_w_up)[0], "ffn2_w_down": f(ffn2_w_down)[0],
        "w_in": f(w_in)[0], "w_out_ret": f(w_out_ret)[0], "w_out_attn": f(w_out_attn)[0], "w_out": f(w_out)[0],
        "gains": gains, "ident": np.eye(128, dtype=np.float32),
    }
    tabs = [_tables(0), _tables(1)]
    in_maps = []
    for c in range(NCORES):
        b, half = c // 2, c % 2
        rot, rtab, rscal, am = tabs[half]
        m = dict(shared)
        m["x"] = np.ascontiguousarray(x[b, half * TOK:(half + 1) * TOK, :])
        m["rot"] = rot; m["rtab"] = rtab; m["rscal"] = rscal; m["amask"] = am
        in_maps.append(m)
    if "nc" not in _CACHE:
        nc, P = build_program(stop=_STOP[0])
        finalize(nc, P)
        _CACHE["nc"] = nc
        _CACHE["used"] = set(P.used_inputs)
    in_maps = [{k: v for k, v in m.items() if k in _CACHE["used"]} for m in in_maps]
    res = run_bass_kernel_spmd(_CACHE["nc"], in_maps, core_ids=list(range(NCORES)))
    out = np.empty((4, SEQ, D), np.float32)
    for c in range(NCORES):
        b, half = c // 2, c % 2
        out[b, half * TOK:(half + 1) * TOK, :] = np.asarray(res.results[c]["y"], dtype=np.float32)
    return out
```
